# Optimizing a Trainium2 kernel written in Bass

```python
import math
import jax, jax.numpy as jnp
from jax import lax
import numpy as np

D_MODEL = 1024
BATCH = 8
SEQ = 2048
DEPTH = 1
DEC_BATCH = 128
DEC_SEQ = 1
PAST_LEN = 16384
PAGE_SIZE = 128

D_MIX = D_MODEL
R_WIDTH = D_MIX // 2
R_HEAD = 64
R_HEADS = R_WIDTH // R_HEAD
D_DECAY_LORA = 64
D_AAA_LORA = 64
D_GATE_LORA = 128
M_WIDTH = D_MIX - R_WIDTH
M_HEADDIM = 64
M_HEADS = M_WIDTH // M_HEADDIM
M_GROUPS = 2
M_HPG = M_HEADS // M_GROUPS
D_STATE = 128
CONV_K = 4
CONV_DIM = M_WIDTH + 2 * M_GROUPS * D_STATE
SSD_CHUNK = 256
D_FF = 4 * D_MODEL
RWKV_PROJ = 3 * R_WIDTH + D_DECAY_LORA + D_AAA_LORA + D_GATE_LORA
PROJ = RWKV_PROJ + M_WIDTH + CONV_DIM + M_HEADS
ALPHA = (2 * DEPTH) ** 0.25
BETA = (8 * DEPTH) ** -0.25
LN_EPS = 1e-5
LNX_EPS = 64e-5
RMS_EPS = 1e-5

kernel_name = 'rwkv7_mamba2_hybrid_step'


def layer_norm(x, g, b, eps):
    xf = x.astype(jnp.float32)
    mu = jnp.mean(xf, -1, keepdims=True)
    var = jnp.mean(jnp.square(xf - mu), -1, keepdims=True)
    return (xf - mu) * lax.rsqrt(var + eps) * g.astype(jnp.float32) + b.astype(jnp.float32)


def rwkv7_mix(u, wkv0, w0, w_up, a0, a_up, g_up, k_k, k_a, r_k, lnx_w, lnx_b):
    b, L, _ = u.shape
    cuts = [R_WIDTH, 2 * R_WIDTH, 3 * R_WIDTH, 3 * R_WIDTH + D_DECAY_LORA,
            3 * R_WIDTH + D_DECAY_LORA + D_AAA_LORA]
    r, k, v, uw, ua, ug = jnp.split(u, cuts, axis=-1)
    w_log = -jax.nn.softplus(-(w0 + jnp.tanh(uw) @ w_up)) - 0.5
    decay = jnp.exp(-jnp.exp(w_log))
    a = jax.nn.sigmoid(a0 + ua @ a_up)
    g = jax.nn.sigmoid(ug) @ g_up
    heads = lambda t: t.reshape(b, L, R_HEADS, R_HEAD)
    kk = heads(k * k_k)
    kk = kk * lax.rsqrt(jnp.maximum(jnp.sum(jnp.square(kk), -1, keepdims=True), 1e-24))
    k = heads(k * (1.0 + (a - 1.0) * k_a))
    r, v, decay, a = heads(r), heads(v), heads(decay), heads(a)

    def step(S, inp):
        r_t, w_t, k_t, v_t, ka_t, kb_t = inp
        Sa = jnp.einsum('bhvk,bhk->bhv', S, ka_t)
        S = (S * w_t[:, :, None, :] + Sa[..., None] * kb_t[:, :, None, :]
             + v_t[..., None] * k_t[:, :, None, :])
        return S, jnp.einsum('bhvk,bhk->bhv', S, r_t)

    seq = tuple(jnp.swapaxes(t, 0, 1) for t in (r, decay, k, v, -kk, kk * a))
    S_fin, y = lax.scan(step, wkv0, seq)
    y = jnp.swapaxes(y, 0, 1)
    y = layer_norm(y, lnx_w.reshape(R_HEADS, R_HEAD), lnx_b.reshape(R_HEADS, R_HEAD), LNX_EPS)
    y = y + jnp.sum(r * k * r_k, -1, keepdims=True) * v
    return y.reshape(b, L, R_WIDTH) * g, S_fin


def segsum(a):
    q = a.shape[-1]
    cs = jnp.cumsum(a, -1)
    diff = cs[..., :, None] - cs[..., None, :]
    mask = jnp.tril(jnp.ones((q, q), dtype=bool))
    return jnp.where(mask, diff, -jnp.inf)


def ssd_scan(X, Adt, Bm, Cm, h0):
    b, L = X.shape[:2]
    q = min(SSD_CHUNK, L)
    pad = (-L) % q
    if pad:
        padw = lambda t: jnp.pad(t, [(0, 0), (0, pad)] + [(0, 0)] * (t.ndim - 2))
        X, Adt, Bm, Cm = padw(X), padw(Adt), padw(Bm), padw(Cm)
    nc = (L + pad) // q
    X = X.reshape(b, nc, q, M_GROUPS, M_HPG, M_HEADDIM)
    Adt = Adt.reshape(b, nc, q, M_GROUPS, M_HPG).transpose(0, 3, 4, 1, 2)
    Bm = Bm.reshape(b, nc, q, M_GROUPS, D_STATE)
    Cm = Cm.reshape(b, nc, q, M_GROUPS, D_STATE)
    A_cs = jnp.cumsum(Adt, -1)
    Lmat = jnp.exp(segsum(Adt))
    CB = jnp.einsum('bclgn,bcsgn->bcgls', Cm, Bm)
    y_diag = jnp.einsum('bcgls,bgecls,bcsgep->bclgep', CB, Lmat, X)
    decay_states = jnp.exp(A_cs[..., -1:] - A_cs)
    states = jnp.einsum('bclgn,bgecl,bclgep->bcgepn', Bm, decay_states, X)
    states = jnp.concatenate([h0[:, None], states], axis=1)
    chunk_decay = jnp.exp(segsum(jnp.pad(A_cs[..., -1], ((0, 0), (0, 0), (0, 0), (1, 0)))))
    states = jnp.einsum('bgezc,bcgepn->bzgepn', chunk_decay, states)
    h_in, h_fin = states[:, :-1], states[:, -1]
    y_off = jnp.einsum('bclgn,bcgepn,bgecl->bclgep', Cm, h_in, jnp.exp(A_cs))
    y = (y_diag + y_off).reshape(b, nc * q, M_GROUPS, M_HPG, M_HEADDIM)[:, :L]
    return y, h_fin


def mamba2_mix(z, xbc, dt, conv_prev, ssm0, conv_w, conv_b, dt_bias, a_log, d_skip, gnorm_w):
    b, L, _ = xbc.shape
    xpad = jnp.concatenate([conv_prev, xbc], axis=1)
    conv = conv_b + sum(xpad[:, i:i + L] * conv_w[i] for i in range(CONV_K))
    new_conv = xpad[:, L:]
    xbc = jax.nn.silu(conv)
    xs, Bm, Cm = jnp.split(xbc, [M_WIDTH, M_WIDTH + M_GROUPS * D_STATE], axis=-1)
    xs = xs.reshape(b, L, M_GROUPS, M_HPG, M_HEADDIM)
    Bm = Bm.reshape(b, L, M_GROUPS, D_STATE)
    Cm = Cm.reshape(b, L, M_GROUPS, D_STATE)
    dt = jax.nn.softplus(dt + dt_bias).reshape(b, L, M_GROUPS, M_HPG)
    A = -jnp.exp(a_log).reshape(M_GROUPS, M_HPG)
    y, h_fin = ssd_scan(xs * dt[..., None], dt * A, Bm, Cm,
                        ssm0.reshape(b, M_GROUPS, M_HPG, M_HEADDIM, D_STATE))
    y = y + d_skip.reshape(M_GROUPS, M_HPG)[:, :, None] * xs
    gw = M_WIDTH // M_GROUPS
    y = y.reshape(b, L, M_GROUPS, gw) * jax.nn.silu(z).reshape(b, L, M_GROUPS, gw)
    y = y * lax.rsqrt(jnp.mean(jnp.square(y), -1, keepdims=True) + RMS_EPS)
    return (y.reshape(b, L, M_WIDTH) * gnorm_w, new_conv,
            h_fin.reshape(b, M_HEADS, M_HEADDIM, D_STATE))


def decoder_layer(x, shift_prev, wkv0, conv_prev, ssm0, w_in, mu_shift, w0, w_up, a0, a_up,
                  g_up, k_k, k_a, r_k, lnx_w, lnx_b, conv_w, conv_b, dt_bias, a_log, d_skip,
                  gnorm_w, w_out, ln1_g, ln1_b, w_ff1, w_ff2, ln2_g, ln2_b):
    L = x.shape[1]
    proj = (x @ w_in).astype(jnp.float32)
    pr = proj[..., :RWKV_PROJ]
    prev = jnp.concatenate([shift_prev[:, None].astype(jnp.float32), pr[:, :L - 1]], axis=1)
    u = pr + (prev - pr) * mu_shift
    o_r, wkv_fin = rwkv7_mix(u, wkv0.astype(jnp.float32), w0, w_up, a0, a_up, g_up,
                             k_k, k_a, r_k, lnx_w, lnx_b)
    m0 = RWKV_PROJ
    z = proj[..., m0:m0 + M_WIDTH]
    xbc = proj[..., m0 + M_WIDTH:m0 + M_WIDTH + CONV_DIM]
    dt = proj[..., m0 + M_WIDTH + CONV_DIM:]
    o_m, conv_new, ssm_fin = mamba2_mix(z, xbc, dt, conv_prev.astype(jnp.float32),
                                        ssm0.astype(jnp.float32), conv_w, conv_b,
                                        dt_bias, a_log, d_skip, gnorm_w)
    mix = jnp.concatenate([o_r, o_m], axis=-1).astype(x.dtype) @ w_out
    h = layer_norm(ALPHA * x + mix, ln1_g, ln1_b, LN_EPS).astype(x.dtype)
    ff = jnp.square(jax.nn.relu(h @ w_ff1)) @ w_ff2
    y = layer_norm(ALPHA * h + ff, ln2_g, ln2_b, LN_EPS).astype(x.dtype)
    return y, pr[:, -1], wkv_fin, conv_new, ssm_fin


def setup_inputs(seed: int = 0) -> dict:
    key = jax.random.key(seed)
    ks = jax.random.split(key, 32)
    nrm = lambda k, shape, s: jax.random.normal(k, shape, jnp.float32) * s
    dt = jnp.exp(jax.random.uniform(ks[20], (DEPTH, M_HEADS), jnp.float32,
                                    minval=math.log(1e-3), maxval=math.log(1e-1)))
    return {
        'x_prompt': nrm(ks[0], (BATCH, SEQ, D_MODEL), 1.0),
        'x_sample': nrm(ks[1], (DEC_BATCH, DEC_SEQ, D_MODEL), 1.0),
        'state_shift': nrm(ks[2], (DEPTH, DEC_BATCH, RWKV_PROJ), 1.0),
        'state_wkv': nrm(ks[3], (DEPTH, DEC_BATCH, R_HEADS, R_HEAD, R_HEAD), 0.3),
        'state_conv': nrm(ks[4], (DEPTH, DEC_BATCH, CONV_K - 1, CONV_DIM), 1.0),
        'state_ssm': nrm(ks[5], (DEPTH, DEC_BATCH, M_HEADS, M_HEADDIM, D_STATE), 0.3),
        'w_in': nrm(ks[6], (DEPTH, D_MODEL, PROJ), D_MODEL ** -0.5),
        'mu_shift': jax.random.uniform(ks[7], (DEPTH, RWKV_PROJ), jnp.float32),
        'w0': nrm(ks[8], (DEPTH, R_WIDTH), 0.5) - 1.0,
        'w_up': nrm(ks[9], (DEPTH, D_DECAY_LORA, R_WIDTH), 0.1),
        'a0': nrm(ks[10], (DEPTH, R_WIDTH), 0.1),
        'a_up': nrm(ks[11], (DEPTH, D_AAA_LORA, R_WIDTH), 0.1),
        'g_up': nrm(ks[12], (DEPTH, D_GATE_LORA, R_WIDTH), D_GATE_LORA ** -0.5),
        'k_k': 0.85 + nrm(ks[13], (DEPTH, R_WIDTH), 0.05),
        'k_a': 1.0 + nrm(ks[14], (DEPTH, R_WIDTH), 0.05),
        'r_k': nrm(ks[15], (DEPTH, R_HEADS, R_HEAD), 0.1),
        'lnx_w': 1.0 + nrm(ks[16], (DEPTH, R_WIDTH), 0.02),
        'lnx_b': nrm(ks[17], (DEPTH, R_WIDTH), 0.02),
        'conv_w': nrm(ks[18], (DEPTH, CONV_K, CONV_DIM), CONV_K ** -0.5),
        'conv_b': nrm(ks[19], (DEPTH, CONV_DIM), 0.02),
        'dt_bias': dt + jnp.log(-jnp.expm1(-dt)),
        'a_log': jnp.log(jax.random.uniform(ks[21], (DEPTH, M_HEADS), jnp.float32, minval=1.0, maxval=16.0)),
        'd_skip': 1.0 + nrm(ks[22], (DEPTH, M_HEADS), 0.1),
        'gnorm_w': 1.0 + nrm(ks[23], (DEPTH, M_WIDTH), 0.02),
        'w_out': nrm(ks[24], (DEPTH, D_MIX, D_MODEL), BETA * D_MIX ** -0.5),
        'ln1_g': 1.0 + nrm(ks[25], (DEPTH, D_MODEL), 0.02),
        'ln1_b': nrm(ks[26], (DEPTH, D_MODEL), 0.02),
        'w_ff1': nrm(ks[27], (DEPTH, D_MODEL, D_FF), D_MODEL ** -0.5),
        'w_ff2': nrm(ks[28], (DEPTH, D_FF, D_MODEL), BETA * D_FF ** -0.5),
        'ln2_g': 1.0 + nrm(ks[29], (DEPTH, D_MODEL), 0.02),
        'ln2_b': nrm(ks[30], (DEPTH, D_MODEL), 0.02),
    }


def reference(x_prompt, x_sample, state_shift, state_wkv, state_conv, state_ssm, w_in, mu_shift,
              w0, w_up, a0, a_up, g_up, k_k, k_a, r_k, lnx_w, lnx_b, conv_w, conv_b, dt_bias,
              a_log, d_skip, gnorm_w, w_out, ln1_g, ln1_b, w_ff1, w_ff2, ln2_g, ln2_b):
    h_p, h_s = x_prompt, x_sample
    nb = x_prompt.shape[0]
    new_p, new_s = [], []
    for l in range(DEPTH):
        lp = (w_in[l], mu_shift[l], w0[l], w_up[l], a0[l], a_up[l], g_up[l], k_k[l], k_a[l],
              r_k[l], lnx_w[l], lnx_b[l], conv_w[l], conv_b[l], dt_bias[l], a_log[l], d_skip[l],
              gnorm_w[l], w_out[l], ln1_g[l], ln1_b[l], w_ff1[l], w_ff2[l], ln2_g[l], ln2_b[l])
        h_p, *sp = decoder_layer(
            h_p,
            jnp.zeros((nb, RWKV_PROJ), jnp.float32),
            jnp.zeros((nb, R_HEADS, R_HEAD, R_HEAD), jnp.float32),
            jnp.zeros((nb, CONV_K - 1, CONV_DIM), jnp.float32),
            jnp.zeros((nb, M_HEADS, M_HEADDIM, D_STATE), jnp.float32),
            *lp)
        h_s, *ss = decoder_layer(h_s, state_shift[l], state_wkv[l], state_conv[l], state_ssm[l], *lp)
        new_p.append(sp)
        new_s.append(ss)
    st = lambda outs, i, ref: jnp.stack([o[i] for o in outs]).astype(ref.dtype)
    return (h_p, h_s,
            st(new_p, 0, state_shift), st(new_p, 1, state_wkv), st(new_p, 2, state_conv), st(new_p, 3, state_ssm),
            st(new_s, 0, state_shift), st(new_s, 1, state_wkv), st(new_s, 2, state_conv), st(new_s, 3, state_ssm))
```

```python
import contextlib
import os
import math
import numpy as np
import concourse.bass as bass
import concourse.mybir as mybir
from concourse.bass_utils import run_bass_kernel_spmd

F32 = mybir.dt.float32
BF16 = mybir.dt.bfloat16
AF = mybir.ActivationFunctionType
ALU = mybir.AluOpType
AX = mybir.AxisListType

ENGS = ("pe", "act", "dve", "pool", "sp")
NCORES = 8
LP = 2048
NSMP = 16
NT = LP + NSMP
DM = 1024
RPROJ = 1792
PROJ = 3336
DFF = 4096
C0 = math.exp(-0.5)
ALPHA = 2.0 ** 0.25
SLABS = [(0, 512), (512, 512), (1024, 512), (1536, 512), (2048, 16)]
TILES = [(i * 128, 128) for i in range(16)] + [(2048, 16)]
ARENA_W = 53000


class Dep:
    __slots__ = ("wn", "rn", "name", "excl")

    def __init__(self, name="", excl=False):
        self.wn = None
        self.rn = []
        self.name = name
        self.excl = excl


class _Rec:
    def __getattr__(self, name):
        def f(*a, **k):
            self.last = (name, a, k)
            return self
        return f


class Node:
    __slots__ = ("eng", "kind", "payload", "preds", "cost", "seg", "order", "t0", "t1", "idx", "ev", "prio", "nsucc", "succs", "npend")

    def __init__(self, eng, kind, payload, cost, seg, order):
        self.eng = eng
        self.kind = kind
        self.payload = payload
        self.preds = {}
        self.cost = cost
        self.seg = seg
        self.order = order
        self.t0 = 0.0
        self.t1 = 0.0
        self.idx = 0
        self.ev = None
        self.prio = 0.0
        self.succs = []
        self.npend = 0


def _free_elems(ap):
    n = 1
    for d in ap.shape[1:]:
        n *= int(d)
    return n


def _op_cost(E, rec):
    name, a, k = rec
    try:
        out = a[0]
        n = _free_elems(out)
        if E == "pe":
            passes = 4 if (len(a) > 1 and a[1].dtype == F32) else 1
            return 0.035 + n * passes / 2400.0
        if E == "pool":
            return 0.4 + n / 240.0
        return 0.17 + n / 960.0
    except Exception:
        return 0.5


class Prog:
    def __init__(self, nc, n_dma_sems=48):
        self.nc = nc
        self.n_dma_sems = n_dma_sems
        self.stack = contextlib.ExitStack()
        self.sem = {}
        for e in ENGS:
            self.sem[e] = self.stack.enter_context(nc.semaphore("s_" + e))
        for i in range(n_dma_sems):
            self.sem[("dma", i)] = self.stack.enter_context(nc.semaphore("s_dma%d" % i))
        self.arena = self.stack.enter_context(nc.sbuf_tensor("arena", [128, ARENA_W], F32))
        self.top = 0
        self.banks = []
        for i in range(8):
            t = self.stack.enter_context(nc.psum_tensor("bank%d" % i, [128, 512], F32))
            self.banks.append((t, Dep("bank%d" % i, excl=True)))
        self.bank_rr = 0
        self.held = set()
        self.nodes = []
        self.seg = 0
        self.pools = {}

    def bank(self, hold=False, pool=None):
        if pool is not None:
            lst = pool
            k = self.pools.get(tuple(lst), 0)
            self.pools[tuple(lst)] = k + 1
            return self.banks[lst[k % len(lst)]]
        while self.bank_rr in self.held:
            self.bank_rr = (self.bank_rr + 1) % 8
        i = self.bank_rr
        self.bank_rr = (self.bank_rr + 1) % 8
        if hold:
            self.held.add(i)
        return self.banks[i]

    def release(self, bank):
        for i, b in enumerate(self.banks):
            if b[0] is bank[0]:
                self.held.discard(i)

    def alloc(self, cols, dt=F32):
        w = cols if dt == F32 else (cols + 1) // 2
        off = self.top
        self.top += w
        assert self.top <= ARENA_W, ("arena overflow", self.top)
        ap = self.arena[:, off:off + w]
        if dt != F32:
            ap = ap.bitcast(dt)[:, 0:cols]
        return ap

    def _link(self, node, reads, writes):
        for d in reads:
            if d.wn is not None:
                node.preds[d.wn] = True
            if d.excl:
                for r in d.rn:
                    if r.eng != node.eng:
                        node.preds[r] = True
        for d in writes:
            if d.wn is not None:
                node.preds[d.wn] = True
            for r in d.rn:
                if r is node:
                    continue
                hard = (r.eng != node.eng) or (r.kind == "dma") or (node.kind == "dma")
                if r not in node.preds or hard:
                    node.preds[r] = hard or node.preds.get(r, False)
        for d in reads:
            d.rn.append(node)
        for d in writes:
            d.wn = node
            d.rn = []
        node.preds.pop(node, None)

    def op(self, E, fn, reads=(), writes=()):
        rec = _Rec()
        fn(rec)
        node = Node(E, "op", rec.last, _op_cost(E, rec.last), self.seg, len(self.nodes))
        self._link(node, reads, writes)
        self.nodes.append(node)
        return node

    def dma(self, E, out, in_, reads=(), writes=()):
        try:
            nbytes = int(out.shape[0]) * _free_elems(out) * (4 if out.dtype == F32 else 2)
        except Exception:
            nbytes = 1 << 16
        cost = 2.0 + nbytes / 120e3
        node = Node(E, "dma", (out, in_), cost, self.seg, len(self.nodes))
        self._link(node, reads, writes)
        self.nodes.append(node)
        return node

    def barrier(self):
        self.seg += 1

    def _schedule_segment(self, nodes, t_base):
        import heapq
        inseg = set(nodes)
        for n in nodes:
            n.succs = []
        for n in nodes:
            n.npend = 0
            for p in n.preds:
                if p in inseg:
                    p.succs.append(n)
                    n.npend += 1
        for n in reversed(nodes):
            best = 0.0
            for s_ in n.succs:
                if s_.prio > best:
                    best = s_.prio
            n.prio = best + n.cost + 0.3
        efree = {e: t_base for e in ENGS}
        ready = [n for n in nodes if n.npend == 0]
        order = []
        LAT = 0.45
        while ready:
            bestn = None
            bestkey = None
            for n in ready:
                t = efree[n.eng]
                for p in n.preds:
                    if p in inseg:
                        tp = p.t1 + (LAT if (p.eng != n.eng or p.kind == "dma") else 0.05)
                        if p.eng == n.eng and not n.preds[p]:
                            tp = p.t0
                        if tp > t:
                            t = tp
                key = (t, -n.prio, n.order)
                if bestkey is None or key < bestkey:
                    bestkey = key
                    bestn = n
            n = bestn
            ready.remove(n)
            n.t0 = bestkey[0]
            if n.kind == "dma":
                n.t1 = n.t0 + n.cost
                efree[n.eng] = n.t0 + (1.0 if n.eng == "pool" else 0.1)
            else:
                n.t1 = n.t0 + n.cost
                efree[n.eng] = n.t1
            order.append(n)
            for s_ in n.succs:
                s_.npend -= 1
                if s_.npend == 0:
                    ready.append(s_)
        assert len(order) == len(nodes)
        t_end = max([n.t1 for n in nodes] + [t_base])
        return order, t_end

    def finish(self):
        streams = {e: [] for e in ENGS}
        count = {e: 0 for e in ENGS}
        known = {e: {} for e in ENGS}
        dma_val = [0] * self.n_dma_sems
        dma_rr = 0

        def wait(E, key, val):
            if val == 0:
                return
            if known[E].get(key, 0) >= val:
                return
            known[E][key] = val
            streams[E].append(("wait", key, val))

        def full_barrier():
            for E in ENGS:
                for X in ENGS:
                    if X != E and count[X]:
                        wait(E, X, count[X])
                for k in range(self.n_dma_sems):
                    if dma_val[k]:
                        wait(E, ("dma", k), dma_val[k])

        nseg = self.seg + 1
        segs = [[] for _ in range(nseg)]
        for n in self.nodes:
            segs[n.seg].append(n)
        t_base = 0.0
        for si, seg_nodes in enumerate(segs):
            if si > 0:
                full_barrier()
            if not seg_nodes:
                continue
            order, t_base = self._schedule_segment(seg_nodes, t_base)
            need = set()
            last = {}
            pos = {}
            for i_, n in enumerate(order):
                pos[n] = i_
            waitsets = {}
            for n in order:
                if n.kind == "op":
                    last[n.eng] = n
                best = {}
                for p, hard in n.preds.items():
                    if p.kind == "dma" or p not in pos:
                        continue
                    if p.eng == n.eng and n.kind != "dma" and (n.eng == "pe" or not hard):
                        continue
                    b_ = best.get(p.eng)
                    if b_ is None or pos[p] > pos[b_]:
                        best[p.eng] = p
                waitsets[n] = set(best.values())
                need.update(best.values())
            for n in last.values():
                need.add(n)
            for n in order:
                E = n.eng
                ws_ = waitsets[n]
                for p, hard in n.preds.items():
                    if p.ev is None:
                        continue
                    if p.kind != "dma" and p in pos and p not in ws_:
                        continue
                    key, val = p.ev
                    if p.kind != "dma" and p.eng == E:
                        if E == "pe" or not hard:
                            continue
                    wait(E, key, val)
                if n.kind == "op":
                    if n in need:
                        count[E] += 1
                        n.ev = (E, count[E])
                        streams[E].append(("op", n.payload, n.ev))
                    else:
                        n.ev = None
                        streams[E].append(("opq", n.payload))
                else:
                    k = dma_rr
                    dma_rr = (dma_rr + 1) % self.n_dma_sems
                    key = ("dma", k)
                    wait(E, key, dma_val[k])
                    dma_val[k] += 16
                    n.ev = (key, dma_val[k])
                    streams[E].append(("dma", n.payload[0], n.payload[1], n.ev))
        self.est_us = t_base
        for k in range(self.n_dma_sems):
            if dma_val[k]:
                wait("sp", ("dma", k), dma_val[k])
        for e in ENGS:
            if e != "sp" and count[e]:
                wait("sp", e, count[e])
        self.count = count
        self.streams = streams
        sem = self.sem

        def replay(eng, items):
            for it in items:
                if it[0] == "wait":
                    eng.wait_ge(sem[it[1]], it[2])
                elif it[0] == "op":
                    nm, a, k = it[1]
                    getattr(eng, nm)(*a, **k).then_inc(sem[it[2][0]], 1)
                elif it[0] == "opq":
                    nm, a, k = it[1]
                    getattr(eng, nm)(*a, **k)
                else:
                    eng.dma_start(out=it[1], in_=it[2]).then_inc(sem[it[3][0]], 16)

        with self.nc.Block() as block:
            @block.tensor
            def _(e):
                replay(e, streams["pe"])

            @block.scalar
            def _(e):
                replay(e, streams["act"])

            @block.vector
            def _(e):
                replay(e, streams["dve"])

            @block.gpsimd
            def _(e):
                replay(e, streams["pool"])

            @block.sync
            def _(e):
                replay(e, streams["sp"])
        self.stack.close()


def v3(ap, a):
    return ap.rearrange("p (a b) -> p a b", a=a)


def v4(ap, a, b):
    return ap.rearrange("p (a b c) -> p a b c", a=a, b=b)


class _Stop(Exception):
    pass


def build_program():
    try:
        return _build_program()
    except _Stop as st:
        return st.args[0]


def _build_program():
    nc = bass.Bass("TRN2", target_bir_lowering=False)

    def din(name, shape):
        return nc.dram_tensor(name, list(shape), F32, kind="ExternalInput").ap()

    def dout(name, shape):
        return nc.dram_tensor(name, list(shape), F32, kind="ExternalOutput").ap()

    x_p = din("x_p", [LP, DM]); x_s = din("x_s", [NSMP, DM])
    st_shift = din("st_shift", [NSMP, RPROJ]); st_wkv = din("st_wkv", [128, 4096])
    st_conv = din("st_conv", [NSMP, 3072]); st_ssm = din("st_ssm", [128, 8192])
    w_in = din("w_in", [DM, PROJ]); w_up = din("w_up", [64, 512]); a_up = din("a_up", [64, 512])
    g_up = din("g_up", [128, 512]); w_out = din("w_out", [DM, DM])
    w_ff1 = din("w_ff1", [DM, DFF]); w_ff2 = din("w_ff2", [DFF, DM])
    cst = din("cst", [128, 1024]); fv = din("fv", [128, 96]); rkb = din("rkb", [128, 512])
    mu_r = din("mu_r", [1, RPROJ]); w0_r = din("w0_r", [1, 512]); a0_r = din("a0_r", [1, 512])
    kk_r = din("kk_r", [1, 512]); ka_r = din("ka_r", [1, 512]); lw_r = din("lw_r", [1, 512])
    lb_r = din("lb_r", [1, 512]); rk_r = din("rk_r", [1, 512]); cw_r = din("cw_r", [1, 4096])
    cb_r = din("cb_r", [1, 1024]); dtb_r = din("dtb_r", [1, 8]); alog_r = din("alog_r", [1, 8])
    dsk_r = din("dsk_r", [1, 8]); gnw_r = din("gnw_r", [1, 512])
    l1g_r = din("l1g_r", [1, DM]); l1b_r = din("l1b_r", [1, DM]); l2g_r = din("l2g_r", [1, DM]); l2b_r = din("l2b_r", [1, DM])

    y_p = dout("y_p", [LP, DM]); y_s = dout("y_s", [NSMP, DM])
    nsh_p = dout("nsh_p", [1, RPROJ]); nwkv_p = dout("nwkv_p", [512, 64]); ncv_p = dout("ncv_p", [3, 1024])
    nssm_p = dout("nssm_p", [512, 128]); nsh_s = dout("nsh_s", [NSMP, RPROJ]); nwkv_s = dout("nwkv_s", [128, 4096])
    ncv_s = dout("ncv_s", [NSMP, 3072]); nssm_s = dout("nssm_s", [128, 8192])
    scrA = nc.dram_tensor("scrA", [128, 7 * 64], F32, kind="Internal").ap()
    scrB = nc.dram_tensor("scrB", [128, 64], F32, kind="Internal").ap()
    scrC = nc.dram_tensor("scrC", [128, 64 + 128 + 128 + 8], F32, kind="Internal").ap()
    scrD = nc.dram_tensor("scrD", [128, 64], F32, kind="Internal").ap()

    P = Prog(nc)

    def stop(tag):
        if os.environ.get('MK_STOP') == tag:
            P.finish()
            raise _Stop(nc)

    op = P.op
    dma = P.dma
    rr = {"i": 0}

    def ev_eng():
        rr["i"] += 1
        return "act" if rr["i"] % 2 else "dve"

    def cp(E, out, in_, reads, writes):
        if E == "act":
            return op("act", lambda e: e.copy(out, in_), reads, writes)
        return op(E, lambda e: e.tensor_copy(out, in_), reads, writes)

    def mm(out, lhsT, rhs, start, stop, reads, writes):
        return op("pe", lambda e: e.matmul(out, lhsT, rhs, start=start, stop=stop), reads, writes)

    CST = P.alloc(1024); dCST = Dep()
    dma("sp", CST, cst, writes=[dCST])
    identf = CST[:, 0:128]; mSU = CST[:, 128:256]; mSL = CST[:, 256:384]; mUI = CST[:, 384:512]
    bones = CST[:, 512:640]; ones = CST[:, 640:768]; negm = CST[:, 768:896]; mask4 = CST[:, 896:1024]
    CSTB = P.alloc(512, BF16); dCSTB = Dep()
    dma("pool", CSTB[:, 0:128], cst[:, 0:128], writes=[dCSTB])
    dma("pool", CSTB[:, 128:256], cst[:, 512:640], writes=[dCSTB])
    identb = CSTB[:, 0:128]; bonesb = CSTB[:, 128:256]
    NUI = P.alloc(128)
    op("dve", lambda e: e.tensor_scalar(NUI, mUI, -1.0, None, ALU.mult), [dCST], [dCST])
    FV = P.alloc(96); dFV = Dep()
    dma("sp", FV, fv, writes=[dFV])
    OMK = P.alloc(4)
    op("dve", lambda e: e.tensor_scalar(OMK, FV[:, 26:30], -1.0, 1.0, ALU.mult, ALU.add), [dFV], [dFV])
    EPS = P.alloc(4)
    op("dve", lambda e: e.memset(EPS[:, 0:1], 64e-5), [], [dFV])
    op("dve", lambda e: e.memset(EPS[:, 1:2], 1e-5), [], [dFV])
    op("dve", lambda e: e.memset(EPS[:, 2:3], 1.0), [], [dFV])
    op("dve", lambda e: e.memset(EPS[:, 3:4], 0.5), [], [dFV])
    HB0 = P.alloc(8)
    op("dve", lambda e: e.tensor_scalar(HB0, FV[:, 14:22], -1.0, None, ALU.mult), [dFV], [dFV])
    RKB = P.alloc(512, BF16)
    dma("pool", RKB, rkb, writes=[dFV])
    WUA = P.alloc(512, BF16); GUP = P.alloc(512, BF16); dLW = Dep()
    dma("pool", WUA[0:64, :], w_up, writes=[dLW])
    dma("pool", WUA[64:128, :], a_up, writes=[dLW])
    dma("pool", GUP, g_up, writes=[dLW])
    STs = [P.alloc(8) for _ in range(4)]; dSTs = [Dep() for _ in range(4)]
    SM = P.alloc(64); dSM = Dep()
    XT = P.alloc(8 * NT, BF16); XT3 = v3(XT, 8)
    dXT = [Dep() for _ in TILES]
    MIXT = P.alloc(8 * NT, BF16); MIXT3 = v3(MIXT, 8)
    dMIX = Dep()
    mAfterMIX = P.top
    PRS = P.alloc(PROJ); dPRS = Dep()
    LUW = P.alloc(NT, BF16); LUG = P.alloc(NT, BF16); dLUW = [Dep() for _ in range(5)]; dLUG = [Dep() for _ in range(5)]
    base_top = P.top

    def xt_deps(t0, n):
        return [dXT[j] for j, (a, r) in enumerate(TILES) if a < t0 + n and a + r > t0]

    XB = [P.alloc(DM, BF16) for _ in range(2)]; dXB = [Dep(), Dep()]
    for j, (t0, rows) in enumerate(TILES):
        b = j % 2
        src = x_p[t0:t0 + rows, :] if j < 16 else x_s
        dma("pool", XB[b][0:rows, :], src, writes=[dXB[b]])
        bk, dbk = P.bank()
        psb = bk[:].bitcast(BF16)
        for dc in range(8):
            op("pe", lambda e, b=b, dc=dc, rows=rows, psb=psb: e.transpose(psb[:, dc * 128:dc * 128 + rows], XB[b][0:rows, dc * 128:(dc + 1) * 128], identb[0:rows, 0:rows]),
               [dXB[b], dCSTB], [dbk])
        cp(ev_eng(), XT3[:, :, t0:t0 + rows], v3(psb, 8)[:, :, 0:rows], [dbk], [dXT[j]])

    stop('A')
    def load_w(buf, dbuf, col0, ncols):
        w3 = v3(buf, 8)[:, :, 0:ncols]
        dma("pool", w3, w_in[:, col0:col0 + ncols].rearrange("(a p) c -> p a c", p=128), writes=[dbuf])
        return w3

    def proj_slab(w3, dw, t0, n, dst, ddst, eng="act"):
        bk, dbk = P.bank()
        for dc in range(8):
            mm(bk[:, 0:n], w3[:, dc, :], XT3[:, dc, t0:t0 + n], dc == 0, dc == 7, [dw] + xt_deps(t0, n), [dbk])
        cp(eng, dst, bk[:, 0:n], [dbk], [ddst])

    def proj_samples(w3, dw, col0, ncols):
        bk, dbk = P.bank()
        for dc in range(8):
            mm(bk[0:16, 0:ncols], XT3[:, dc, LP:NT], w3[:, dc, :], dc == 0, dc == 7, [dw, dXT[16]], [dbk])
        cp("dve", PRS[0:16, col0:col0 + ncols], bk[0:16, 0:ncols], [dbk], [dPRS])

    def col_to_dram(src_col, dram_row, reads):
        bk, dbk = P.bank()
        mm(bk[0:1, 0:128], src_col, identf, True, True, reads + [dCST], [dbk])
        cp("dve", ROWT[0:1, :], bk[0:1, 0:128], [dbk], [dROWT])
        dma("sp", dram_row, ROWT[0:1, :], reads=[dROWT])

    ROWT = P.alloc(128); dROWT = Dep()

    def bc8(ap):
        return ap.unsqueeze(2).to_broadcast([128, 8, 64])

    mM = P.top
    BC8 = P.alloc(24); dBC8 = Dep()
    dma("sp", BC8[:, 0:8], dtb_r.partition_broadcast(128), writes=[dBC8])
    dma("sp", BC8[:, 8:16], alog_r.partition_broadcast(128), writes=[dBC8])
    dma("sp", BC8[:, 16:24], dsk_r.partition_broadcast(128), writes=[dBC8])
    op("act", lambda e: e.activation(BC8[:, 8:16], BC8[:, 8:16], AF.Exp), [dBC8], [dBC8])
    op("dve", lambda e: e.tensor_scalar(BC8[:, 8:16], BC8[:, 8:16], -1.0, None, ALU.mult), [dBC8], [dBC8])
    GNW = P.alloc(512); dGNW = Dep()
    dma("sp", GNW, gnw_r.partition_broadcast(128), writes=[dGNW])
    HT32 = P.alloc(512); HTB = P.alloc(512, BF16); dHT = Dep()
    op("dve", lambda e: e.memset(HT32, 0.0), [], [dHT])
    op("dve", lambda e: e.memset(HTB, 0.0), [], [dHT])
    DT = P.alloc(128); DT3 = v3(DT, 16); ADT = P.alloc(128); ADT3 = v3(ADT, 16); dDT = Dep()
    WZ = P.alloc(8 * 512, BF16); dWZ = Dep()
    WZ3 = load_w(WZ, dWZ, RPROJ, 512)
    WDT = P.alloc(8 * 8, BF16); dWDT = Dep()
    WDT3 = load_w(WDT, dWDT, PROJ - 8, 8)
    WX = [P.alloc(8 * 128, BF16) for _ in range(3)]; dWX = [Dep(), Dep(), Dep()]
    bk, dbk = P.bank()
    for j, (t0, rows) in enumerate(TILES):
        for dc in range(8):
            mm(bk[0:rows, j * 8:(j + 1) * 8], XT3[:, dc, t0:t0 + rows], WDT3[:, dc, :], dc == 0, dc == 7, [dWDT, dXT[j]], [dbk])
    cp("dve", PRS[0:16, PROJ - 8:PROJ], bk[0:16, 128:136], [dbk], [dPRS])
    op("dve", lambda e: e.tensor_tensor(DT3, v3(bk[:, 0:128], 16), BC8[:, 0:8].unsqueeze(1).to_broadcast([128, 16, 8]), ALU.add), [dbk, dBC8], [dDT])
    op("act", lambda e: e.activation(DT, DT, AF.Exp), [dDT], [dDT])
    op("act", lambda e: e.activation(DT, DT, AF.Ln, bias=EPS[:, 2:3], scale=1.0), [dDT, dFV], [dDT])
    op("dve", lambda e: e.tensor_tensor(ADT3, DT3, BC8[:, 8:16].unsqueeze(1).to_broadcast([128, 16, 8]), ALU.mult), [dDT, dBC8], [dDT])
    stop('M1')
    proj_samples(WZ3, dWZ, RPROJ, 512)
    stop('M1b')

    XBC = P.alloc(8 * 515); XBC3 = v3(XBC, 8); dXBCs = [Dep() for _ in range(8)]
    op("dve", lambda e: e.memset(XBC, 0.0), [], dXBCs)
    ACCs = [P.alloc(512) for _ in range(3)]; dACCs = [Dep() for _ in range(3)]
    MSETS = []
    for _sb in range(2):
        MSETS.append((v3(P.alloc(8 * 512, BF16), 8), Dep(), v3(P.alloc(4 * 512, BF16), 4), Dep(), v3(P.alloc(4 * 256, BF16), 4), Dep(), v3(P.alloc(4 * 512, BF16), 4), Dep()))
    NCV = P.alloc(1024); dNCV = Dep()
    CS = P.alloc(64); dCS = Dep()
    ADTB = P.alloc(8 * 128); ADTB3 = v3(ADTB, 8); dADTB = Dep()
    XBF = P.alloc(512, BF16); XHAT = P.alloc(512, BF16); dXB2 = Dep()
    CBM = P.alloc(256); CBM3 = v3(CBM, 2); dCBM = Dep()
    LSB = P.alloc(1024); LSB3 = v3(LSB, 8); dLSB = Dep()
    GB = P.alloc(1024, BF16); GB3 = v3(GB, 8); dGB = Dep()
    T1 = P.alloc(512); T2 = P.alloc(512); dT1 = Dep(); dT2 = Dep()
    SS = P.alloc(8); dSS = Dep()
    OMB = P.alloc(512, BF16); dOMB = Dep()

    for sl_i, (s0, sn) in enumerate(SLABS[:4]):
        XSA3, dXSA, XSTOK3, dXST, BTOK3, dBT, ZS3, dZS = MSETS[sl_i % 2]
        for jj in range(4):
            j = sl_i * 4 + jj
            t0 = j * 128
            bk, dbk = P.bank()
            for dc in range(8):
                mm(bk[:, :], XT3[:, dc, t0:t0 + 128], WZ3[:, dc, :], dc == 0, dc == 7, [dWZ, dXT[j]], [dbk])
            op("act", lambda e: e.activation(ZS3[:, jj, :], bk[:, :], AF.Silu), [dbk], [dZS])
        stop('M2')
        for fc in range(8):
            wi = (sl_i * 8 + fc) % 3
            dXBC = dXBCs[fc]; ACC = ACCs[wi]; dACC = dACCs[wi]
            w3 = load_w(WX[wi], dWX[wi], RPROJ + 512 + fc * 128, 128)
            if sl_i == 0:
                proj_samples(w3, dWX[wi], RPROJ + 512 + fc * 128, 128)
            else:
                op("dve", lambda e: e.tensor_copy(XBC3[:, fc, 0:3], XBC3[:, fc, 512:515]), [dXBC], [dXBC])
            proj_slab(w3, dWX[wi], s0, 512, XBC3[:, fc, 3:515], dXBC)
            cw = lambda i: FV[:, 38 + fc * 4 + i:39 + fc * 4 + i]
            op("act", lambda e: e.activation(ACC, XBC3[:, fc, 0:512], AF.Identity, bias=FV[:, 70 + fc:71 + fc], scale=cw(0)), [dXBC, dFV], [dACC])
            for i in (1, 2, 3):
                op("dve", lambda e: e.scalar_tensor_tensor(ACC, XBC3[:, fc, i:i + 512], cw(i), ACC, ALU.mult, ALU.add), [dXBC, dFV, dACC], [dACC])
            op("act", lambda e: e.activation(XSA3[:, fc, :], ACC, AF.Silu), [dACC], [dXSA])
            if sl_i == 3:
                bk, dbk = P.bank()
                mm(bk[0:3, 0:128], XBC3[:, fc, 512:515], identf, True, True, [dXBC, dCST], [dbk])
                cp("dve", NCV[0:3, fc * 128:(fc + 1) * 128], bk[0:3, 0:128], [dbk], [dNCV])
        stop('M3')
        for jj in range(4):
            bk, dbk = P.bank()
            psb = bk[:].bitcast(BF16)
            for fc in range(6):
                op("pe", lambda e: e.transpose(psb[:, fc * 128:(fc + 1) * 128], XSA3[:, fc, jj * 128:(jj + 1) * 128], identb), [dXSA, dCSTB], [dbk])
                stop('T%d' % fc)
            cp("act", XSTOK3[:, jj, :], psb[:, 0:512], [dbk], [dXST])
            stop('T6')
            cp("act", BTOK3[:, jj, :], psb[:, 512:768], [dbk], [dBT])
            stop('T7')
        stop('M4')
        for jj in range(4):
            j = sl_i * 4 + jj
            tsl = slice(jj * 128, (jj + 1) * 128)
            gsl = slice(j * 128, (j + 1) * 128)
            bk, dbk = P.bank()
            mm(bk[:, 0:8], mUI, ADT3[:, j, :], True, True, [dCST, dDT], [dbk])
            mm(bk[:, 8:16], ones, ADT3[:, j, :], True, True, [dCST, dDT], [dbk])
            op("act", lambda e: e.activation(CS[:, 0:16], bk[:, 0:16], AF.Exp), [dbk], [dCS])
            op("dve", lambda e: e.tensor_copy(CS[:, 32:48], bk[:, 0:16]), [dbk], [dCS])
            op("dve", lambda e: e.tensor_tensor(CS[:, 24:32], CS[:, 40:48], CS[:, 32:40], ALU.subtract), [dCS], [dCS])
            op("act", lambda e: e.activation(CS[:, 16:24], CS[:, 24:32], AF.Exp), [dCS], [dCS])
            op("act", lambda e: e.copy(ADTB3, ADT3[:, j, :].unsqueeze(2).to_broadcast([128, 8, 128])), [dDT], [dADTB])
            op("dve", lambda e: e.tensor_tensor(v3(XBF, 8), v3(XSTOK3[:, jj, :], 8), bc8(DT3[:, j, :]), ALU.mult), [dXST, dDT], [dXB2])
            op("dve", lambda e: e.tensor_tensor(v3(XHAT, 8), v3(XBF, 8), bc8(CS[:, 16:24]), ALU.mult), [dXB2, dCS], [dXB2])
            stop('M5')
            bkc, dbkc = P.bank()
            for g in range(2):
                mm(bkc[:, g * 128:(g + 1) * 128], XSA3[:, 4 + g, tsl], XSA3[:, 6 + g, tsl], True, True, [dXSA], [dbkc])
            op("dve", lambda e: e.tensor_tensor(CBM3, v3(bkc[:, 0:256], 2), mUI.unsqueeze(1).to_broadcast([128, 2, 128]), ALU.mult), [dbkc, dCST], [dCBM])
            for g in range(2):
                bkd, dbkd = P.bank()
                for hh in range(4):
                    h = g * 4 + hh
                    o = bkd[:, hh * 128:(hh + 1) * 128]
                    mm(o, ADTB3[:, h, :], mUI, True, False, [dADTB, dCST], [dbkd])
                    mm(o, NUI, ADTB3[:, h, :], False, False, [dADTB, dCST], [dbkd])
                    mm(o, identf, negm, False, True, [dCST], [dbkd])
                op("act", lambda e: e.activation(LSB[:, g * 512:(g + 1) * 512], bkd[:, :], AF.Exp), [dbkd], [dLSB])
                op("dve", lambda e: e.tensor_tensor(GB3[:, g * 4:(g + 1) * 4, :], LSB3[:, g * 4:(g + 1) * 4, :], CBM3[:, g, :].unsqueeze(1).to_broadcast([128, 4, 128]), ALU.mult), [dLSB, dCBM], [dGB])
            stop('M6')
            bky, dbky = P.bank()
            for h in range(8):
                mm(bky[:, h * 64:(h + 1) * 64], GB3[:, h, :], XBF[:, h * 64:(h + 1) * 64], True, True, [dGB, dXB2], [dbky])
            bko, dbko = P.bank()
            for g in range(2):
                mm(bko[:, g * 256:(g + 1) * 256], XSA3[:, 6 + g, tsl], HTB[:, g * 256:(g + 1) * 256], True, True, [dXSA, dHT], [dbko])
            op("dve", lambda e: e.tensor_tensor(v3(T1, 8), v3(bko[:, :], 8), bc8(CS[:, 0:8]), ALU.mult), [dbko, dCS], [dT1])
            op("dve", lambda e: e.tensor_tensor(T1, T1, bky[:, :], ALU.add), [dbky, dT1], [dT1])
            op("pool", lambda e: e.tensor_tensor(v3(T2, 8), v3(XSTOK3[:, jj, :], 8), bc8(BC8[:, 16:24]), ALU.mult), [dXST, dBC8], [dT2])
            op("dve", lambda e: e.tensor_tensor(T1, T1, T2, ALU.add), [dT1, dT2], [dT1])
            op("dve", lambda e: e.tensor_tensor(T1, T1, ZS3[:, jj, :], ALU.mult), [dT1, dZS], [dT1])
            for g in range(2):
                op("act", lambda e: e.activation(T2[:, g * 256:(g + 1) * 256], T1[:, g * 256:(g + 1) * 256], AF.Square, accum_out=SS[:, g:g + 1]), [dT1, dT2], [dT2, dSS])
            op("act", lambda e: e.activation(SS[:, 2:4], SS[:, 0:2], AF.Ln, bias=EPS[:, 1:2], scale=1.0 / 256.0), [dSS, dFV], [dSS])
            op("act", lambda e: e.activation(SS[:, 4:6], SS[:, 2:4], AF.Exp, scale=-0.5), [dSS], [dSS])
            for g in range(2):
                op("dve", lambda e: e.scalar_tensor_tensor(OMB[:, g * 256:(g + 1) * 256], T1[:, g * 256:(g + 1) * 256], SS[:, 4 + g:5 + g], GNW[:, g * 256:(g + 1) * 256], ALU.mult, ALU.mult), [dT1, dSS, dGNW], [dOMB])
            stop('M7')
            bkt, dbkt = P.bank()
            psb = bkt[:].bitcast(BF16)
            for fc in range(4):
                op("pe", lambda e: e.transpose(psb[:, fc * 128:(fc + 1) * 128], OMB[:, fc * 128:(fc + 1) * 128], identb), [dOMB, dCSTB], [dbkt])
            cp("act", MIXT3[:, 4:8, gsl], v3(psb[:, 0:512], 4), [dbkt], [dMIX])
            bkh, dbkh = P.bank()
            for g in range(2):
                mm(bkh[:, g * 256:(g + 1) * 256], BTOK3[:, jj, g * 128:(g + 1) * 128], XHAT[:, g * 256:(g + 1) * 256], True, True, [dBT, dXB2], [dbkh])
            op("dve", lambda e: e.tensor_tensor(v3(HT32, 8), v3(HT32, 8), bc8(CS[:, 8:16]), ALU.mult), [dHT, dCS], [dHT])
            op("dve", lambda e: e.tensor_tensor(HT32, HT32, bkh[:, :], ALU.add), [dHT, dbkh], [dHT])
            cp("act", HTB, HT32, [dHT], [dHT])
    P.held = set()
    dma("sp", ncv_p, NCV[0:3, :], reads=[dNCV])
    for blk in range(4):
        bk, dbk = P.bank()
        mm(bk[:, 0:128], HT32[:, blk * 128:(blk + 1) * 128], identf, True, True, [dHT, dCST], [dbk])
        cp("act", T1[:, (blk % 4) * 128:(blk % 4 + 1) * 128], bk[:, 0:128], [dbk], [dT1])
    dma("sp", nssm_p.rearrange("(b p) n -> p b n", p=128), v3(T1, 4), reads=[dT1])
    P.barrier()
    P.top = mM
    stop('M')
    SHT = P.alloc(32); dSHT = Dep()
    STSH = P.alloc(256); dSTSH = Dep()
    dma("sp", STSH[0:16, :], st_shift[:, 1536:1792], writes=[dSTSH])
    for i in range(2):
        bk, dbk = P.bank()
        mm(bk[:, 0:16], STSH[0:16, i * 128:(i + 1) * 128], identf[0:16, 0:16], True, True, [dSTSH, dCST], [dbk])
        cp("dve", SHT[:, i * 16:(i + 1) * 16], bk[:, 0:16], [dbk], [dSHT])
    WR = [P.alloc(8 * 128, BF16) for _ in range(3)]; dWR = [Dep(), Dep(), Dep()]
    PT = P.alloc(3 * 513); PT3 = v3(PT, 3); dPT = Dep()
    UR = P.alloc(512); UK = P.alloc(512); SW = P.alloc(512); AA = P.alloc(512); CL = P.alloc(512); EE = P.alloc(512); KKN = P.alloc(512)
    EX1 = XB[0].bitcast(F32); EX2 = XB[1].bitcast(F32); dEX1 = dXB[0]; dEX2 = dXB[1]
    dUR = Dep(); dUK = Dep(); dSW = Dep(); dAA = Dep(); dCL = Dep(); dEE = Dep(); dKKN = Dep()
    for fc in (12, 13):
        w3 = load_w(WR[0], dWR[0], fc * 128, 128)
        proj_samples(w3, dWR[0], fc * 128, 128)
        op("dve", lambda e: e.memset(PT3[:, 0, 0:1], 0.0), [], [dPT])
        dst = LUW if fc == 12 else LUG
        for si, (s0, sn) in enumerate(SLABS):
            if si > 0 and si < 4:
                op("dve", lambda e: e.tensor_copy(PT3[:, 0, 0:1], PT3[:, 0, 512:513]), [dPT], [dPT])
            proj_slab(w3, dWR[0], s0, sn, PT3[:, 0, 1:1 + sn], dPT)
            if si == 3:
                col_to_dram(PT3[:, 0, 512:513], nsh_p[0:1, fc * 128:(fc + 1) * 128], [dPT])
            prev = PT3[:, 0, 0:sn] if si < 4 else SHT[:, (fc - 12) * 16:(fc - 11) * 16]
            op("dve", lambda e: e.tensor_tensor(CL[:, 0:sn], prev, PT3[:, 0, 1:1 + sn], ALU.subtract), [dPT, dSHT], [dCL])
            op("dve", lambda e: e.scalar_tensor_tensor(UR[:, 0:sn], CL[:, 0:sn], FV[:, fc:fc + 1], PT3[:, 0, 1:1 + sn], ALU.mult, ALU.add), [dCL, dPT, dFV], [dUR])
            if fc == 12:
                op("act", lambda e: e.activation(LUW[0:64, s0:s0 + sn], UR[0:64, 0:sn], AF.Tanh), [dUR], [dLUW[si]])
                cp("act", LUW[64:128, s0:s0 + sn], UR[64:128, 0:sn], [dUR], [dLUW[si]])
            else:
                op("act", lambda e: e.activation(UR[:, 0:sn], UR[:, 0:sn], AF.Exp, scale=-1.0), [dUR], [dUR])
                op("dve", lambda e: e.tensor_scalar(UR[:, 0:sn], UR[:, 0:sn], 1.0, None, ALU.add), [dUR], [dUR])
                op("dve", lambda e: e.reciprocal(UR[:, 0:sn], UR[:, 0:sn]), [dUR], [dUR])
                cp("act", LUG[:, s0:s0 + sn], UR[:, 0:sn], [dUR], [dLUG[si]])

    stop('R1')
    P.held = {0, 1, 2, 5, 6, 7}
    POOL_Y = [0]
    POOL_B = [1, 2]
    POOL_I = [5, 6, 7]
    SETS = []
    for _sb in range(2):
        st_ = {}
        st_["AZ"] = P.alloc(8 * 128, BF16); st_["AR"] = P.alloc(8 * 128, BF16); st_["TTA"] = P.alloc(8 * 128, BF16)
        st_["AMG"] = P.alloc(8 * 256, BF16); st_["AKZ"] = P.alloc(8 * 256, BF16); st_["ZTG"] = P.alloc(8 * 256, BF16); st_["UVZ"] = P.alloc(8 * 256, BF16)
        st_["VBF"] = P.alloc(512, BF16); st_["RKP"] = P.alloc(512, BF16); st_["PC"] = P.alloc(8)
        for nm in ("dAZ", "dAR", "dTTA", "dAMG", "dAKZ", "dZTG", "dUVZ", "dVB", "dRKP", "dPC"):
            st_[nm] = Dep(nm)
        for nm, dn in (("AZ", "dAZ"), ("AKZ", "dAKZ"), ("ZTG", "dZTG"), ("UVZ", "dUVZ")):
            op("pool", lambda e: e.memset(st_[nm], 0.0), [], [st_[dn]])
        SETS.append(st_)
    BZ = P.alloc(8 * 128, BF16); BZ3 = v3(BZ, 8)
    BK = P.alloc(8 * 128, BF16); BK3 = v3(BK, 8); BK4 = v4(BK, 8, 2)
    BKH = P.alloc(8 * 128, BF16); BKH3 = v3(BKH, 8)
    dBK = Dep()
    op("pool", lambda e: e.memset(BZ, 0.0), [], [dBK])
    SQB = P.alloc(512, BF16); dSQB = Dep()
    Y32 = P.alloc(512); YSQ = P.alloc(512); YC = P.alloc(512); M2 = P.alloc(512)
    dY32 = Dep(); dYSQ = Dep(); dYC = Dep(); dM2 = Dep()
    RST = P.alloc(512); dRST = Dep()
    op("pool", lambda e: e.memset(RST, 1.0), [], [dRST])
    op("pool", lambda e: e.memset(v3(RST, 8)[:, :, 0:1], 0.0), [], [dRST])
    S32 = P.alloc(64); SBF = P.alloc(64, BF16); SZ = [P.alloc(128, BF16) for _ in range(2)]
    dS = Dep(); dSB = Dep(); dSZ = [Dep(), Dep()]
    WSB = P.alloc(64, BF16); dWSB = Dep()
    MBs = [[P.alloc(512, BF16) for _ in range(2)] for _ in range(2)]; NBs = [[P.alloc(512, BF16) for _ in range(2)] for _ in range(2)]; PBs = [[P.alloc(512, BF16) for _ in range(2)] for _ in range(2)]
    dMBs = [[Dep(), Dep()] for _ in range(2)]; dNBs = [[Dep(), Dep()] for _ in range(2)]; dPBs = [[Dep(), Dep()] for _ in range(2)]
    SOUT = P.alloc(128); dSO = Dep()

    for hp in range(4):
        w3s = []
        for kind in range(3):
            fc = kind * 4 + hp
            w3 = load_w(WR[kind], dWR[kind], fc * 128, 128)
            proj_samples(w3, dWR[kind], fc * 128, 128)
            w3s.append(w3)
        op("dve", lambda e: e.memset(PT3[:, :, 0:1], 0.0), [], [dPT])
        op("dve", lambda e: e.memset(S32, 0.0), [], [dS])
        op("dve", lambda e: e.memset(SBF, 0.0), [], [dSB])
        op("dve", lambda e: e.memset(SZ[0], 0.0), [], [dSZ[0]])
        op("dve", lambda e: e.memset(SZ[1], 0.0), [], [dSZ[1]])
        szi = 0
        for g8, (s0, sn) in enumerate(SLABS[:4]):
            tsl = slice(s0, s0 + 512)
            st_ = SETS[(hp * 4 + g8) % 2]
            AZ = st_["AZ"]; AZ3 = v3(AZ, 8); AR = st_["AR"]; AR3 = v3(AR, 8); AR4 = v4(AR, 8, 2); TTA = st_["TTA"]; TTA3 = v3(TTA, 8)
            AMG = st_["AMG"]; AMG4 = v4(AMG, 8, 2); AKZ4 = v4(st_["AKZ"], 8, 2); ZTG4 = v4(st_["ZTG"], 8, 2); UVZ4 = v4(st_["UVZ"], 8, 2)
            VBF = st_["VBF"]; RKP = st_["RKP"]; PC = st_["PC"]
            dAZ = st_["dAZ"]; dAR = st_["dAR"]; dTTA = st_["dTTA"]; dAMG = st_["dAMG"]; dAKZ = st_["dAKZ"]; dZTG = st_["dZTG"]; dUVZ = st_["dUVZ"]
            dVB = st_["dVB"]; dRKP = st_["dRKP"]; dPC = st_["dPC"]
            if g8 > 0:
                op("dve", lambda e: e.tensor_copy(PT3[:, :, 0:1], PT3[:, :, 512:513]), [dPT], [dPT])
            for kind in range(3):
                proj_slab(w3s[kind], dWR[kind], s0, 512, PT3[:, kind, 1:513], dPT)
                if g8 == 3:
                    fc = kind * 4 + hp
                    col_to_dram(PT3[:, kind, 512:513], nsh_p[0:1, fc * 128:(fc + 1) * 128], [dPT])
            for kind, dst, dd in ((0, UR, dUR), (1, UK, dUK), (2, VBF, dVB)):
                fc = kind * 4 + hp
                op("dve", lambda e: e.tensor_tensor(CL, PT3[:, kind, 0:512], PT3[:, kind, 1:513], ALU.subtract), [dPT], [dCL])
                op("dve", lambda e: e.scalar_tensor_tensor(dst, CL, FV[:, fc:fc + 1], PT3[:, kind, 1:513], ALU.mult, ALU.add), [dCL, dPT, dFV], [dd])
            bk, dbk = P.bank()
            mm(bk[:, :], WUA[0:64, hp * 128:(hp + 1) * 128], LUW[0:64, tsl], True, True, [dLW, dLUW[g8]], [dbk])
            op("act", lambda e: e.activation(SW, bk[:, :], AF.Exp, bias=HB0[:, hp:hp + 1], scale=-1.0), [dbk, dFV], [dSW])
            op("act", lambda e: e.activation(SW, SW, AF.Identity, bias=EPS[:, 2:3], scale=1.0), [dSW, dFV], [dSW])
            op("dve", lambda e: e.reciprocal(SW, SW), [dSW], [dSW])
            bk2, dbk2 = P.bank()
            mm(bk2[:, :], WUA[64:128, hp * 128:(hp + 1) * 128], LUW[64:128, tsl], True, True, [dLW, dLUW[g8]], [dbk2])
            op("act", lambda e: e.activation(AA, bk2[:, :], AF.Exp, bias=HB0[:, 4 + hp:5 + hp], scale=-1.0), [dbk2, dFV], [dAA])
            op("act", lambda e: e.activation(AA, AA, AF.Identity, bias=EPS[:, 2:3], scale=1.0), [dAA, dFV], [dAA])
            op("dve", lambda e: e.reciprocal(AA, AA), [dAA], [dAA])
            op("act", lambda e: e.activation(KKN, UK, AF.Identity, scale=FV[:, 22 + hp:23 + hp]), [dUK, dFV], [dKKN])
            op("act", lambda e: e.activation(SQB, KKN, AF.Square), [dKKN], [dSQB])
            bk, dbk = P.bank()
            mm(bk[:, :], bonesb, SQB, True, True, [dCSTB, dSQB], [dbk])
            op("dve", lambda e: e.tensor_scalar(EE, bk[:, :], 1e-24, None, ALU.max), [dbk], [dEE])
            op("act", lambda e: e.activation(EE, EE, AF.Ln), [dEE], [dEE])
            op("act", lambda e: e.activation(EE, EE, AF.Exp, scale=-0.5), [dEE], [dEE])
            op("dve", lambda e: e.tensor_tensor(KKN, KKN, EE, ALU.mult), [dKKN, dEE], [dKKN])
            op("act", lambda e: e.activation(EE, AA, AF.Identity, bias=OMK[:, hp:hp + 1], scale=FV[:, 26 + hp:27 + hp]), [dAA, dFV], [dEE])
            op("dve", lambda e: e.tensor_tensor(UK, UK, EE, ALU.mult), [dUK, dEE], [dUK])
            op("dve", lambda e: e.tensor_tensor(RKP, UR, UK, ALU.mult), [dUR, dUK], [dRKP])
            op("dve", lambda e: e.tensor_tensor_scan(CL, RST, SW, 0.0, ALU.mult, ALU.add), [dRST, dSW], [dCL])
            op("dve", lambda e: e.tensor_tensor(SW, CL, SW, ALU.subtract), [dCL, dSW], [dSW])
            op("act", lambda e: e.activation(EE, SW, AF.Exp, scale=-C0), [dSW], [dEE])
            op("dve", lambda e: e.scalar_tensor_tensor(AR4[:, :, 0, :], v3(KKN, 8), -1.0, v3(EE, 8), ALU.mult, ALU.mult), [dKKN, dEE], [dAR])
            op("act", lambda e: e.activation(EX1, CL, AF.Exp, scale=-C0), [dCL], [dEX1])
            op("dve", lambda e: e.tensor_tensor(AR4[:, :, 1, :], v3(UR, 8), v3(EX1, 8), ALU.mult), [dUR, dEX1], [dAR])
            op("act", lambda e: e.copy(PC, v3(EX1, 8)[:, :, 63]), [dEX1], [dPC])
            op("dve", lambda e: e.tensor_tensor(KKN, KKN, AA, ALU.mult), [dKKN, dAA], [dKKN])
            op("act", lambda e: e.activation(EX2, CL, AF.Exp, scale=C0), [dCL], [dEX2])
            op("dve", lambda e: e.tensor_tensor(BK4[:, :, 0, :], v3(KKN, 8), v3(EX2, 8), ALU.mult), [dKKN, dEX2], [dBK])
            op("dve", lambda e: e.tensor_tensor(BK4[:, :, 1, :], v3(UK, 8), v3(EX2, 8), ALU.mult), [dUK, dEX2], [dBK])
            op("dve", lambda e: e.tensor_tensor(BKH3, BK3, PC.unsqueeze(2).to_broadcast([128, 8, 128]), ALU.mult), [dBK, dPC], [dBK])
            for hh in range(2):
                ps_ = slice(hh * 64, (hh + 1) * 64)
                cp("act", AZ3[ps_, :, hh * 64:(hh + 1) * 64], AR4[ps_, :, 0, :], [dAR], [dAZ])
                cp("act", BZ3[ps_, :, hh * 64:(hh + 1) * 64], BK4[ps_, :, 0, :], [dBK], [dBK])
            stop('R2')
            for gq in range(2):
                MB = MBs[gq]; NB = NBs[gq]; PB = PBs[gq]; dMB = dMBs[gq]; dNB = dNBs[gq]; dPB = dPBs[gq]
                bm, dbm = P.bank(pool=POOL_I); bn, dbn = P.bank(pool=POOL_I)
                for i in range(4):
                    c = gq * 4 + i
                    mm(bm[:, i * 128:(i + 1) * 128], BZ3[:, c, :], AZ3[:, c, :], True, True, [dBK, dAZ], [dbm])
                    mm(bn[:, i * 128:(i + 1) * 128], AZ3[:, c, :], BZ3[:, c, :], True, True, [dBK, dAZ], [dbn])
                cur = 0
                op("dve", lambda e: e.tensor_tensor(v3(MB[0], 4), v3(bm[:, :], 4), mSU.unsqueeze(1).to_broadcast([128, 4, 128]), ALU.mult), [dbm, dCST], [dMB[0]])
                op("dve", lambda e: e.tensor_tensor(v3(NB[0], 4), v3(bn[:, :], 4), mSL.unsqueeze(1).to_broadcast([128, 4, 128]), ALU.mult), [dbn, dCST], [dNB[0]])
                op("dve", lambda e: e.tensor_tensor(v3(PB[0], 4), v3(MB[0], 4), identb.unsqueeze(1).to_broadcast([128, 4, 128]), ALU.add), [dMB[0], dCSTB], [dPB[0]])
                for lvl in range(1, 6):
                    nx = 1 - cur
                    if lvl <= 4:
                        bm, dbm = P.bank(pool=POOL_I)
                        for i in range(4):
                            sl = slice(i * 128, (i + 1) * 128)
                            mm(bm[:, sl], NB[cur][:, sl], MB[cur][:, sl], True, True, [dNB[cur], dMB[cur]], [dbm])
                    bn, dbn = P.bank(pool=POOL_I)
                    for i in range(4):
                        sl = slice(i * 128, (i + 1) * 128)
                        mm(bn[:, sl], MB[cur][:, sl], NB[cur][:, sl], True, True, [dNB[cur], dMB[cur]], [dbn])
                    if lvl <= 4:
                        cp("act", MB[nx], bm[:, :], [dbm], [dMB[nx]])
                    cp("dve", NB[nx], bn[:, :], [dbn], [dNB[nx]])
                    bp, dbp = P.bank(pool=POOL_I)
                    for i in range(4):
                        sl = slice(i * 128, (i + 1) * 128)
                        mm(bp[:, sl], NB[nx][:, sl], PB[cur][:, sl], True, False, [dNB[nx], dPB[cur]], [dbp])
                        mm(bp[:, sl], identb, PB[cur][:, sl], False, True, [dCSTB, dPB[cur]], [dbp])
                    if lvl < 5:
                        cp("act", PB[nx], bp[:, :], [dbp], [dPB[nx]])
                    else:
                        cp("act", TTA[:, gq * 512:(gq + 1) * 512], bp[:, :], [dbp], [dTTA])
                    cur = nx
            stop('R3')
            for half in range(2):
                bks = [P.bank(pool=POOL_I), P.bank(pool=POOL_I)]
                for ci4 in range(4):
                    ci = half * 4 + ci4
                    for hh in range(2):
                        ps_ = slice(hh * 64, (hh + 1) * 64)
                        bk, dbk = bks[hh]
                        mm(bk[:, ci4 * 128:(ci4 + 1) * 128], BK3[ps_, ci, :], AR3[ps_, ci, :], True, True, [dBK, dAR], [dbk])
                for hh in range(2):
                    bk, dbk = bks[hh]
                    op("dve", lambda e: e.tensor_tensor(AMG4[:, half * 4:half * 4 + 4, hh, :], v3(bk[:, :], 4), mask4.unsqueeze(1).to_broadcast([128, 4, 128]), ALU.mult), [dbk, dCST], [dAMG])
                    op("dve", lambda e: e.tensor_tensor(AKZ4[64:128, half * 4:half * 4 + 4, hh, hh * 64:(hh + 1) * 64], v3(bk[64:128, :], 4)[:, :, 0:64], mask4[64:128, 0:64].unsqueeze(1).to_broadcast([64, 4, 64]), ALU.mult), [dbk, dCST], [dAKZ])
            stop('R3b')
            bk, dbk = P.bank(pool=POOL_I)
            psb = bk[:].bitcast(BF16)
            for ci in range(8):
                op("pe", lambda e: e.transpose(psb[:, ci * 128:(ci + 1) * 128], BKH3[:, ci, :], identb), [dBK, dCSTB], [dbk])
            for hh in range(2):
                cp("act" if hh else "dve", ZTG4[:, :, hh, hh * 64:(hh + 1) * 64], v3(psb, 8)[:, :, hh * 64:(hh + 1) * 64], [dbk], [dZTG])
            stop('R3c')
            bk, dbk = P.bank(pool=POOL_I)
            psb = bk[:].bitcast(BF16)
            for ti in range(4):
                op("pe", lambda e: e.transpose(psb[:, ti * 128:(ti + 1) * 128], VBF[:, ti * 128:(ti + 1) * 128], identb), [dVB, dCSTB], [dbk])
            UVZ5 = st_["UVZ"].rearrange("p (t q h c) -> p t q h c", t=4, q=2, h=2)
            for par in range(2):
                for hh in range(2):
                    cp("act" if hh else "dve", UVZ5[64:128, :, par, hh, hh * 64:(hh + 1) * 64], v3(psb[par * 64:(par + 1) * 64, 0:512], 4)[:, :, hh * 64:(hh + 1) * 64], [dbk], [dUVZ])
            stop('R4')
            bky, dbky = P.bank(pool=POOL_Y)
            for ci in range(8):
                bw, dbw = P.bank(pool=POOL_B)
                mm(bw[:, 0:64], AZ3[:, ci, :], SBF, True, False, [dAZ, dSB], [dbw])
                for hh in range(2):
                    mm(bw[:, 0:64], AKZ4[64:128, ci, hh, :], UVZ4[64:128, ci, hh, hh * 64:(hh + 1) * 64], False, hh == 1, [dAKZ, dUVZ], [dbw])
                cp("act", WSB, bw[:, 0:64], [dbw], [dWSB])
                bu, dbu = P.bank(pool=POOL_B)
                mm(bu[:, 0:64], TTA3[:, ci, :], WSB, True, True, [dTTA, dWSB], [dbu])
                cp("dve", UVZ4[0:64, ci, 0, 0:64], bu[0:64, 0:64], [dbu], [dUVZ])
                cp("act", UVZ4[0:64, ci, 1, 64:128], bu[64:128, 0:64], [dbu], [dUVZ])
                yo = bky[:, ci * 64:(ci + 1) * 64]
                mm(yo, SZ[szi], AR4[:, ci, 1, :], True, False, [dSZ[szi], dAR], [dbky])
                for hh in range(2):
                    mm(yo, UVZ4[:, ci, hh, :], AMG4[:, ci, hh, 64:128], False, hh == 1, [dUVZ, dAMG], [dbky])
                bs, dbs = P.bank(pool=POOL_B)
                for hh in range(2):
                    mm(bs[:, 0:64], ZTG4[:, ci, hh, :], UVZ4[:, ci, hh, hh * 64:(hh + 1) * 64], hh == 0, hh == 1, [dZTG, dUVZ], [dbs])
                op("dve", lambda e: e.scalar_tensor_tensor(SBF, S32, PC[:, ci:ci + 1], bs[:, 0:64], ALU.mult, ALU.add), [dS, dPC, dbs], [dSB])
                op("dve", lambda e: e.scalar_tensor_tensor(S32, S32, PC[:, ci:ci + 1], bs[:, 0:64], ALU.mult, ALU.add), [dS, dPC, dbs], [dS])
                szi = 1 - szi
                for hh in range(2):
                    ps_ = slice(hh * 64, (hh + 1) * 64)
                    cp("act", SZ[szi][ps_, hh * 64:(hh + 1) * 64], S32[ps_, :], [dS], [dSZ[szi]])
            stop('R5')
            cp("act", Y32, bky[:, :], [dbky], [dY32])
            op("act", lambda e: e.activation(YSQ, Y32, AF.Square), [dY32], [dYSQ])
            stop('Ea')
            b1, db1 = P.bank(pool=POOL_B); b2, db2 = P.bank(pool=POOL_B)
            mm(b1[:, :], bones, Y32, True, True, [dCST, dY32], [db1])
            mm(b2[:, :], bones, YSQ, True, True, [dCST, dYSQ], [db2])
            stop('Ea1')
            op("dve", lambda e: e.scalar_tensor_tensor(YC, b1[:, :], -1.0 / 64.0, Y32, ALU.mult, ALU.add), [db1, dY32], [dYC])
            stop('Ea2')
            op("act", lambda e: e.activation(M2, b1[:, :], AF.Square, scale=1.0 / 64.0), [db1], [dM2])
            stop('Ea3')
            op("dve", lambda e: e.scalar_tensor_tensor(M2, b2[:, :], 1.0 / 64.0, M2, ALU.mult, ALU.subtract), [db2, dM2], [dM2])
            stop('Ea4')
            op("act", lambda e: e.activation(M2, M2, AF.Ln, bias=EPS[:, 0:1], scale=1.0), [dM2, dFV], [dM2])
            stop('Ea5')
            op("act", lambda e: e.activation(M2, M2, AF.Exp, scale=-0.5), [dM2], [dM2])
            stop('Ea6')
            op("dve", lambda e: e.tensor_tensor(YC, YC, M2, ALU.mult), [dYC, dM2], [dYC])
            stop('Ea7')
            op("act", lambda e: e.activation(YC, YC, AF.Identity, bias=FV[:, 34 + hp:35 + hp], scale=FV[:, 30 + hp:31 + hp]), [dYC, dFV], [dYC])
            stop('Ea8')
            stop('Eb')
            b3, db3 = P.bank(pool=POOL_B)
            mm(b3[:, :], RKB[:, hp * 128:(hp + 1) * 128], RKP, True, True, [dFV, dRKP], [db3])
            op("dve", lambda e: e.tensor_tensor(M2, b3[:, :], VBF, ALU.mult), [db3, dVB, dM2], [dM2])
            op("dve", lambda e: e.tensor_tensor(YC, YC, M2, ALU.add), [dYC, dM2], [dYC])
            b4, db4 = P.bank(pool=POOL_B)
            mm(b4[:, :], GUP[:, hp * 128:(hp + 1) * 128], LUG[:, tsl], True, True, [dLW, dLUG[g8]], [db4])
            op("dve", lambda e: e.tensor_tensor(MIXT3[:, hp, tsl], YC, b4[:, :], ALU.mult), [db4, dYC], [dMIX])
            stop('E%d' % g8)
        stop('R6')
        bk, dbk = P.bank(pool=POOL_B)
        mm(bk[0:64, 0:128], S32, identf, True, True, [dS, dCST], [dbk])
        cp("dve", SOUT[0:64, :], bk[0:64, 0:128], [dbk], [dSO])
        dma("sp", nwkv_p.rearrange("(h v) k -> v h k", v=64)[:, 2 * hp:2 * hp + 2, :], v3(SOUT[0:64, :], 2), reads=[dSO])
    dma("sp", nsh_s, PRS[0:16, 0:RPROJ], reads=[dPRS])
    P.barrier()
    P.held = set()
    P.top = mM
    stop('R')
    R16 = slice(0, 16)
    OS = P.alloc(1024); dOS = Dep()
    mS = P.top
    U = P.alloc(RPROJ); dU = Dep(); dLD = Dep()
    VEC = P.alloc(7 * 512); VEC3 = v3(VEC, 7)
    Q = P.alloc(6 * 512); Q3 = v3(Q, 6); dQ = Dep()
    TS = P.alloc(4 * 512); TS3 = v3(TS, 4); dTS = Dep()
    QB = P.alloc(6 * 64); QB3 = v3(QB, 6); dQB = Dep()
    YSB = P.alloc(128); dYSB = Dep()
    YT = P.alloc(512); dYT = Dep()
    mS1 = P.top
    MU = P.alloc(RPROJ); SHP = P.alloc(RPROJ)
    dma("sp", MU[R16, :], mu_r.partition_broadcast(16), writes=[dLD])
    dma("sp", SHP[R16, :], st_shift, writes=[dLD])
    for i, src in enumerate((w0_r, a0_r, kk_r, ka_r, lw_r, lb_r, rk_r)):
        dma("sp", VEC3[R16, i, :], src.partition_broadcast(16), writes=[dLD])
    W0v, A0v, KKv, KAv, LWv, LBv, RKv = [VEC3[R16, i, :] for i in range(7)]
    op("dve", lambda e: e.tensor_tensor(U[R16, :], SHP[R16, :], PRS[R16, 0:RPROJ], ALU.subtract), [dLD, dPRS], [dU])
    op("dve", lambda e: e.tensor_tensor(U[R16, :], U[R16, :], MU[R16, :], ALU.mult), [dU, dLD], [dU])
    op("dve", lambda e: e.tensor_tensor(U[R16, :], U[R16, :], PRS[R16, 0:RPROJ], ALU.add), [dU, dPRS], [dU])
    rS, kS, vS = U[R16, 0:512], U[R16, 512:1024], U[R16, 1024:1536]
    h8 = lambda ap: ap.rearrange("p (h k) -> p h k", h=8)
    b8 = lambda ap: ap.unsqueeze(2).to_broadcast([16, 8, 64])
    bk, dbk = P.bank()
    mm(bk[R16, :], LUW[0:64, LP:NT], WUA[0:64, :], True, True, [dLUW[4], dLW], [dbk])
    op("dve", lambda e: e.tensor_tensor(Q3[R16, 1, :], bk[R16, :], W0v, ALU.add), [dbk, dLD], [dQ])
    op("act", lambda e: e.activation(Q3[R16, 1, :], Q3[R16, 1, :], AF.Sigmoid), [dQ], [dQ])
    op("act", lambda e: e.activation(Q3[R16, 1, :], Q3[R16, 1, :], AF.Exp, scale=-C0), [dQ], [dQ])
    bk, dbk = P.bank()
    mm(bk[R16, :], LUW[64:128, LP:NT], WUA[64:128, :], True, True, [dLUW[4], dLW], [dbk])
    op("dve", lambda e: e.tensor_tensor(TS3[R16, 0, :], bk[R16, :], A0v, ALU.add), [dbk, dLD], [dTS])
    op("act", lambda e: e.activation(TS3[R16, 0, :], TS3[R16, 0, :], AF.Sigmoid), [dTS], [dTS])
    bk, dbk = P.bank()
    mm(bk[R16, :], LUG[:, LP:NT], GUP, True, True, [dLUG[4], dLW], [dbk])
    cp("act", TS3[R16, 3, :], bk[R16, :], [dbk], [dTS])
    aS = TS3[R16, 0, :]; kkS = TS3[R16, 1, :]; tS = TS3[R16, 2, :]; gS = TS3[R16, 3, :]
    op("dve", lambda e: e.tensor_copy(Q3[R16, 0, :], rS), [dU], [dQ])
    op("dve", lambda e: e.tensor_copy(Q3[R16, 3, :], vS), [dU], [dQ])
    op("dve", lambda e: e.tensor_tensor(kkS, kS, KKv, ALU.mult), [dU, dLD], [dTS])
    op("dve", lambda e: e.tensor_tensor(tS, kkS, kkS, ALU.mult), [dTS], [dTS])
    op("dve", lambda e: e.reduce_sum(SM[R16, 0:8], h8(tS), AX.X), [dTS], [dSM])
    op("dve", lambda e: e.tensor_scalar(SM[R16, 0:8], SM[R16, 0:8], 1e-24, None, ALU.max), [dSM], [dSM])
    op("act", lambda e: e.activation(SM[R16, 0:8], SM[R16, 0:8], AF.Sqrt), [dSM], [dSM])
    op("dve", lambda e: e.reciprocal(SM[R16, 0:8], SM[R16, 0:8]), [dSM], [dSM])
    op("dve", lambda e: e.tensor_tensor(h8(kkS), h8(kkS), b8(SM[R16, 0:8]), ALU.mult), [dTS, dSM], [dTS])
    op("dve", lambda e: e.tensor_scalar(Q3[R16, 4, :], kkS, -1.0, None, ALU.mult), [dTS], [dQ])
    op("dve", lambda e: e.tensor_tensor(Q3[R16, 5, :], kkS, aS, ALU.mult), [dTS], [dQ])
    op("dve", lambda e: e.tensor_tensor(tS, aS, KAv, ALU.mult), [dTS, dLD], [dTS])
    op("dve", lambda e: e.tensor_tensor(tS, tS, KAv, ALU.subtract), [dTS, dLD], [dTS])
    op("dve", lambda e: e.tensor_scalar(tS, tS, 1.0, None, ALU.add), [dTS], [dTS])
    op("dve", lambda e: e.tensor_tensor(Q3[R16, 2, :], kS, tS, ALU.mult), [dU, dTS], [dQ])
    op("dve", lambda e: e.tensor_tensor(tS, rS, Q3[R16, 2, :], ALU.mult), [dU, dQ], [dTS])
    op("dve", lambda e: e.tensor_tensor(tS, tS, RKv, ALU.mult), [dTS, dLD], [dTS])
    op("dve", lambda e: e.reduce_sum(SM[R16, 8:16], h8(tS), AX.X), [dTS], [dSM])
    dSCR = Dep()
    for q_ in range(6):
        dma("sp", scrA.rearrange("(b h) (q k) -> b q h k", h=8, k=64)[:, q_], Q3[R16, q_, :].rearrange("p (h k) -> p h k", h=8), reads=[dQ], writes=[dSCR])
    dma("sp", QB, scrA[:, 0:384], reads=[dSCR], writes=[dQB])
    SW_ = P.alloc(4096); TMP = P.alloc(4096); dSt = Dep(); dTmp = Dep()
    dma("sp", SW_, st_wkv, writes=[dSt])
    S3_ = v3(SW_, 64); T3_ = v3(TMP, 64)
    bv = lambda q: QB3[:, q, :].unsqueeze(1).to_broadcast([128, 64, 64])
    bkk = lambda ap: ap.unsqueeze(2).to_broadcast([128, 64, 64])
    op("dve", lambda e: e.tensor_tensor(T3_, S3_, bv(4), ALU.mult), [dSt, dQB], [dTmp])
    op("dve", lambda e: e.reduce_sum(YSB[:, 64:128], T3_, AX.X), [dTmp], [dYSB])
    op("dve", lambda e: e.tensor_tensor(S3_, S3_, bv(1), ALU.mult), [dSt, dQB, dTmp], [dSt])
    op("dve", lambda e: e.tensor_tensor(T3_, bkk(YSB[:, 64:128]), bv(5), ALU.mult), [dYSB, dQB], [dTmp])
    op("dve", lambda e: e.tensor_tensor(S3_, S3_, T3_, ALU.add), [dSt, dTmp], [dSt])
    op("dve", lambda e: e.tensor_tensor(T3_, bkk(QB3[:, 3, :]), bv(2), ALU.mult), [dQB, dSt], [dTmp])
    op("dve", lambda e: e.tensor_tensor(S3_, S3_, T3_, ALU.add), [dSt, dTmp], [dSt])
    dma("sp", nwkv_s, SW_, reads=[dSt])
    op("dve", lambda e: e.tensor_tensor(T3_, S3_, bv(0), ALU.mult), [dSt, dQB], [dTmp])
    op("dve", lambda e: e.reduce_sum(YSB[:, 0:64], T3_, AX.X), [dTmp], [dYSB])
    dSCR2 = Dep()
    dma("sp", scrB, YSB[:, 0:64], reads=[dYSB], writes=[dSCR2])
    dma("sp", YT[R16, :], scrB.rearrange("(b h) v -> b (h v)", h=8), reads=[dSCR2], writes=[dYT])
    yT = YT[R16, :]
    op("dve", lambda e: e.reduce_sum(SM[R16, 16:24], h8(yT), AX.X), [dYT], [dSM])
    op("dve", lambda e: e.tensor_scalar(SM[R16, 16:24], SM[R16, 16:24], 1.0 / 64.0, None, ALU.mult), [dSM], [dSM])
    op("dve", lambda e: e.tensor_tensor(h8(yT), h8(yT), b8(SM[R16, 16:24]), ALU.subtract), [dYT, dSM], [dYT])
    op("dve", lambda e: e.tensor_tensor(tS, yT, yT, ALU.mult), [dYT, dTS], [dTS])
    op("dve", lambda e: e.reduce_sum(SM[R16, 24:32], h8(tS), AX.X), [dTS], [dSM])
    op("dve", lambda e: e.tensor_scalar(SM[R16, 24:32], SM[R16, 24:32], 1.0 / 64.0, 64e-5, ALU.mult, ALU.add), [dSM], [dSM])
    op("act", lambda e: e.activation(SM[R16, 24:32], SM[R16, 24:32], AF.Sqrt), [dSM], [dSM])
    op("dve", lambda e: e.reciprocal(SM[R16, 24:32], SM[R16, 24:32]), [dSM], [dSM])
    op("dve", lambda e: e.tensor_tensor(h8(yT), h8(yT), b8(SM[R16, 24:32]), ALU.mult), [dYT, dSM], [dYT])
    op("dve", lambda e: e.tensor_tensor(yT, yT, LWv, ALU.mult), [dYT, dLD], [dYT])
    op("dve", lambda e: e.tensor_tensor(yT, yT, LBv, ALU.add), [dYT, dLD], [dYT])
    op("dve", lambda e: e.tensor_tensor(h8(tS), h8(vS), b8(SM[R16, 8:16]), ALU.mult), [dU, dSM, dTS], [dTS])
    op("dve", lambda e: e.tensor_tensor(yT, yT, tS, ALU.add), [dYT, dTS], [dYT])
    op("dve", lambda e: e.tensor_tensor(OS[R16, 0:512], yT, gS, ALU.mult), [dYT, dTS], [dOS])
    P.barrier()
    P.top = mS
    XA = P.alloc(1024); dLD2 = Dep(); dXA = Dep()
    TT_ = P.alloc(1024); dTT = Dep()
    B8 = P.alloc(32); dB8 = Dep()
    GNS = P.alloc(512)
    PK = P.alloc(8 * 328); PK3 = v3(PK, 8); dPK = Dep()
    PKB = P.alloc(328); dPKB = Dep()
    YM = P.alloc(64); dYM = Dep()
    YMT = P.alloc(512); dYMT = Dep()
    OSB = P.alloc(1024, BF16)
    mS2 = P.top
    CW = P.alloc(4096); CB = P.alloc(1024); SCV = P.alloc(3072)
    dma("sp", CW[R16, :], cw_r.partition_broadcast(16), writes=[dLD2])
    dma("sp", CB[R16, :], cb_r.partition_broadcast(16), writes=[dLD2])
    dma("sp", SCV[R16, :], st_conv, writes=[dLD2])
    dma("sp", ncv_s[:, 0:2048], SCV[R16, 1024:3072], reads=[dLD2])
    dma("sp", ncv_s[:, 2048:3072], PRS[R16, RPROJ + 512:RPROJ + 1536], reads=[dPRS])
    op("dve", lambda e: e.tensor_tensor(XA[R16, :], PRS[R16, RPROJ + 512:RPROJ + 1536], CW[R16, 3072:4096], ALU.mult), [dPRS, dLD2], [dXA])
    op("dve", lambda e: e.tensor_tensor(XA[R16, :], XA[R16, :], CB[R16, :], ALU.add), [dXA, dLD2], [dXA])
    for i in range(3):
        op("dve", lambda e: e.tensor_tensor(TT_[R16, :], SCV[R16, i * 1024:(i + 1) * 1024], CW[R16, i * 1024:(i + 1) * 1024], ALU.mult), [dLD2], [dTT])
        op("dve", lambda e: e.tensor_tensor(XA[R16, :], XA[R16, :], TT_[R16, :], ALU.add), [dXA, dTT], [dXA])
    op("act", lambda e: e.activation(XA[R16, :], XA[R16, :], AF.Silu), [dXA], [dXA])
    dma("sp", B8[R16, 0:8], dtb_r.partition_broadcast(16), writes=[dB8])
    dma("sp", B8[R16, 8:16], alog_r.partition_broadcast(16), writes=[dB8])
    dma("sp", B8[R16, 16:24], dsk_r.partition_broadcast(16), writes=[dB8])
    dma("sp", GNS[R16, :], gnw_r.partition_broadcast(16), writes=[dB8])
    op("act", lambda e: e.activation(B8[R16, 8:16], B8[R16, 8:16], AF.Exp), [dB8], [dB8])
    op("dve", lambda e: e.tensor_tensor(B8[R16, 24:32], PRS[R16, PROJ - 8:PROJ], B8[R16, 0:8], ALU.add), [dPRS, dB8], [dB8])
    op("act", lambda e: e.activation(B8[R16, 24:32], B8[R16, 24:32], AF.Exp), [dB8], [dB8])
    op("act", lambda e: e.activation(B8[R16, 24:32], B8[R16, 24:32], AF.Ln, bias=EPS[R16, 2:3], scale=1.0), [dB8, dFV], [dB8])
    op("dve", lambda e: e.memset(PK[R16, :], 0.0), [], [dPK])
    op("dve", lambda e: e.tensor_tensor(PK3[R16, :, 0:64], h8(XA[R16, 0:512]), b8(B8[R16, 24:32]), ALU.mult), [dXA, dB8], [dPK])
    for g in range(2):
        op("dve", lambda e: e.tensor_copy(PK3[R16, 4 * g:4 * g + 4, 64:192], XA[R16, 512 + g * 128:640 + g * 128].unsqueeze(1).to_broadcast([16, 4, 128])), [dXA], [dPK])
        op("dve", lambda e: e.tensor_copy(PK3[R16, 4 * g:4 * g + 4, 192:320], XA[R16, 768 + g * 128:896 + g * 128].unsqueeze(1).to_broadcast([16, 4, 128])), [dXA], [dPK])
    op("dve", lambda e: e.tensor_tensor(B8[R16, 8:16], B8[R16, 8:16], B8[R16, 24:32], ALU.mult), [dB8], [dB8])
    op("act", lambda e: e.activation(PK3[R16, :, 320], B8[R16, 8:16], AF.Exp, scale=-1.0), [dB8, dPK], [dPK])
    dSCR3 = Dep()
    dma("sp", scrC.rearrange("(b h) c -> b (h c)", h=8), PK[R16, :], reads=[dPK], writes=[dSCR3])
    dma("sp", PKB, scrC, reads=[dSCR3], writes=[dPKB])
    HS = P.alloc(4096); TM2 = P.alloc(4096); dHS = Dep(); dTM2 = Dep()
    H3 = v3(HS, 32); M3 = v3(TM2, 32)
    for half in range(2):
        dma("sp", HS, st_ssm[:, half * 4096:(half + 1) * 4096], writes=[dHS])
        op("dve", lambda e: e.tensor_scalar(HS, HS, PKB[:, 320:321], None, ALU.mult), [dHS, dPKB], [dHS])
        op("dve", lambda e: e.tensor_tensor(M3, PKB[:, half * 32:(half + 1) * 32].unsqueeze(2).to_broadcast([128, 32, 128]), PKB[:, 64:192].unsqueeze(1).to_broadcast([128, 32, 128]), ALU.mult), [dPKB], [dTM2])
        op("dve", lambda e: e.tensor_tensor(HS, HS, TM2, ALU.add), [dHS, dTM2], [dHS])
        dma("sp", nssm_s[:, half * 4096:(half + 1) * 4096], HS, reads=[dHS])
        op("dve", lambda e: e.tensor_tensor(M3, H3, PKB[:, 192:320].unsqueeze(1).to_broadcast([128, 32, 128]), ALU.mult), [dHS, dPKB], [dTM2])
        op("dve", lambda e: e.reduce_sum(YM[:, half * 32:(half + 1) * 32], M3, AX.X), [dTM2], [dYM])
    dSCR4 = Dep()
    dma("sp", scrD, YM, reads=[dYM], writes=[dSCR4])
    dma("sp", YMT[R16, :], scrD.rearrange("(b h) v -> b (h v)", h=8), reads=[dSCR4], writes=[dYMT])
    ym = YMT[R16, :]
    op("dve", lambda e: e.tensor_tensor(h8(TT_[R16, 0:512]), h8(XA[R16, 0:512]), b8(B8[R16, 16:24]), ALU.mult), [dXA, dB8], [dTT])
    op("dve", lambda e: e.tensor_tensor(ym, ym, TT_[R16, 0:512], ALU.add), [dYMT, dTT], [dYMT])
    op("act", lambda e: e.activation(TT_[R16, 512:1024], PRS[R16, RPROJ:RPROJ + 512], AF.Silu), [dPRS, dTT], [dTT])
    op("dve", lambda e: e.tensor_tensor(ym, ym, TT_[R16, 512:1024], ALU.mult), [dYMT, dTT], [dYMT])
    op("dve", lambda e: e.tensor_tensor(TT_[R16, 0:512], ym, ym, ALU.mult), [dYMT, dTT], [dTT])
    op("dve", lambda e: e.reduce_sum(SM[R16, 32:34], TT_[R16, 0:512].rearrange("p (g c) -> p g c", g=2), AX.X), [dTT], [dSM])
    op("dve", lambda e: e.tensor_scalar(SM[R16, 32:34], SM[R16, 32:34], 1.0 / 256.0, 1e-5, ALU.mult, ALU.add), [dSM], [dSM])
    op("act", lambda e: e.activation(SM[R16, 32:34], SM[R16, 32:34], AF.Sqrt), [dSM], [dSM])
    op("dve", lambda e: e.reciprocal(SM[R16, 32:34], SM[R16, 32:34]), [dSM], [dSM])
    op("dve", lambda e: e.tensor_tensor(ym.rearrange("p (g c) -> p g c", g=2), ym.rearrange("p (g c) -> p g c", g=2), SM[R16, 32:34].unsqueeze(2).to_broadcast([16, 2, 256]), ALU.mult), [dYMT, dSM], [dYMT])
    op("dve", lambda e: e.tensor_tensor(OS[R16, 512:1024], ym, GNS[R16, :], ALU.mult), [dYMT, dB8], [dOS])
    cp("act", OSB[R16, :], OS[R16, :], [dOS], [dOS])
    bk, dbk = P.bank()
    psb = bk[:].bitcast(BF16)
    for dc in range(8):
        op("pe", lambda e: e.transpose(psb[:, dc * 16:(dc + 1) * 16], OSB[R16, dc * 128:(dc + 1) * 128], identb[0:16, 0:16]), [dOS, dCSTB], [dbk])
    cp("dve", MIXT3[:, :, LP:NT], v3(psb[:, 0:128], 8), [dbk], [dMIX])
    P.barrier()
    P.top = mM

    stop('S')
    def ln_tile(V, rows, G, Bv, dV, dGB, par=0):
        rs_ = slice(0, rows)
        ST = STs[par]; dST = dSTs[par]; JUNK = JUNKs[par]; dJ = dJs[par]
        op("act", lambda e: e.activation(JUNK[rs_, :], V[rs_, :], AF.Copy, accum_out=ST[rs_, 0:1]), [dV, dST], [dJ, dST])
        op("act", lambda e: e.activation(JUNK[rs_, :], V[rs_, :], AF.Square, accum_out=ST[rs_, 1:2]), [dV, dST], [dJ, dST])
        op("dve", lambda e: e.tensor_scalar(ST[rs_, 2:3], ST[rs_, 0:1], 1.0 / DM, None, ALU.mult), [dST], [dST])
        op("dve", lambda e: e.tensor_tensor(ST[rs_, 3:4], ST[rs_, 2:3], ST[rs_, 2:3], ALU.mult), [dST], [dST])
        op("dve", lambda e: e.scalar_tensor_tensor(ST[rs_, 4:5], ST[rs_, 1:2], 1.0 / DM, ST[rs_, 3:4], ALU.mult, ALU.subtract), [dST], [dST])
        op("act", lambda e: e.activation(ST[rs_, 5:6], ST[rs_, 4:5], AF.Sqrt, bias=EPS[rs_, 1:2], scale=1.0), [dST, dFV], [dST])
        op("dve", lambda e: e.reciprocal(ST[rs_, 6:7], ST[rs_, 5:6]), [dST], [dST])
        op("dve", lambda e: e.scalar_tensor_tensor(ST[rs_, 7:8], ST[rs_, 2:3], -1.0, ST[rs_, 6:7], ALU.mult, ALU.mult), [dST], [dST])
        op("act", lambda e: e.activation(V[rs_, :], V[rs_, :], AF.Identity, bias=ST[rs_, 7:8], scale=ST[rs_, 6:7]), [dV, dST], [dV])
        op("dve", lambda e: e.tensor_tensor(V[rs_, :], V[rs_, :], G[rs_, :], ALU.mult), [dV, dGB], [dV])
        op("dve", lambda e: e.tensor_tensor(V[rs_, :], V[rs_, :], Bv[rs_, :], ALU.add), [dV, dGB], [dV])

    P.top = mAfterMIX
    W2 = P.alloc(32 * DM, BF16); W23 = v3(W2, 32); dW2 = Dep()
    mAfterW2 = P.top
    mO = P.top
    WO = P.alloc(8 * DM, BF16); WO3 = v3(WO, 8); dWO = Dep()
    dma("pool", WO3, w_out.rearrange("(a p) c -> p a c", p=128), writes=[dWO])
    for q in range(8):
        dma("pool", W23[:, q * 4:(q + 1) * 4, :], w_ff2[q * 512:(q + 1) * 512, :].rearrange("(a p) c -> p a c", p=128), writes=[dW2])
    LG = P.alloc(DM); LB_ = P.alloc(DM); dLG = Dep()
    dma("sp", LG, l1g_r.partition_broadcast(128), writes=[dLG])
    dma("sp", LB_, l1b_r.partition_broadcast(128), writes=[dLG])
    XIN = [P.alloc(DM) for _ in range(3)]; dXIN = [Dep() for _ in range(3)]
    V32 = [P.alloc(DM) for _ in range(3)]; dV32 = [Dep() for _ in range(3)]
    JUNKs = [P.alloc(DM, BF16) for _ in range(3)]; dJs = [Dep() for _ in range(3)]
    HBs = [P.alloc(DM, BF16) for _ in range(3)]; dHBs = [Dep() for _ in range(3)]
    HT3 = XT3
    dYD = [Dep() for _ in TILES]

    def ytile(j):
        t0, rows = TILES[j]
        return y_p[t0:t0 + rows, :] if j < 16 else y_s

    for j, (t0, rows) in enumerate(TILES):
        b = j % 3
        rs_ = slice(0, rows)
        dma("sp", XIN[b][rs_, :], x_p[t0:t0 + rows, :] if j < 16 else x_s, writes=[dXIN[b]])
        for half in range(2):
            bk, dbk = P.bank()
            for dc in range(8):
                mm(bk[rs_, :], MIXT3[:, dc, t0:t0 + rows], WO3[:, dc, half * 512:(half + 1) * 512], dc == 0, dc == 7, [dMIX, dWO], [dbk])
            op("dve", lambda e: e.scalar_tensor_tensor(V32[b][rs_, half * 512:(half + 1) * 512], XIN[b][rs_, half * 512:(half + 1) * 512], ALPHA, bk[rs_, :], ALU.mult, ALU.add), [dXIN[b], dbk], [dV32[b]])
        ln_tile(V32[b], rows, LG, LB_, dV32[b], dLG, b)
        HB = HBs[b]; dHB = dHBs[b]
        dma("sp", ytile(j), V32[b][rs_, :], reads=[dV32[b]], writes=[dYD[j]])
        cp("act", HB[rs_, :], V32[b][rs_, :], [dV32[b]], [dHB])
        bk, dbk = P.bank()
        psb = bk[:].bitcast(BF16)
        for dc in range(8):
            op("pe", lambda e: e.transpose(psb[:, dc * 128:dc * 128 + rows], HB[rs_, dc * 128:(dc + 1) * 128], identb[0:rows, 0:rows]), [dHB, dCSTB], [dbk])
        cp(ev_eng(), HT3[:, :, t0:t0 + rows], v3(psb, 8)[:, :, 0:rows], [dbk], [dXT[j]])
    P.barrier()

    stop('O')
    P.top = mAfterW2
    F1T = MIXT[:, 0:32 * 512]; F1T3 = v3(F1T, 32); dF1 = Dep()
    NW1 = 4
    W1 = [P.alloc(8 * 512, BF16) for _ in range(NW1)]; dW1 = [Dep() for _ in range(NW1)]
    RL = [P.alloc(512) for _ in range(2)]; dRL = [Dep(), Dep()]
    RLS = [P.alloc(16) for _ in range(2)]; dRLS = [Dep(), Dep()]
    F1S = P.alloc(32 * 16, BF16); F1S3 = v3(F1S, 32); dF1S = Dep()
    LG2 = P.alloc(DM); LB2 = P.alloc(DM); dLG2 = Dep()
    dma("sp", LG2, l2g_r.partition_broadcast(128), writes=[dLG2])
    dma("sp", LB2, l2b_r.partition_broadcast(128), writes=[dLG2])
    HIN = [P.alloc(DM) for _ in range(2)]; dHIN = [Dep(), Dep()]
    VV = [P.alloc(DM) for _ in range(2)]; dVV = [Dep(), Dep()]
    JUNKs = [P.alloc(DM, BF16), P.alloc(DM, BF16)]; dJs = [Dep(), Dep()]
    w1i = 0; rli = 0; tcount = 0
    for (s0, sn) in SLABS[:4]:
        last_blk = (s0 == 1536)
        for jg in range(8):
            wb = w1i % NW1; w1i += 1
            w13 = v3(W1[wb], 8)
            dma("pool", w13, w_ff1[:, jg * 512:(jg + 1) * 512].rearrange("(a p) c -> p a c", p=128), writes=[dW1[wb]])
            for jc in range(4):
                bk, dbk = P.bank()
                for dc in range(8):
                    mm(bk[:, 0:sn], w13[:, dc, jc * 128:(jc + 1) * 128], HT3[:, dc, s0:s0 + sn], dc == 0, dc == 7, [dW1[wb]] + xt_deps(s0, sn), [dbk])
                rb = rli % 2; rli += 1
                op("act", lambda e: e.activation(RL[rb][:, 0:sn], bk[:, 0:sn], AF.Relu), [dbk], [dRL[rb]])
                op("dve", lambda e: e.tensor_tensor(F1T3[:, jg * 4 + jc, 0:sn], RL[rb][:, 0:sn], RL[rb][:, 0:sn], ALU.mult), [dRL[rb]], [dF1])
                if last_blk:
                    bk, dbk = P.bank()
                    for dc in range(8):
                        mm(bk[:, 0:16], w13[:, dc, jc * 128:(jc + 1) * 128], HT3[:, dc, LP:NT], dc == 0, dc == 7, [dW1[wb], dXT[16]], [dbk])
                    op("act", lambda e: e.activation(RLS[rb][:, 0:16], bk[:, 0:16], AF.Relu), [dbk], [dRLS[rb]])
                    op("dve", lambda e: e.tensor_tensor(F1S3[:, jg * 4 + jc, :], RLS[rb][:, 0:16], RLS[rb][:, 0:16], ALU.mult), [dRLS[rb]], [dF1S])
        for j, (t0, rows) in enumerate(TILES):
            if not ((t0 >= s0 and t0 < s0 + sn) or (last_blk and j == 16)):
                continue
            b = tcount % 2; tcount += 1
            rs_ = slice(0, rows)
            lo = t0 - s0
            dma("sp", HIN[b][rs_, :], ytile(j), reads=[dYD[j]], writes=[dHIN[b]])
            for half in range(2):
                bk, dbk = P.bank()
                for jc in range(32):
                    if j == 16:
                        mm(bk[rs_, :], F1S3[:, jc, :], W23[:, jc, half * 512:(half + 1) * 512], jc == 0, jc == 31, [dF1S, dW2], [dbk])
                    else:
                        mm(bk[rs_, :], F1T3[:, jc, lo:lo + rows], W23[:, jc, half * 512:(half + 1) * 512], jc == 0, jc == 31, [dF1, dW2], [dbk])
                op("dve", lambda e: e.scalar_tensor_tensor(VV[b][rs_, half * 512:(half + 1) * 512], HIN[b][rs_, half * 512:(half + 1) * 512], ALPHA, bk[rs_, :], ALU.mult, ALU.add), [dHIN[b], dbk], [dVV[b]])
            ln_tile(VV[b], rows, LG2, LB2, dVV[b], dLG2, b)
            dma("sp", ytile(j), VV[b][rs_, :], reads=[dVV[b], dHIN[b]], writes=[dYD[j]])
    P.finish()
    return nc


def _consts():
    r = np.arange(128)
    c = np.arange(128)
    R, Cc = np.meshgrid(r, c, indexing="ij")
    ident = (R == Cc).astype(np.float32)
    sblk = (R // 64 == Cc // 64)
    mSU = (R < Cc).astype(np.float32)
    mSL = (R > Cc).astype(np.float32)
    mUI = (R <= Cc).astype(np.float32)
    bones = sblk.astype(np.float32)
    ones = np.ones((128, 128), np.float32)
    negm = np.where(R > Cc, -30000.0, 0.0).astype(np.float32)
    s_ = R % 64
    mask4 = np.where(Cc < 64, s_ < Cc, s_ <= (Cc - 64)).astype(np.float32)
    return np.concatenate([ident, mSU, mSL, mUI, bones, ones, negm, mask4], axis=1)


_NC_CACHE = {}


def kernel(x_prompt, x_sample, state_shift, state_wkv, state_conv, state_ssm, w_in, mu_shift,
           w0, w_up, a0, a_up, g_up, k_k, k_a, r_k, lnx_w, lnx_b, conv_w, conv_b, dt_bias,
           a_log, d_skip, gnorm_w, w_out, ln1_g, ln1_b, w_ff1, w_ff2, ln2_g, ln2_b):
    f = lambda a: np.ascontiguousarray(np.asarray(a, dtype=np.float32))
    x_prompt = f(x_prompt); x_sample = f(x_sample)
    fv = np.zeros((128, 96), np.float32)
    fv[:, 0:14] = f(mu_shift)[0].reshape(14, 128).T
    for i, v in enumerate((w0, a0, k_k, k_a, lnx_w, lnx_b)):
        fv[:, 14 + 4 * i:18 + 4 * i] = f(v)[0].reshape(4, 128).T
    cw = f(conv_w)[0]
    for fc in range(8):
        for i in range(4):
            fv[:, 38 + fc * 4 + i] = cw[i, fc * 128:(fc + 1) * 128]
    fv[:, 70:78] = f(conv_b)[0].reshape(8, 128).T
    rk = f(r_k)[0]
    rkb = np.zeros((128, 4, 128), np.float32)
    for hp in range(4):
        for hh in range(2):
            rkb[hh * 64:(hh + 1) * 64, hp, hh * 64:(hh + 1) * 64] = rk[2 * hp + hh][:, None]
    rkb = rkb.reshape(128, 512)
    cst = _consts()
    row = lambda a: f(a).reshape(1, -1)
    common = {
        "w_in": f(w_in)[0], "w_up": f(w_up)[0], "a_up": f(a_up)[0], "g_up": f(g_up)[0], "w_out": f(w_out)[0],
        "w_ff1": f(w_ff1)[0], "w_ff2": f(w_ff2)[0], "cst": cst, "fv": fv, "rkb": rkb,
        "mu_r": row(mu_shift), "w0_r": row(w0), "a0_r": row(a0), "kk_r": row(k_k), "ka_r": row(k_a),
        "lw_r": row(lnx_w), "lb_r": row(lnx_b), "rk_r": row(r_k), "cw_r": row(conv_w), "cb_r": row(conv_b),
        "dtb_r": row(dt_bias), "alog_r": row(a_log), "dsk_r": row(d_skip), "gnw_r": row(gnorm_w),
        "l1g_r": row(ln1_g), "l1b_r": row(ln1_b), "l2g_r": row(ln2_g), "l2b_r": row(ln2_b),
    }
    in_maps = []
    for c in range(NCORES):
        sl = slice(c * NSMP, (c + 1) * NSMP)
        m = dict(common)
        m["x_p"] = x_prompt[c]
        m["x_s"] = x_sample[sl, 0, :]
        m["st_shift"] = f(state_shift)[0, sl]
        m["st_wkv"] = f(state_wkv)[0, sl].reshape(128, 4096)
        m["st_conv"] = f(state_conv)[0, sl].reshape(NSMP, 3072)
        m["st_ssm"] = f(state_ssm)[0, sl].reshape(128, 8192)
        in_maps.append({k: np.ascontiguousarray(v) for k, v in m.items()})
    nc = build_program()
    res = run_bass_kernel_spmd(nc, in_maps, core_ids=list(range(NCORES)))
    R = res.results
    cat = lambda k: np.stack([np.asarray(r[k], dtype=np.float32) for r in R])
    y_p = cat("y_p")
    y_s = cat("y_s").reshape(128, 1, DM)
    nsh_p = cat("nsh_p").reshape(1, 8, RPROJ)
    nwkv_p = cat("nwkv_p").reshape(1, 8, 8, 64, 64)
    ncv_p = cat("ncv_p").reshape(1, 8, 3, 1024)
    nssm_p = cat("nssm_p").reshape(1, 8, 8, 64, 128)
    nsh_s = cat("nsh_s").reshape(1, 128, RPROJ)
    nwkv_s = cat("nwkv_s").reshape(1, 128, 8, 64, 64)
    ncv_s = cat("ncv_s").reshape(1, 128, 3, 1024)
    nssm_s = cat("nssm_s").reshape(1, 128, 8, 64, 128)
    return (y_p, y_s, nsh_p, nwkv_p, ncv_p, nssm_p, nsh_s, nwkv_s, ncv_s, nssm_s)
```

```python
import contextlib
import os
import math
import numpy as np
import concourse.bass as bass
import concourse.mybir as mybir
from concourse.bass_utils import run_bass_kernel_spmd

F32 = mybir.dt.float32
BF16 = mybir.dt.bfloat16
AF = mybir.ActivationFunctionType
ALU = mybir.AluOpType
AX = mybir.AxisListType

ENGS = ("pe", "act", "dve", "pool", "sp")
NCORES = 8
LP = 2048
NSMP = 16
NT = LP + NSMP
DM = 1024
RPROJ = 1792
PROJ = 3336
DFF = 4096
C0 = math.exp(-0.5)
ALPHA = 2.0 ** 0.25
SLABS = [(0, 512), (512, 512), (1024, 512), (1536, 512), (2048, 16)]
TILES = [(i * 128, 128) for i in range(16)] + [(2048, 16)]
ARENA_W = 53000


class Dep:
    __slots__ = ("wn", "rn", "name", "excl")

    def __init__(self, name="", excl=False):
        self.wn = None
        self.rn = []
        self.name = name
        self.excl = excl


class _Rec:
    def __getattr__(self, name):
        def f(*a, **k):
            self.last = (name, a, k)
            return self
        return f


class Node:
    __slots__ = ("eng", "kind", "payload", "preds", "cost", "seg", "order", "t0", "t1", "idx", "ev", "prio", "nsucc", "succs", "npend")

    def __init__(self, eng, kind, payload, cost, seg, order):
        self.eng = eng
        self.kind = kind
        self.payload = payload
        self.preds = {}
        self.cost = cost
        self.seg = seg
        self.order = order
        self.t0 = 0.0
        self.t1 = 0.0
        self.idx = 0
        self.ev = None
        self.prio = 0.0
        self.succs = []
        self.npend = 0


def _free_elems(ap):
    n = 1
    for d in ap.shape[1:]:
        n *= int(d)
    return n


def _op_cost(E, rec):
    name, a, k = rec
    try:
        out = a[0]
        n = _free_elems(out)
        if E == "pe":
            passes = 4 if (len(a) > 1 and a[1].dtype == F32) else 1
            return 0.035 + n * passes / 2400.0
        if E == "pool":
            return 0.4 + n / 240.0
        return 0.17 + n / 960.0
    except Exception:
        return 0.5


class Prog:
    def __init__(self, nc, n_dma_sems=48):
        self.nc = nc
        self.n_dma_sems = n_dma_sems
        self.stack = contextlib.ExitStack()
        self.sem = {}
        for e in ENGS:
            self.sem[e] = self.stack.enter_context(nc.semaphore("s_" + e))
        for i in range(n_dma_sems):
            self.sem[("dma", i)] = self.stack.enter_context(nc.semaphore("s_dma%d" % i))
        self.arena = self.stack.enter_context(nc.sbuf_tensor("arena", [128, ARENA_W], F32))
        self.top = 0
        self.banks = []
        for i in range(8):
            t = self.stack.enter_context(nc.psum_tensor("bank%d" % i, [128, 512], F32))
            self.banks.append((t, Dep("bank%d" % i, excl=True)))
        self.bank_rr = 0
        self.held = set()
        self.nodes = []
        self.seg = 0
        self.pools = {}

    def bank(self, hold=False, pool=None):
        if pool is not None:
            lst = pool
            k = self.pools.get(tuple(lst), 0)
            self.pools[tuple(lst)] = k + 1
            return self.banks[lst[k % len(lst)]]
        while self.bank_rr in self.held:
            self.bank_rr = (self.bank_rr + 1) % 8
        i = self.bank_rr
        self.bank_rr = (self.bank_rr + 1) % 8
        if hold:
            self.held.add(i)
        return self.banks[i]

    def release(self, bank):
        for i, b in enumerate(self.banks):
            if b[0] is bank[0]:
                self.held.discard(i)

    def alloc(self, cols, dt=F32):
        w = cols if dt == F32 else (cols + 1) // 2
        off = self.top
        self.top += w
        assert self.top <= ARENA_W, ("arena overflow", self.top)
        ap = self.arena[:, off:off + w]
        if dt != F32:
            ap = ap.bitcast(dt)[:, 0:cols]
        return ap

    def _link(self, node, reads, writes):
        for d in reads:
            if d.wn is not None:
                node.preds[d.wn] = True
            if d.excl:
                for r in d.rn:
                    if r.eng != node.eng:
                        node.preds[r] = True
        for d in writes:
            if d.wn is not None:
                node.preds[d.wn] = True
            for r in d.rn:
                if r is node:
                    continue
                hard = (r.eng != node.eng) or (r.kind == "dma") or (node.kind == "dma")
                if r not in node.preds or hard:
                    node.preds[r] = hard or node.preds.get(r, False)
        for d in reads:
            d.rn.append(node)
        for d in writes:
            d.wn = node
            d.rn = []
        node.preds.pop(node, None)

    def op(self, E, fn, reads=(), writes=()):
        rec = _Rec()
        fn(rec)
        node = Node(E, "op", rec.last, _op_cost(E, rec.last), self.seg, len(self.nodes))
        self._link(node, reads, writes)
        self.nodes.append(node)
        return node

    def dma(self, E, out, in_, reads=(), writes=()):
        try:
            nbytes = int(out.shape[0]) * _free_elems(out) * (4 if out.dtype == F32 else 2)
        except Exception:
            nbytes = 1 << 16
        cost = 2.0 + nbytes / 120e3
        node = Node(E, "dma", (out, in_), cost, self.seg, len(self.nodes))
        self._link(node, reads, writes)
        self.nodes.append(node)
        return node

    def barrier(self):
        self.seg += 1

    def _schedule_segment(self, nodes, t_base):
        import heapq
        inseg = set(nodes)
        for n in nodes:
            n.succs = []
        for n in nodes:
            n.npend = 0
            for p in n.preds:
                if p in inseg:
                    p.succs.append(n)
                    n.npend += 1
        for n in reversed(nodes):
            best = 0.0
            for s_ in n.succs:
                if s_.prio > best:
                    best = s_.prio
            n.prio = best + n.cost + 0.3
        efree = {e: t_base for e in ENGS}
        ready = [n for n in nodes if n.npend == 0]
        order = []
        LAT = 0.45
        while ready:
            bestn = None
            bestkey = None
            for n in ready:
                t = efree[n.eng]
                for p in n.preds:
                    if p in inseg:
                        tp = p.t1 + (LAT if (p.eng != n.eng or p.kind == "dma") else 0.05)
                        if p.eng == n.eng and not n.preds[p]:
                            tp = p.t0
                        if tp > t:
                            t = tp
                key = (t, -n.prio, n.order)
                if bestkey is None or key < bestkey:
                    bestkey = key
                    bestn = n
            n = bestn
            ready.remove(n)
            n.t0 = bestkey[0]
            if n.kind == "dma":
                n.t1 = n.t0 + n.cost
                efree[n.eng] = n.t0 + (1.0 if n.eng == "pool" else 0.1)
            else:
                n.t1 = n.t0 + n.cost
                efree[n.eng] = n.t1
            order.append(n)
            for s_ in n.succs:
                s_.npend -= 1
                if s_.npend == 0:
                    ready.append(s_)
        assert len(order) == len(nodes)
        t_end = max([n.t1 for n in nodes] + [t_base])
        return order, t_end

    def finish(self):
        streams = {e: [] for e in ENGS}
        count = {e: 0 for e in ENGS}
        known = {e: {} for e in ENGS}
        dma_val = [0] * self.n_dma_sems
        dma_rr = 0

        def wait(E, key, val):
            if val == 0:
                return
            if known[E].get(key, 0) >= val:
                return
            known[E][key] = val
            streams[E].append(("wait", key, val))

        def full_barrier():
            for E in ENGS:
                for X in ENGS:
                    if X != E and count[X]:
                        wait(E, X, count[X])
                for k in range(self.n_dma_sems):
                    if dma_val[k]:
                        wait(E, ("dma", k), dma_val[k])

        nseg = self.seg + 1
        segs = [[] for _ in range(nseg)]
        for n in self.nodes:
            segs[n.seg].append(n)
        t_base = 0.0
        for si, seg_nodes in enumerate(segs):
            if si > 0:
                full_barrier()
            if not seg_nodes:
                continue
            order, t_base = self._schedule_segment(seg_nodes, t_base)
            need = set()
            last = {}
            pos = {}
            for i_, n in enumerate(order):
                pos[n] = i_
            waitsets = {}
            for n in order:
                if n.kind == "op":
                    last[n.eng] = n
                best = {}
                for p, hard in n.preds.items():
                    if p.kind == "dma" or p not in pos:
                        continue
                    if p.eng == n.eng and n.kind != "dma" and (n.eng == "pe" or not hard):
                        continue
                    b_ = best.get(p.eng)
                    if b_ is None or pos[p] > pos[b_]:
                        best[p.eng] = p
                waitsets[n] = set(best.values())
                need.update(best.values())
            for n in last.values():
                need.add(n)
            for n in order:
                E = n.eng
                ws_ = waitsets[n]
                for p, hard in n.preds.items():
                    if p.ev is None:
                        continue
                    if p.kind != "dma" and p in pos and p not in ws_:
                        continue
                    key, val = p.ev
                    if p.kind != "dma" and p.eng == E:
                        if E == "pe" or not hard:
                            continue
                    wait(E, key, val)
                if n.kind == "op":
                    if n in need:
                        count[E] += 1
                        n.ev = (E, count[E])
                        streams[E].append(("op", n.payload, n.ev))
                    else:
                        n.ev = None
                        streams[E].append(("opq", n.payload))
                else:
                    k = dma_rr
                    dma_rr = (dma_rr + 1) % self.n_dma_sems
                    key = ("dma", k)
                    wait(E, key, dma_val[k])
                    dma_val[k] += 16
                    n.ev = (key, dma_val[k])
                    streams[E].append(("dma", n.payload[0], n.payload[1], n.ev))
        self.est_us = t_base
        for k in range(self.n_dma_sems):
            if dma_val[k]:
                wait("sp", ("dma", k), dma_val[k])
        for e in ENGS:
            if e != "sp" and count[e]:
                wait("sp", e, count[e])
        self.count = count
        self.streams = streams
        sem = self.sem

        def replay(eng, items):
            for it in items:
                if it[0] == "wait":
                    eng.wait_ge(sem[it[1]], it[2])
                elif it[0] == "op":
                    nm, a, k = it[1]
                    getattr(eng, nm)(*a, **k).then_inc(sem[it[2][0]], 1)
                elif it[0] == "opq":
                    nm, a, k = it[1]
                    getattr(eng, nm)(*a, **k)
                else:
                    eng.dma_start(out=it[1], in_=it[2]).then_inc(sem[it[3][0]], 16)

        with self.nc.Block() as block:
            @block.tensor
            def _(e):
                replay(e, streams["pe"])

            @block.scalar
            def _(e):
                replay(e, streams["act"])

            @block.vector
            def _(e):
                replay(e, streams["dve"])

            @block.gpsimd
            def _(e):
                replay(e, streams["pool"])

            @block.sync
            def _(e):
                replay(e, streams["sp"])
        self.stack.close()


def v3(ap, a):
    return ap.rearrange("p (a b) -> p a b", a=a)


def v4(ap, a, b):
    return ap.rearrange("p (a b c) -> p a b c", a=a, b=b)


class _Stop(Exception):
    pass


def build_program():
    try:
        return _build_program()
    except _Stop as st:
        return st.args[0]


def _build_program():
    nc = bass.Bass("TRN2", target_bir_lowering=False)

    def din(name, shape):
        return nc.dram_tensor(name, list(shape), F32, kind="ExternalInput").ap()

    def dout(name, shape):
        return nc.dram_tensor(name, list(shape), F32, kind="ExternalOutput").ap()

    x_p = din("x_p", [LP, DM]); x_s = din("x_s", [NSMP, DM])
    st_shift = din("st_shift", [NSMP, RPROJ]); st_wkv = din("st_wkv", [128, 4096])
    st_conv = din("st_conv", [NSMP, 3072]); st_ssm = din("st_ssm", [128, 8192])
    w_in = din("w_in", [DM, PROJ]); w_up = din("w_up", [64, 512]); a_up = din("a_up", [64, 512])
    g_up = din("g_up", [128, 512]); w_out = din("w_out", [DM, DM])
    w_ff1 = din("w_ff1", [DM, DFF]); w_ff2 = din("w_ff2", [DFF, DM])
    cst = din("cst", [128, 1024]); fv = din("fv", [128, 96]); rkb = din("rkb", [128, 512])
    mu_r = din("mu_r", [1, RPROJ]); w0_r = din("w0_r", [1, 512]); a0_r = din("a0_r", [1, 512])
    kk_r = din("kk_r", [1, 512]); ka_r = din("ka_r", [1, 512]); lw_r = din("lw_r", [1, 512])
    lb_r = din("lb_r", [1, 512]); rk_r = din("rk_r", [1, 512]); cw_r = din("cw_r", [1, 4096])
    cb_r = din("cb_r", [1, 1024]); dtb_r = din("dtb_r", [1, 8]); alog_r = din("alog_r", [1, 8])
    dsk_r = din("dsk_r", [1, 8]); gnw_r = din("gnw_r", [1, 512])
    l1g_r = din("l1g_r", [1, DM]); l1b_r = din("l1b_r", [1, DM]); l2g_r = din("l2g_r", [1, DM]); l2b_r = din("l2b_r", [1, DM])

    y_p = dout("y_p", [LP, DM]); y_s = dout("y_s", [NSMP, DM])
    nsh_p = dout("nsh_p", [1, RPROJ]); nwkv_p = dout("nwkv_p", [512, 64]); ncv_p = dout("ncv_p", [3, 1024])
    nssm_p = dout("nssm_p", [512, 128]); nsh_s = dout("nsh_s", [NSMP, RPROJ]); nwkv_s = dout("nwkv_s", [128, 4096])
    ncv_s = dout("ncv_s", [NSMP, 3072]); nssm_s = dout("nssm_s", [128, 8192])
    scrA = nc.dram_tensor("scrA", [128, 7 * 64], F32, kind="Internal").ap()
    scrB = nc.dram_tensor("scrB", [128, 64], F32, kind="Internal").ap()
    scrC = nc.dram_tensor("scrC", [128, 64 + 128 + 128 + 8], F32, kind="Internal").ap()
    scrD = nc.dram_tensor("scrD", [128, 64], F32, kind="Internal").ap()

    P = Prog(nc)

    def stop(tag):
        if os.environ.get('MK_STOP') == tag:
            P.finish()
            raise _Stop(nc)

    op = P.op
    dma = P.dma
    rr = {"i": 0}

    def ev_eng():
        rr["i"] += 1
        return "act" if rr["i"] % 2 else "dve"

    def cp(E, out, in_, reads, writes):
        if E == "act":
            return op("act", lambda e: e.copy(out, in_), reads, writes)
        return op(E, lambda e: e.tensor_copy(out, in_), reads, writes)

    def mm(out, lhsT, rhs, start, stop, reads, writes):
        return op("pe", lambda e: e.matmul(out, lhsT, rhs, start=start, stop=stop), reads, writes)

    CST = P.alloc(1024); dCST = Dep()
    dma("sp", CST, cst, writes=[dCST])
    identf = CST[:, 0:128]; mSU = CST[:, 128:256]; mSL = CST[:, 256:384]; mUI = CST[:, 384:512]
    bones = CST[:, 512:640]; ones = CST[:, 640:768]; negm = CST[:, 768:896]; mask4 = CST[:, 896:1024]
    CSTB = P.alloc(512, BF16); dCSTB = Dep()
    dma("pool", CSTB[:, 0:128], cst[:, 0:128], writes=[dCSTB])
    dma("pool", CSTB[:, 128:256], cst[:, 512:640], writes=[dCSTB])
    identb = CSTB[:, 0:128]; bonesb = CSTB[:, 128:256]
    NUI = P.alloc(128)
    op("dve", lambda e: e.tensor_scalar(NUI, mUI, -1.0, None, ALU.mult), [dCST], [dCST])
    FV = P.alloc(96); dFV = Dep()
    dma("sp", FV, fv, writes=[dFV])
    OMM = P.alloc(14)
    op("dve", lambda e: e.tensor_scalar(OMM, FV[:, 0:14], -1.0, 1.0, ALU.mult, ALU.add), [dFV], [dFV])
    OMK = P.alloc(4)
    op("dve", lambda e: e.tensor_scalar(OMK, FV[:, 26:30], -1.0, 1.0, ALU.mult, ALU.add), [dFV], [dFV])
    EPS = P.alloc(4)
    op("dve", lambda e: e.memset(EPS[:, 0:1], 64e-5), [], [dFV])
    op("dve", lambda e: e.memset(EPS[:, 1:2], 1e-5), [], [dFV])
    op("dve", lambda e: e.memset(EPS[:, 2:3], 1.0), [], [dFV])
    op("dve", lambda e: e.memset(EPS[:, 3:4], 0.5), [], [dFV])
    HB0 = P.alloc(8)
    op("dve", lambda e: e.tensor_scalar(HB0, FV[:, 14:22], -1.0, None, ALU.mult), [dFV], [dFV])
    RKB = P.alloc(512, BF16)
    dma("pool", RKB, rkb, writes=[dFV])
    WUA = P.alloc(512, BF16); GUP = P.alloc(512, BF16); dLW = Dep()
    dma("pool", WUA[0:64, :], w_up, writes=[dLW])
    dma("pool", WUA[64:128, :], a_up, writes=[dLW])
    dma("pool", GUP, g_up, writes=[dLW])
    STs = [P.alloc(8) for _ in range(4)]; dSTs = [Dep() for _ in range(4)]
    SM = P.alloc(64); dSM = Dep()
    XT = P.alloc(8 * NT, BF16); XT3 = v3(XT, 8)
    dXT = [Dep() for _ in TILES]
    MIXT = P.alloc(8 * NT, BF16); MIXT3 = v3(MIXT, 8)
    dMIX = Dep()
    mAfterMIX = P.top
    PRS = P.alloc(PROJ); dPRS = Dep()
    LUW = P.alloc(NT, BF16); LUG = P.alloc(NT, BF16); dLUW = [Dep() for _ in range(5)]; dLUG = [Dep() for _ in range(5)]
    base_top = P.top

    def xt_deps(t0, n):
        return [dXT[j] for j, (a, r) in enumerate(TILES) if a < t0 + n and a + r > t0]

    XB = [P.alloc(DM, BF16) for _ in range(2)]; dXB = [Dep(), Dep()]
    for j, (t0, rows) in enumerate(TILES):
        b = j % 2
        src = x_p[t0:t0 + rows, :] if j < 16 else x_s
        dma("pool", XB[b][0:rows, :], src, writes=[dXB[b]])
        bk, dbk = P.bank()
        psb = bk[:].bitcast(BF16)
        for dc in range(8):
            op("pe", lambda e, b=b, dc=dc, rows=rows, psb=psb: e.transpose(psb[:, dc * 128:dc * 128 + rows], XB[b][0:rows, dc * 128:(dc + 1) * 128], identb[0:rows, 0:rows]),
               [dXB[b], dCSTB], [dbk])
        cp(ev_eng(), XT3[:, :, t0:t0 + rows], v3(psb, 8)[:, :, 0:rows], [dbk], [dXT[j]])

    stop('A')
    def load_w(buf, dbuf, col0, ncols):
        w3 = v3(buf, 8)[:, :, 0:ncols]
        dma("pool", w3, w_in[:, col0:col0 + ncols].rearrange("(a p) c -> p a c", p=128), writes=[dbuf])
        return w3

    def proj_slab(w3, dw, t0, n, dst, ddst, eng="act"):
        bk, dbk = P.bank()
        for dc in range(8):
            mm(bk[:, 0:n], w3[:, dc, :], XT3[:, dc, t0:t0 + n], dc == 0, dc == 7, [dw] + xt_deps(t0, n), [dbk])
        cp(eng, dst, bk[:, 0:n], [dbk], [ddst])

    def proj_samples(w3, dw, col0, ncols):
        bk, dbk = P.bank()
        for dc in range(8):
            mm(bk[0:16, 0:ncols], XT3[:, dc, LP:NT], w3[:, dc, :], dc == 0, dc == 7, [dw, dXT[16]], [dbk])
        cp("dve", PRS[0:16, col0:col0 + ncols], bk[0:16, 0:ncols], [dbk], [dPRS])

    def col_to_dram(src_col, dram_row, reads):
        bk, dbk = P.bank()
        mm(bk[0:1, 0:128], src_col, identf, True, True, reads + [dCST], [dbk])
        cp("dve", ROWT[0:1, :], bk[0:1, 0:128], [dbk], [dROWT])
        dma("sp", dram_row, ROWT[0:1, :], reads=[dROWT])

    ROWT = P.alloc(128); dROWT = Dep()

    def bc8(ap):
        return ap.unsqueeze(2).to_broadcast([128, 8, 64])

    mM = P.top
    BC8 = P.alloc(24); dBC8 = Dep()
    dma("sp", BC8[:, 0:8], dtb_r.partition_broadcast(128), writes=[dBC8])
    dma("sp", BC8[:, 8:16], alog_r.partition_broadcast(128), writes=[dBC8])
    dma("sp", BC8[:, 16:24], dsk_r.partition_broadcast(128), writes=[dBC8])
    op("act", lambda e: e.activation(BC8[:, 8:16], BC8[:, 8:16], AF.Exp), [dBC8], [dBC8])
    op("dve", lambda e: e.tensor_scalar(BC8[:, 8:16], BC8[:, 8:16], -1.0, None, ALU.mult), [dBC8], [dBC8])
    GNW = P.alloc(512); dGNW = Dep()
    dma("sp", GNW, gnw_r.partition_broadcast(128), writes=[dGNW])
    HT32 = P.alloc(512); HTB = P.alloc(512, BF16); dHT = Dep()
    op("dve", lambda e: e.memset(HT32, 0.0), [], [dHT])
    op("dve", lambda e: e.memset(HTB, 0.0), [], [dHT])
    DT = P.alloc(128); DT3 = v3(DT, 16); ADT = P.alloc(128); ADT3 = v3(ADT, 16); dDT = Dep()
    WZ = P.alloc(8 * 512, BF16); dWZ = Dep()
    WZ3 = load_w(WZ, dWZ, RPROJ, 512)
    WDT = P.alloc(8 * 8, BF16); dWDT = Dep()
    WDT3 = load_w(WDT, dWDT, PROJ - 8, 8)
    WX = [P.alloc(8 * 128, BF16) for _ in range(3)]; dWX = [Dep(), Dep(), Dep()]
    bk, dbk = P.bank()
    for j, (t0, rows) in enumerate(TILES):
        for dc in range(8):
            mm(bk[0:rows, j * 8:(j + 1) * 8], XT3[:, dc, t0:t0 + rows], WDT3[:, dc, :], dc == 0, dc == 7, [dWDT, dXT[j]], [dbk])
    cp("dve", PRS[0:16, PROJ - 8:PROJ], bk[0:16, 128:136], [dbk], [dPRS])
    op("dve", lambda e: e.tensor_tensor(DT3, v3(bk[:, 0:128], 16), BC8[:, 0:8].unsqueeze(1).to_broadcast([128, 16, 8]), ALU.add), [dbk, dBC8], [dDT])
    op("act", lambda e: e.activation(DT, DT, AF.Exp), [dDT], [dDT])
    op("act", lambda e: e.activation(DT, DT, AF.Ln, bias=EPS[:, 2:3], scale=1.0), [dDT, dFV], [dDT])
    op("dve", lambda e: e.tensor_tensor(ADT3, DT3, BC8[:, 8:16].unsqueeze(1).to_broadcast([128, 16, 8]), ALU.mult), [dDT, dBC8], [dDT])
    stop('M1')
    proj_samples(WZ3, dWZ, RPROJ, 512)
    stop('M1b')

    XBC = P.alloc(8 * 515); XBC3 = v3(XBC, 8); dXBCs = [Dep() for _ in range(8)]
    op("dve", lambda e: e.memset(XBC, 0.0), [], dXBCs)
    ACCs = [P.alloc(512) for _ in range(3)]; dACCs = [Dep() for _ in range(3)]
    MSETS = []
    for _sb in range(2):
        MSETS.append((v3(P.alloc(8 * 512, BF16), 8), Dep(), v3(P.alloc(4 * 512, BF16), 4), Dep(), v3(P.alloc(4 * 256, BF16), 4), Dep(), v3(P.alloc(4 * 512, BF16), 4), Dep()))
    NCV = P.alloc(1024); dNCV = Dep()
    CS = P.alloc(64); dCS = Dep()
    ADTB = P.alloc(8 * 128); ADTB3 = v3(ADTB, 8); dADTB = Dep()
    XBF = P.alloc(512, BF16); XHAT = P.alloc(512, BF16); dXB2 = Dep()
    CBM = P.alloc(256); CBM3 = v3(CBM, 2); dCBM = Dep()
    LSB = P.alloc(1024); LSB3 = v3(LSB, 8); dLSB = Dep()
    GB = P.alloc(1024, BF16); GB3 = v3(GB, 8); dGB = Dep()
    T1 = P.alloc(512); T2 = P.alloc(512); dT1 = Dep(); dT2 = Dep()
    SS = P.alloc(8); dSS = Dep()
    OMB = P.alloc(512, BF16); dOMB = Dep()

    for sl_i, (s0, sn) in enumerate(SLABS[:4]):
        XSA3, dXSA, XSTOK3, dXST, BTOK3, dBT, ZS3, dZS = MSETS[sl_i % 2]
        for jj in range(4):
            j = sl_i * 4 + jj
            t0 = j * 128
            bk, dbk = P.bank()
            for dc in range(8):
                mm(bk[:, :], XT3[:, dc, t0:t0 + 128], WZ3[:, dc, :], dc == 0, dc == 7, [dWZ, dXT[j]], [dbk])
            op("act", lambda e: e.activation(ZS3[:, jj, :], bk[:, :], AF.Silu), [dbk], [dZS])
        stop('M2')
        for fc in range(8):
            wi = (sl_i * 8 + fc) % 3
            dXBC = dXBCs[fc]; ACC = ACCs[wi]; dACC = dACCs[wi]
            w3 = load_w(WX[wi], dWX[wi], RPROJ + 512 + fc * 128, 128)
            if sl_i == 0:
                proj_samples(w3, dWX[wi], RPROJ + 512 + fc * 128, 128)
            else:
                op("dve", lambda e: e.tensor_copy(XBC3[:, fc, 0:3], XBC3[:, fc, 512:515]), [dXBC], [dXBC])
            proj_slab(w3, dWX[wi], s0, 512, XBC3[:, fc, 3:515], dXBC)
            cw = lambda i: FV[:, 38 + fc * 4 + i:39 + fc * 4 + i]
            op("act", lambda e: e.activation(ACC, XBC3[:, fc, 0:512], AF.Identity, bias=FV[:, 70 + fc:71 + fc], scale=cw(0)), [dXBC, dFV], [dACC])
            for i in (1, 2, 3):
                op("dve", lambda e: e.scalar_tensor_tensor(ACC, XBC3[:, fc, i:i + 512], cw(i), ACC, ALU.mult, ALU.add), [dXBC, dFV, dACC], [dACC])
            op("act", lambda e: e.activation(XSA3[:, fc, :], ACC, AF.Silu), [dACC], [dXSA])
            if sl_i == 3:
                bk, dbk = P.bank()
                mm(bk[0:3, 0:128], XBC3[:, fc, 512:515], identf, True, True, [dXBC, dCST], [dbk])
                cp("dve", NCV[0:3, fc * 128:(fc + 1) * 128], bk[0:3, 0:128], [dbk], [dNCV])
        stop('M3')
        for jj in range(4):
            bk, dbk = P.bank()
            psb = bk[:].bitcast(BF16)
            for fc in range(6):
                op("pe", lambda e: e.transpose(psb[:, fc * 128:(fc + 1) * 128], XSA3[:, fc, jj * 128:(jj + 1) * 128], identb), [dXSA, dCSTB], [dbk])
                stop('T%d' % fc)
            cp("act", XSTOK3[:, jj, :], psb[:, 0:512], [dbk], [dXST])
            stop('T6')
            cp("act", BTOK3[:, jj, :], psb[:, 512:768], [dbk], [dBT])
            stop('T7')
        stop('M4')
        for jj in range(4):
            j = sl_i * 4 + jj
            tsl = slice(jj * 128, (jj + 1) * 128)
            gsl = slice(j * 128, (j + 1) * 128)
            bk, dbk = P.bank()
            mm(bk[:, 0:8], mUI, ADT3[:, j, :], True, True, [dCST, dDT], [dbk])
            mm(bk[:, 8:16], ones, ADT3[:, j, :], True, True, [dCST, dDT], [dbk])
            op("act", lambda e: e.activation(CS[:, 0:16], bk[:, 0:16], AF.Exp), [dbk], [dCS])
            op("dve", lambda e: e.tensor_copy(CS[:, 32:48], bk[:, 0:16]), [dbk], [dCS])
            op("dve", lambda e: e.tensor_tensor(CS[:, 24:32], CS[:, 40:48], CS[:, 32:40], ALU.subtract), [dCS], [dCS])
            op("act", lambda e: e.activation(CS[:, 16:24], CS[:, 24:32], AF.Exp), [dCS], [dCS])
            op("act", lambda e: e.copy(ADTB3, ADT3[:, j, :].unsqueeze(2).to_broadcast([128, 8, 128])), [dDT], [dADTB])
            op("dve", lambda e: e.tensor_tensor(v3(XBF, 8), v3(XSTOK3[:, jj, :], 8), bc8(DT3[:, j, :]), ALU.mult), [dXST, dDT], [dXB2])
            op("dve", lambda e: e.tensor_tensor(v3(XHAT, 8), v3(XBF, 8), bc8(CS[:, 16:24]), ALU.mult), [dXB2, dCS], [dXB2])
            stop('M5')
            bkc, dbkc = P.bank()
            for g in range(2):
                mm(bkc[:, g * 128:(g + 1) * 128], XSA3[:, 4 + g, tsl], XSA3[:, 6 + g, tsl], True, True, [dXSA], [dbkc])
            op("dve", lambda e: e.tensor_tensor(CBM3, v3(bkc[:, 0:256], 2), mUI.unsqueeze(1).to_broadcast([128, 2, 128]), ALU.mult), [dbkc, dCST], [dCBM])
            for g in range(2):
                bkd, dbkd = P.bank()
                for hh in range(4):
                    h = g * 4 + hh
                    o = bkd[:, hh * 128:(hh + 1) * 128]
                    mm(o, ADTB3[:, h, :], mUI, True, False, [dADTB, dCST], [dbkd])
                    mm(o, NUI, ADTB3[:, h, :], False, False, [dADTB, dCST], [dbkd])
                    mm(o, identf, negm, False, True, [dCST], [dbkd])
                op("act", lambda e: e.activation(LSB[:, g * 512:(g + 1) * 512], bkd[:, :], AF.Exp), [dbkd], [dLSB])
                op("dve", lambda e: e.tensor_tensor(GB3[:, g * 4:(g + 1) * 4, :], LSB3[:, g * 4:(g + 1) * 4, :], CBM3[:, g, :].unsqueeze(1).to_broadcast([128, 4, 128]), ALU.mult), [dLSB, dCBM], [dGB])
            stop('M6')
            bky, dbky = P.bank()
            for h in range(8):
                mm(bky[:, h * 64:(h + 1) * 64], GB3[:, h, :], XBF[:, h * 64:(h + 1) * 64], True, True, [dGB, dXB2], [dbky])
            bko, dbko = P.bank()
            for g in range(2):
                mm(bko[:, g * 256:(g + 1) * 256], XSA3[:, 6 + g, tsl], HTB[:, g * 256:(g + 1) * 256], True, True, [dXSA, dHT], [dbko])
            op("dve", lambda e: e.tensor_tensor(v3(T1, 8), v3(bko[:, :], 8), bc8(CS[:, 0:8]), ALU.mult), [dbko, dCS], [dT1])
            op("dve", lambda e: e.tensor_tensor(T1, T1, bky[:, :], ALU.add), [dbky, dT1], [dT1])
            op("pool", lambda e: e.tensor_tensor(v3(T2, 8), v3(XSTOK3[:, jj, :], 8), bc8(BC8[:, 16:24]), ALU.mult), [dXST, dBC8], [dT2])
            op("dve", lambda e: e.tensor_tensor(T1, T1, T2, ALU.add), [dT1, dT2], [dT1])
            op("dve", lambda e: e.tensor_tensor(T1, T1, ZS3[:, jj, :], ALU.mult), [dT1, dZS], [dT1])
            for g in range(2):
                op("act", lambda e: e.activation(T2[:, g * 256:(g + 1) * 256], T1[:, g * 256:(g + 1) * 256], AF.Square, accum_out=SS[:, g:g + 1]), [dT1, dT2], [dT2, dSS])
            op("act", lambda e: e.activation(SS[:, 2:4], SS[:, 0:2], AF.Ln, bias=EPS[:, 1:2], scale=1.0 / 256.0), [dSS, dFV], [dSS])
            op("act", lambda e: e.activation(SS[:, 4:6], SS[:, 2:4], AF.Exp, scale=-0.5), [dSS], [dSS])
            op("dve", lambda e: e.tensor_tensor(v3(T1, 2), v3(T1, 2), SS[:, 4:6].unsqueeze(2).to_broadcast([128, 2, 256]), ALU.mult), [dT1, dSS], [dT1])
            op("dve", lambda e: e.tensor_tensor(OMB, T1, GNW, ALU.mult), [dT1, dGNW], [dOMB])
            stop('M7')
            bkt, dbkt = P.bank()
            psb = bkt[:].bitcast(BF16)
            for fc in range(4):
                op("pe", lambda e: e.transpose(psb[:, fc * 128:(fc + 1) * 128], OMB[:, fc * 128:(fc + 1) * 128], identb), [dOMB, dCSTB], [dbkt])
            cp("act", MIXT3[:, 4:8, gsl], v3(psb[:, 0:512], 4), [dbkt], [dMIX])
            bkh, dbkh = P.bank()
            for g in range(2):
                mm(bkh[:, g * 256:(g + 1) * 256], BTOK3[:, jj, g * 128:(g + 1) * 128], XHAT[:, g * 256:(g + 1) * 256], True, True, [dBT, dXB2], [dbkh])
            op("dve", lambda e: e.tensor_tensor(v3(HT32, 8), v3(HT32, 8), bc8(CS[:, 8:16]), ALU.mult), [dHT, dCS], [dHT])
            op("dve", lambda e: e.tensor_tensor(HT32, HT32, bkh[:, :], ALU.add), [dHT, dbkh], [dHT])
            cp("act", HTB, HT32, [dHT], [dHT])
    P.held = set()
    dma("sp", ncv_p, NCV[0:3, :], reads=[dNCV])
    for blk in range(4):
        bk, dbk = P.bank()
        mm(bk[:, 0:128], HT32[:, blk * 128:(blk + 1) * 128], identf, True, True, [dHT, dCST], [dbk])
        cp("act", T1[:, (blk % 4) * 128:(blk % 4 + 1) * 128], bk[:, 0:128], [dbk], [dT1])
    dma("sp", nssm_p.rearrange("(b p) n -> p b n", p=128), v3(T1, 4), reads=[dT1])
    P.barrier()
    P.top = mM
    stop('M')
    SHT = P.alloc(32); dSHT = Dep()
    STSH = P.alloc(256); dSTSH = Dep()
    dma("sp", STSH[0:16, :], st_shift[:, 1536:1792], writes=[dSTSH])
    for i in range(2):
        bk, dbk = P.bank()
        mm(bk[:, 0:16], STSH[0:16, i * 128:(i + 1) * 128], identf[0:16, 0:16], True, True, [dSTSH, dCST], [dbk])
        cp("dve", SHT[:, i * 16:(i + 1) * 16], bk[:, 0:16], [dbk], [dSHT])
    WR = [P.alloc(8 * 128, BF16) for _ in range(3)]; dWR = [Dep(), Dep(), Dep()]
    PT = P.alloc(3 * 513); PT3 = v3(PT, 3); dPT = Dep()
    UR = P.alloc(512); UK = P.alloc(512); SW = P.alloc(512); AA = P.alloc(512); CL = P.alloc(512); EE = P.alloc(512); KKN = P.alloc(512)
    EX1 = XB[0].bitcast(F32); EX2 = XB[1].bitcast(F32); dEX1 = dXB[0]; dEX2 = dXB[1]
    dUR = Dep(); dUK = Dep(); dSW = Dep(); dAA = Dep(); dCL = Dep(); dEE = Dep(); dKKN = Dep()
    for fc in (12, 13):
        w3 = load_w(WR[0], dWR[0], fc * 128, 128)
        proj_samples(w3, dWR[0], fc * 128, 128)
        op("dve", lambda e: e.memset(PT3[:, 0, 0:1], 0.0), [], [dPT])
        dst = LUW if fc == 12 else LUG
        for si, (s0, sn) in enumerate(SLABS):
            if si > 0 and si < 4:
                op("dve", lambda e: e.tensor_copy(PT3[:, 0, 0:1], PT3[:, 0, 512:513]), [dPT], [dPT])
            proj_slab(w3, dWR[0], s0, sn, PT3[:, 0, 1:1 + sn], dPT)
            if si == 3:
                col_to_dram(PT3[:, 0, 512:513], nsh_p[0:1, fc * 128:(fc + 1) * 128], [dPT])
            prev = PT3[:, 0, 0:sn] if si < 4 else SHT[:, (fc - 12) * 16:(fc - 11) * 16]
            op("dve", lambda e: e.tensor_tensor(CL[:, 0:sn], prev, PT3[:, 0, 1:1 + sn], ALU.subtract), [dPT, dSHT], [dCL])
            op("dve", lambda e: e.scalar_tensor_tensor(UR[:, 0:sn], CL[:, 0:sn], FV[:, fc:fc + 1], PT3[:, 0, 1:1 + sn], ALU.mult, ALU.add), [dCL, dPT, dFV], [dUR])
            if fc == 12:
                op("act", lambda e: e.activation(LUW[0:64, s0:s0 + sn], UR[0:64, 0:sn], AF.Tanh), [dUR], [dLUW[si]])
                cp("act", LUW[64:128, s0:s0 + sn], UR[64:128, 0:sn], [dUR], [dLUW[si]])
            else:
                op("act", lambda e: e.activation(UR[:, 0:sn], UR[:, 0:sn], AF.Exp, scale=-1.0), [dUR], [dUR])
                op("dve", lambda e: e.tensor_scalar(UR[:, 0:sn], UR[:, 0:sn], 1.0, None, ALU.add), [dUR], [dUR])
                op("dve", lambda e: e.reciprocal(UR[:, 0:sn], UR[:, 0:sn]), [dUR], [dUR])
                cp("act", LUG[:, s0:s0 + sn], UR[:, 0:sn], [dUR], [dLUG[si]])

    stop('R1')
    P.held = {0, 1, 2, 5, 6, 7}
    POOL_Y = [0]
    POOL_B = [1, 2]
    POOL_I = [5, 6, 7]
    SETS = []
    for _sb in range(2):
        st_ = {}
        st_["AZ"] = P.alloc(8 * 128, BF16); st_["AR"] = P.alloc(8 * 128, BF16); st_["TTA"] = P.alloc(8 * 128, BF16)
        st_["AMG"] = P.alloc(8 * 256, BF16); st_["AKZ"] = P.alloc(8 * 256, BF16); st_["ZTG"] = P.alloc(8 * 256, BF16); st_["UVZ"] = P.alloc(8 * 256, BF16)
        st_["VBF"] = P.alloc(512, BF16); st_["RKP"] = P.alloc(512, BF16); st_["PC"] = P.alloc(8)
        for nm in ("dAZ", "dAR", "dTTA", "dAMG", "dAKZ", "dZTG", "dUVZ", "dVB", "dRKP", "dPC"):
            st_[nm] = Dep(nm)
        for nm, dn in (("AZ", "dAZ"), ("AKZ", "dAKZ"), ("ZTG", "dZTG"), ("UVZ", "dUVZ")):
            op("pool", lambda e: e.memset(st_[nm], 0.0), [], [st_[dn]])
        SETS.append(st_)
    BZ = P.alloc(8 * 128, BF16); BZ3 = v3(BZ, 8)
    BK = P.alloc(8 * 128, BF16); BK3 = v3(BK, 8); BK4 = v4(BK, 8, 2)
    BKH = P.alloc(8 * 128, BF16); BKH3 = v3(BKH, 8)
    dBK = Dep()
    op("pool", lambda e: e.memset(BZ, 0.0), [], [dBK])
    SQB = P.alloc(512, BF16); dSQB = Dep()
    Y32 = P.alloc(512); YSQ = P.alloc(512); YC = P.alloc(512); M2 = P.alloc(512)
    dY32 = Dep(); dYSQ = Dep(); dYC = Dep(); dM2 = Dep()
    RST = P.alloc(512); dRST = Dep()
    op("pool", lambda e: e.memset(RST, 1.0), [], [dRST])
    op("pool", lambda e: e.memset(v3(RST, 8)[:, :, 0:1], 0.0), [], [dRST])
    S32 = P.alloc(64); SBF = P.alloc(64, BF16); SZ = [P.alloc(128, BF16) for _ in range(2)]
    dS = Dep(); dSB = Dep(); dSZ = [Dep(), Dep()]
    WSB = P.alloc(64, BF16); dWSB = Dep()
    MBs = [[P.alloc(512, BF16) for _ in range(2)] for _ in range(2)]; NBs = [[P.alloc(512, BF16) for _ in range(2)] for _ in range(2)]; PBs = [[P.alloc(512, BF16) for _ in range(2)] for _ in range(2)]
    dMBs = [[Dep(), Dep()] for _ in range(2)]; dNBs = [[Dep(), Dep()] for _ in range(2)]; dPBs = [[Dep(), Dep()] for _ in range(2)]
    SOUT = P.alloc(128); dSO = Dep()

    for hp in range(4):
        w3s = []
        for kind in range(3):
            fc = kind * 4 + hp
            w3 = load_w(WR[kind], dWR[kind], fc * 128, 128)
            proj_samples(w3, dWR[kind], fc * 128, 128)
            w3s.append(w3)
        op("dve", lambda e: e.memset(PT3[:, :, 0:1], 0.0), [], [dPT])
        op("dve", lambda e: e.memset(S32, 0.0), [], [dS])
        op("dve", lambda e: e.memset(SBF, 0.0), [], [dSB])
        op("dve", lambda e: e.memset(SZ[0], 0.0), [], [dSZ[0]])
        op("dve", lambda e: e.memset(SZ[1], 0.0), [], [dSZ[1]])
        szi = 0
        for g8, (s0, sn) in enumerate(SLABS[:4]):
            tsl = slice(s0, s0 + 512)
            st_ = SETS[(hp * 4 + g8) % 2]
            AZ = st_["AZ"]; AZ3 = v3(AZ, 8); AR = st_["AR"]; AR3 = v3(AR, 8); AR4 = v4(AR, 8, 2); TTA = st_["TTA"]; TTA3 = v3(TTA, 8)
            AMG = st_["AMG"]; AMG4 = v4(AMG, 8, 2); AKZ4 = v4(st_["AKZ"], 8, 2); ZTG4 = v4(st_["ZTG"], 8, 2); UVZ4 = v4(st_["UVZ"], 8, 2)
            VBF = st_["VBF"]; RKP = st_["RKP"]; PC = st_["PC"]
            dAZ = st_["dAZ"]; dAR = st_["dAR"]; dTTA = st_["dTTA"]; dAMG = st_["dAMG"]; dAKZ = st_["dAKZ"]; dZTG = st_["dZTG"]; dUVZ = st_["dUVZ"]
            dVB = st_["dVB"]; dRKP = st_["dRKP"]; dPC = st_["dPC"]
            if g8 > 0:
                op("dve", lambda e: e.tensor_copy(PT3[:, :, 0:1], PT3[:, :, 512:513]), [dPT], [dPT])
            for kind, dst, dd, xs_, dxs in ((0, UR, dUR, UR, dUR), (1, UK, dUK, UK, dUK), (2, VBF, dVB, CL, dCL)):
                fc = kind * 4 + hp
                bk, dbk = P.bank()
                for dc in range(8):
                    mm(bk[:, 0:512], w3s[kind][:, dc, :], XT3[:, dc, s0:s0 + 512], dc == 0, dc == 7, [dWR[kind]] + xt_deps(s0, 512), [dbk])
                cp("act", PT3[:, kind, 1:513], bk[:, 0:512], [dbk], [dPT])
                op("act", lambda e: e.activation(xs_, bk[:, 0:512], AF.Identity, scale=OMM[:, fc:fc + 1]), [dbk, dFV], [dxs])
                if g8 == 3:
                    col_to_dram(PT3[:, kind, 512:513], nsh_p[0:1, fc * 128:(fc + 1) * 128], [dPT])
                op("dve", lambda e: e.scalar_tensor_tensor(dst, PT3[:, kind, 0:512], FV[:, fc:fc + 1], xs_, ALU.mult, ALU.add), [dPT, dxs, dFV], [dd])
            bk, dbk = P.bank()
            mm(bk[:, :], WUA[0:64, hp * 128:(hp + 1) * 128], LUW[0:64, tsl], True, True, [dLW, dLUW[g8]], [dbk])
            op("act", lambda e: e.activation(SW, bk[:, :], AF.Exp, bias=HB0[:, hp:hp + 1], scale=-1.0), [dbk, dFV], [dSW])
            op("act", lambda e: e.activation(SW, SW, AF.Identity, bias=EPS[:, 2:3], scale=1.0), [dSW, dFV], [dSW])
            op("dve", lambda e: e.reciprocal(SW, SW), [dSW], [dSW])
            bk2, dbk2 = P.bank()
            mm(bk2[:, :], WUA[64:128, hp * 128:(hp + 1) * 128], LUW[64:128, tsl], True, True, [dLW, dLUW[g8]], [dbk2])
            op("act", lambda e: e.activation(AA, bk2[:, :], AF.Exp, bias=HB0[:, 4 + hp:5 + hp], scale=-1.0), [dbk2, dFV], [dAA])
            op("act", lambda e: e.activation(AA, AA, AF.Identity, bias=EPS[:, 2:3], scale=1.0), [dAA, dFV], [dAA])
            op("dve", lambda e: e.reciprocal(AA, AA), [dAA], [dAA])
            op("act", lambda e: e.activation(KKN, UK, AF.Identity, scale=FV[:, 22 + hp:23 + hp]), [dUK, dFV], [dKKN])
            op("act", lambda e: e.activation(SQB, KKN, AF.Square), [dKKN], [dSQB])
            bk, dbk = P.bank()
            mm(bk[:, :], bonesb, SQB, True, True, [dCSTB, dSQB], [dbk])
            op("dve", lambda e: e.tensor_scalar(EE, bk[:, :], 1e-24, None, ALU.max), [dbk], [dEE])
            op("act", lambda e: e.activation(EE, EE, AF.Ln), [dEE], [dEE])
            op("act", lambda e: e.activation(EE, EE, AF.Exp, scale=-0.5), [dEE], [dEE])
            op("dve", lambda e: e.tensor_tensor(KKN, KKN, EE, ALU.mult), [dKKN, dEE], [dKKN])
            op("act", lambda e: e.activation(EE, AA, AF.Identity, bias=OMK[:, hp:hp + 1], scale=FV[:, 26 + hp:27 + hp]), [dAA, dFV], [dEE])
            op("dve", lambda e: e.tensor_tensor(UK, UK, EE, ALU.mult), [dUK, dEE], [dUK])
            op("dve", lambda e: e.tensor_tensor(RKP, UR, UK, ALU.mult), [dUR, dUK], [dRKP])
            op("dve", lambda e: e.tensor_tensor_scan(CL, RST, SW, 0.0, ALU.mult, ALU.add), [dRST, dSW], [dCL])
            op("dve", lambda e: e.tensor_tensor(SW, CL, SW, ALU.subtract), [dCL, dSW], [dSW])
            op("act", lambda e: e.activation(EE, SW, AF.Exp, scale=-C0), [dSW], [dEE])
            op("dve", lambda e: e.scalar_tensor_tensor(AR4[:, :, 0, :], v3(KKN, 8), -1.0, v3(EE, 8), ALU.mult, ALU.mult), [dKKN, dEE], [dAR])
            op("act", lambda e: e.activation(EX1, CL, AF.Exp, scale=-C0), [dCL], [dEX1])
            op("dve", lambda e: e.tensor_tensor(AR4[:, :, 1, :], v3(UR, 8), v3(EX1, 8), ALU.mult), [dUR, dEX1], [dAR])
            op("dve", lambda e: e.tensor_copy(PC, v3(EX1, 8)[:, :, 63]), [dEX1], [dPC])
            op("dve", lambda e: e.tensor_tensor(KKN, KKN, AA, ALU.mult), [dKKN, dAA], [dKKN])
            op("act", lambda e: e.activation(EX2, CL, AF.Exp, scale=C0), [dCL], [dEX2])
            op("dve", lambda e: e.tensor_tensor(BK4[:, :, 0, :], v3(KKN, 8), v3(EX2, 8), ALU.mult), [dKKN, dEX2], [dBK])
            op("dve", lambda e: e.tensor_tensor(BK4[:, :, 1, :], v3(UK, 8), v3(EX2, 8), ALU.mult), [dUK, dEX2], [dBK])
            op("dve", lambda e: e.tensor_tensor(BKH3, BK3, PC.unsqueeze(2).to_broadcast([128, 8, 128]), ALU.mult), [dBK, dPC], [dBK])
            for hh in range(2):
                ps_ = slice(hh * 64, (hh + 1) * 64)
                cp("act", AZ3[ps_, :, hh * 64:(hh + 1) * 64], AR4[ps_, :, 0, :], [dAR], [dAZ])
                cp("act", BZ3[ps_, :, hh * 64:(hh + 1) * 64], BK4[ps_, :, 0, :], [dBK], [dBK])
            stop('R2')
            for gq in range(2):
                MB = MBs[gq]; NB = NBs[gq]; PB = PBs[gq]; dMB = dMBs[gq]; dNB = dNBs[gq]; dPB = dPBs[gq]
                bm, dbm = P.bank(pool=POOL_I); bn, dbn = P.bank(pool=POOL_I)
                for i in range(4):
                    c = gq * 4 + i
                    mm(bm[:, i * 128:(i + 1) * 128], BZ3[:, c, :], AZ3[:, c, :], True, True, [dBK, dAZ], [dbm])
                    mm(bn[:, i * 128:(i + 1) * 128], AZ3[:, c, :], BZ3[:, c, :], True, True, [dBK, dAZ], [dbn])
                cur = 0
                op("dve", lambda e: e.tensor_tensor(v3(MB[0], 4), v3(bm[:, :], 4), mSU.unsqueeze(1).to_broadcast([128, 4, 128]), ALU.mult), [dbm, dCST], [dMB[0]])
                op("dve", lambda e: e.tensor_tensor(v3(NB[0], 4), v3(bn[:, :], 4), mSL.unsqueeze(1).to_broadcast([128, 4, 128]), ALU.mult), [dbn, dCST], [dNB[0]])
                op("dve", lambda e: e.tensor_tensor(v3(PB[0], 4), v3(MB[0], 4), identb.unsqueeze(1).to_broadcast([128, 4, 128]), ALU.add), [dMB[0], dCSTB], [dPB[0]])
                for lvl in range(1, 6):
                    nx = 1 - cur
                    if lvl <= 4:
                        bm, dbm = P.bank(pool=POOL_I)
                        for i in range(4):
                            sl = slice(i * 128, (i + 1) * 128)
                            mm(bm[:, sl], NB[cur][:, sl], MB[cur][:, sl], True, True, [dNB[cur], dMB[cur]], [dbm])
                    bn, dbn = P.bank(pool=POOL_I)
                    for i in range(4):
                        sl = slice(i * 128, (i + 1) * 128)
                        mm(bn[:, sl], MB[cur][:, sl], NB[cur][:, sl], True, True, [dNB[cur], dMB[cur]], [dbn])
                    if lvl <= 4:
                        cp("act", MB[nx], bm[:, :], [dbm], [dMB[nx]])
                    cp("dve", NB[nx], bn[:, :], [dbn], [dNB[nx]])
                    bp, dbp = P.bank(pool=POOL_I)
                    for i in range(4):
                        sl = slice(i * 128, (i + 1) * 128)
                        mm(bp[:, sl], NB[nx][:, sl], PB[cur][:, sl], True, False, [dNB[nx], dPB[cur]], [dbp])
                        mm(bp[:, sl], identb, PB[cur][:, sl], False, True, [dCSTB, dPB[cur]], [dbp])
                    if lvl < 5:
                        cp("act", PB[nx], bp[:, :], [dbp], [dPB[nx]])
                    else:
                        cp("act", TTA[:, gq * 512:(gq + 1) * 512], bp[:, :], [dbp], [dTTA])
                    cur = nx
            stop('R3')
            for half in range(2):
                bks = [P.bank(pool=POOL_I), P.bank(pool=POOL_I)]
                for ci4 in range(4):
                    ci = half * 4 + ci4
                    for hh in range(2):
                        ps_ = slice(hh * 64, (hh + 1) * 64)
                        bk, dbk = bks[hh]
                        mm(bk[:, ci4 * 128:(ci4 + 1) * 128], BK3[ps_, ci, :], AR3[ps_, ci, :], True, True, [dBK, dAR], [dbk])
                for hh in range(2):
                    bk, dbk = bks[hh]
                    op("dve", lambda e: e.tensor_tensor(AMG4[:, half * 4:half * 4 + 4, hh, :], v3(bk[:, :], 4), mask4.unsqueeze(1).to_broadcast([128, 4, 128]), ALU.mult), [dbk, dCST], [dAMG])
                    op("dve", lambda e: e.tensor_tensor(AKZ4[64:128, half * 4:half * 4 + 4, hh, hh * 64:(hh + 1) * 64], v3(bk[64:128, :], 4)[:, :, 0:64], mask4[64:128, 0:64].unsqueeze(1).to_broadcast([64, 4, 64]), ALU.mult), [dbk, dCST], [dAKZ])
            stop('R3b')
            bk, dbk = P.bank(pool=POOL_I)
            psb = bk[:].bitcast(BF16)
            for ci in range(8):
                op("pe", lambda e: e.transpose(psb[:, ci * 128:(ci + 1) * 128], BKH3[:, ci, :], identb), [dBK, dCSTB], [dbk])
            for hh in range(2):
                cp("act" if hh else "dve", ZTG4[:, :, hh, hh * 64:(hh + 1) * 64], v3(psb, 8)[:, :, hh * 64:(hh + 1) * 64], [dbk], [dZTG])
            stop('R3c')
            bk, dbk = P.bank(pool=POOL_I)
            psb = bk[:].bitcast(BF16)
            for ti in range(4):
                op("pe", lambda e: e.transpose(psb[:, ti * 128:(ti + 1) * 128], VBF[:, ti * 128:(ti + 1) * 128], identb), [dVB, dCSTB], [dbk])
            UVZ5 = st_["UVZ"].rearrange("p (t q h c) -> p t q h c", t=4, q=2, h=2)
            for par in range(2):
                for hh in range(2):
                    cp("act" if hh else "dve", UVZ5[64:128, :, par, hh, hh * 64:(hh + 1) * 64], v3(psb[par * 64:(par + 1) * 64, 0:512], 4)[:, :, hh * 64:(hh + 1) * 64], [dbk], [dUVZ])
            stop('R4')
            bky, dbky = P.bank(pool=POOL_Y)
            for ci in range(8):
                bw, dbw = P.bank(pool=POOL_B)
                mm(bw[:, 0:64], AZ3[:, ci, :], SBF, True, False, [dAZ, dSB], [dbw])
                for hh in range(2):
                    mm(bw[:, 0:64], AKZ4[64:128, ci, hh, :], UVZ4[64:128, ci, hh, hh * 64:(hh + 1) * 64], False, hh == 1, [dAKZ, dUVZ], [dbw])
                cp("act", WSB, bw[:, 0:64], [dbw], [dWSB])
                bu, dbu = P.bank(pool=POOL_B)
                mm(bu[:, 0:64], TTA3[:, ci, :], WSB, True, True, [dTTA, dWSB], [dbu])
                cp("dve", UVZ4[0:64, ci, 0, 0:64], bu[0:64, 0:64], [dbu], [dUVZ])
                cp("act", UVZ4[0:64, ci, 1, 64:128], bu[64:128, 0:64], [dbu], [dUVZ])
                yo = bky[:, ci * 64:(ci + 1) * 64]
                mm(yo, SZ[szi], AR4[:, ci, 1, :], True, False, [dSZ[szi], dAR], [dbky])
                for hh in range(2):
                    mm(yo, UVZ4[:, ci, hh, :], AMG4[:, ci, hh, 64:128], False, hh == 1, [dUVZ, dAMG], [dbky])
                bs, dbs = P.bank(pool=POOL_B)
                for hh in range(2):
                    mm(bs[:, 0:64], ZTG4[:, ci, hh, :], UVZ4[:, ci, hh, hh * 64:(hh + 1) * 64], hh == 0, hh == 1, [dZTG, dUVZ], [dbs])
                op("dve", lambda e: e.scalar_tensor_tensor(SBF, S32, PC[:, ci:ci + 1], bs[:, 0:64], ALU.mult, ALU.add), [dS, dPC, dbs], [dSB])
                op("dve", lambda e: e.scalar_tensor_tensor(S32, S32, PC[:, ci:ci + 1], bs[:, 0:64], ALU.mult, ALU.add), [dS, dPC, dbs], [dS])
                szi = 1 - szi
                for hh in range(2):
                    ps_ = slice(hh * 64, (hh + 1) * 64)
                    cp("act", SZ[szi][ps_, hh * 64:(hh + 1) * 64], S32[ps_, :], [dS], [dSZ[szi]])
            stop('R5')
            cp("act", Y32, bky[:, :], [dbky], [dY32])
            op("dve", lambda e: e.tensor_tensor(YSQ, Y32, Y32, ALU.mult), [dY32], [dYSQ])
            stop('Ea')
            b1, db1 = P.bank(pool=POOL_B); b2, db2 = P.bank(pool=POOL_B)
            mm(b1[:, :], bones, Y32, True, True, [dCST, dY32], [db1])
            mm(b2[:, :], bones, YSQ, True, True, [dCST, dYSQ], [db2])
            stop('Ea1')
            op("dve", lambda e: e.scalar_tensor_tensor(YC, b1[:, :], -1.0 / 64.0, Y32, ALU.mult, ALU.add), [db1, dY32], [dYC])
            stop('Ea2')
            op("act", lambda e: e.activation(M2, b1[:, :], AF.Square, scale=1.0 / 64.0), [db1], [dM2])
            stop('Ea3')
            op("dve", lambda e: e.scalar_tensor_tensor(M2, b2[:, :], 1.0 / 64.0, M2, ALU.mult, ALU.subtract), [db2, dM2], [dM2])
            stop('Ea4')
            op("act", lambda e: e.activation(M2, M2, AF.Ln, bias=EPS[:, 0:1], scale=1.0), [dM2, dFV], [dM2])
            stop('Ea5')
            op("act", lambda e: e.activation(M2, M2, AF.Exp, scale=-0.5), [dM2], [dM2])
            stop('Ea6')
            op("dve", lambda e: e.tensor_tensor(YC, YC, M2, ALU.mult), [dYC, dM2], [dYC])
            stop('Ea7')
            op("act", lambda e: e.activation(YC, YC, AF.Identity, bias=FV[:, 34 + hp:35 + hp], scale=FV[:, 30 + hp:31 + hp]), [dYC, dFV], [dYC])
            stop('Ea8')
            stop('Eb')
            b3, db3 = P.bank(pool=POOL_B)
            mm(b3[:, :], RKB[:, hp * 128:(hp + 1) * 128], RKP, True, True, [dFV, dRKP], [db3])
            op("dve", lambda e: e.tensor_tensor(M2, b3[:, :], VBF, ALU.mult), [db3, dVB, dM2], [dM2])
            op("dve", lambda e: e.tensor_tensor(YC, YC, M2, ALU.add), [dYC, dM2], [dYC])
            b4, db4 = P.bank(pool=POOL_B)
            mm(b4[:, :], GUP[:, hp * 128:(hp + 1) * 128], LUG[:, tsl], True, True, [dLW, dLUG[g8]], [db4])
            op("dve", lambda e: e.tensor_tensor(MIXT3[:, hp, tsl], YC, b4[:, :], ALU.mult), [db4, dYC], [dMIX])
            stop('E%d' % g8)
        stop('R6')
        bk, dbk = P.bank(pool=POOL_B)
        mm(bk[0:64, 0:128], S32, identf, True, True, [dS, dCST], [dbk])
        cp("dve", SOUT[0:64, :], bk[0:64, 0:128], [dbk], [dSO])
        dma("sp", nwkv_p.rearrange("(h v) k -> v h k", v=64)[:, 2 * hp:2 * hp + 2, :], v3(SOUT[0:64, :], 2), reads=[dSO])
    dma("sp", nsh_s, PRS[0:16, 0:RPROJ], reads=[dPRS])
    P.barrier()
    P.held = set()
    P.top = mM
    stop('R')
    R16 = slice(0, 16)
    OS = P.alloc(1024); dOS = Dep()
    mS = P.top
    U = P.alloc(RPROJ); dU = Dep(); dLD = Dep()
    VEC = P.alloc(7 * 512); VEC3 = v3(VEC, 7)
    Q = P.alloc(6 * 512); Q3 = v3(Q, 6); dQ = Dep()
    TS = P.alloc(4 * 512); TS3 = v3(TS, 4); dTS = Dep()
    QB = P.alloc(6 * 64); QB3 = v3(QB, 6); dQB = Dep()
    YSB = P.alloc(128); dYSB = Dep()
    YT = P.alloc(512); dYT = Dep()
    mS1 = P.top
    MU = P.alloc(RPROJ); SHP = P.alloc(RPROJ)
    dma("sp", MU[R16, :], mu_r.partition_broadcast(16), writes=[dLD])
    dma("sp", SHP[R16, :], st_shift, writes=[dLD])
    for i, src in enumerate((w0_r, a0_r, kk_r, ka_r, lw_r, lb_r, rk_r)):
        dma("sp", VEC3[R16, i, :], src.partition_broadcast(16), writes=[dLD])
    W0v, A0v, KKv, KAv, LWv, LBv, RKv = [VEC3[R16, i, :] for i in range(7)]
    op("dve", lambda e: e.tensor_tensor(U[R16, :], SHP[R16, :], PRS[R16, 0:RPROJ], ALU.subtract), [dLD, dPRS], [dU])
    op("dve", lambda e: e.tensor_tensor(U[R16, :], U[R16, :], MU[R16, :], ALU.mult), [dU, dLD], [dU])
    op("dve", lambda e: e.tensor_tensor(U[R16, :], U[R16, :], PRS[R16, 0:RPROJ], ALU.add), [dU, dPRS], [dU])
    rS, kS, vS = U[R16, 0:512], U[R16, 512:1024], U[R16, 1024:1536]
    h8 = lambda ap: ap.rearrange("p (h k) -> p h k", h=8)
    b8 = lambda ap: ap.unsqueeze(2).to_broadcast([16, 8, 64])
    bk, dbk = P.bank()
    mm(bk[R16, :], LUW[0:64, LP:NT], WUA[0:64, :], True, True, [dLUW[4], dLW], [dbk])
    op("dve", lambda e: e.tensor_tensor(Q3[R16, 1, :], bk[R16, :], W0v, ALU.add), [dbk, dLD], [dQ])
    op("act", lambda e: e.activation(Q3[R16, 1, :], Q3[R16, 1, :], AF.Sigmoid), [dQ], [dQ])
    op("act", lambda e: e.activation(Q3[R16, 1, :], Q3[R16, 1, :], AF.Exp, scale=-C0), [dQ], [dQ])
    bk, dbk = P.bank()
    mm(bk[R16, :], LUW[64:128, LP:NT], WUA[64:128, :], True, True, [dLUW[4], dLW], [dbk])
    op("dve", lambda e: e.tensor_tensor(TS3[R16, 0, :], bk[R16, :], A0v, ALU.add), [dbk, dLD], [dTS])
    op("act", lambda e: e.activation(TS3[R16, 0, :], TS3[R16, 0, :], AF.Sigmoid), [dTS], [dTS])
    bk, dbk = P.bank()
    mm(bk[R16, :], LUG[:, LP:NT], GUP, True, True, [dLUG[4], dLW], [dbk])
    cp("act", TS3[R16, 3, :], bk[R16, :], [dbk], [dTS])
    aS = TS3[R16, 0, :]; kkS = TS3[R16, 1, :]; tS = TS3[R16, 2, :]; gS = TS3[R16, 3, :]
    op("dve", lambda e: e.tensor_copy(Q3[R16, 0, :], rS), [dU], [dQ])
    op("dve", lambda e: e.tensor_copy(Q3[R16, 3, :], vS), [dU], [dQ])
    op("dve", lambda e: e.tensor_tensor(kkS, kS, KKv, ALU.mult), [dU, dLD], [dTS])
    op("dve", lambda e: e.tensor_tensor(tS, kkS, kkS, ALU.mult), [dTS], [dTS])
    op("dve", lambda e: e.reduce_sum(SM[R16, 0:8], h8(tS), AX.X), [dTS], [dSM])
    op("dve", lambda e: e.tensor_scalar(SM[R16, 0:8], SM[R16, 0:8], 1e-24, None, ALU.max), [dSM], [dSM])
    op("act", lambda e: e.activation(SM[R16, 0:8], SM[R16, 0:8], AF.Sqrt), [dSM], [dSM])
    op("dve", lambda e: e.reciprocal(SM[R16, 0:8], SM[R16, 0:8]), [dSM], [dSM])
    op("dve", lambda e: e.tensor_tensor(h8(kkS), h8(kkS), b8(SM[R16, 0:8]), ALU.mult), [dTS, dSM], [dTS])
    op("dve", lambda e: e.tensor_scalar(Q3[R16, 4, :], kkS, -1.0, None, ALU.mult), [dTS], [dQ])
    op("dve", lambda e: e.tensor_tensor(Q3[R16, 5, :], kkS, aS, ALU.mult), [dTS], [dQ])
    op("dve", lambda e: e.tensor_tensor(tS, aS, KAv, ALU.mult), [dTS, dLD], [dTS])
    op("dve", lambda e: e.tensor_tensor(tS, tS, KAv, ALU.subtract), [dTS, dLD], [dTS])
    op("dve", lambda e: e.tensor_scalar(tS, tS, 1.0, None, ALU.add), [dTS], [dTS])
    op("dve", lambda e: e.tensor_tensor(Q3[R16, 2, :], kS, tS, ALU.mult), [dU, dTS], [dQ])
    op("dve", lambda e: e.tensor_tensor(tS, rS, Q3[R16, 2, :], ALU.mult), [dU, dQ], [dTS])
    op("dve", lambda e: e.tensor_tensor(tS, tS, RKv, ALU.mult), [dTS, dLD], [dTS])
    op("dve", lambda e: e.reduce_sum(SM[R16, 8:16], h8(tS), AX.X), [dTS], [dSM])
    dSCR = Dep()
    for q_ in range(6):
        dma("sp", scrA.rearrange("(b h) (q k) -> b q h k", h=8, k=64)[:, q_], Q3[R16, q_, :].rearrange("p (h k) -> p h k", h=8), reads=[dQ], writes=[dSCR])
    dma("sp", QB, scrA[:, 0:384], reads=[dSCR], writes=[dQB])
    SW_ = P.alloc(4096); TMP = P.alloc(4096); dSt = Dep(); dTmp = Dep()
    dma("sp", SW_, st_wkv, writes=[dSt])
    S3_ = v3(SW_, 64); T3_ = v3(TMP, 64)
    bv = lambda q: QB3[:, q, :].unsqueeze(1).to_broadcast([128, 64, 64])
    bkk = lambda ap: ap.unsqueeze(2).to_broadcast([128, 64, 64])
    op("dve", lambda e: e.tensor_tensor(T3_, S3_, bv(4), ALU.mult), [dSt, dQB], [dTmp])
    op("dve", lambda e: e.reduce_sum(YSB[:, 64:128], T3_, AX.X), [dTmp], [dYSB])
    op("dve", lambda e: e.tensor_tensor(S3_, S3_, bv(1), ALU.mult), [dSt, dQB, dTmp], [dSt])
    op("dve", lambda e: e.tensor_tensor(T3_, bkk(YSB[:, 64:128]), bv(5), ALU.mult), [dYSB, dQB], [dTmp])
    op("dve", lambda e: e.tensor_tensor(S3_, S3_, T3_, ALU.add), [dSt, dTmp], [dSt])
    op("dve", lambda e: e.tensor_tensor(T3_, bkk(QB3[:, 3, :]), bv(2), ALU.mult), [dQB, dSt], [dTmp])
    op("dve", lambda e: e.tensor_tensor(S3_, S3_, T3_, ALU.add), [dSt, dTmp], [dSt])
    dma("sp", nwkv_s, SW_, reads=[dSt])
    op("dve", lambda e: e.tensor_tensor(T3_, S3_, bv(0), ALU.mult), [dSt, dQB], [dTmp])
    op("dve", lambda e: e.reduce_sum(YSB[:, 0:64], T3_, AX.X), [dTmp], [dYSB])
    dSCR2 = Dep()
    dma("sp", scrB, YSB[:, 0:64], reads=[dYSB], writes=[dSCR2])
    dma("sp", YT[R16, :], scrB.rearrange("(b h) v -> b (h v)", h=8), reads=[dSCR2], writes=[dYT])
    yT = YT[R16, :]
    op("dve", lambda e: e.reduce_sum(SM[R16, 16:24], h8(yT), AX.X), [dYT], [dSM])
    op("dve", lambda e: e.tensor_scalar(SM[R16, 16:24], SM[R16, 16:24], 1.0 / 64.0, None, ALU.mult), [dSM], [dSM])
    op("dve", lambda e: e.tensor_tensor(h8(yT), h8(yT), b8(SM[R16, 16:24]), ALU.subtract), [dYT, dSM], [dYT])
    op("dve", lambda e: e.tensor_tensor(tS, yT, yT, ALU.mult), [dYT, dTS], [dTS])
    op("dve", lambda e: e.reduce_sum(SM[R16, 24:32], h8(tS), AX.X), [dTS], [dSM])
    op("dve", lambda e: e.tensor_scalar(SM[R16, 24:32], SM[R16, 24:32], 1.0 / 64.0, 64e-5, ALU.mult, ALU.add), [dSM], [dSM])
    op("act", lambda e: e.activation(SM[R16, 24:32], SM[R16, 24:32], AF.Sqrt), [dSM], [dSM])
    op("dve", lambda e: e.reciprocal(SM[R16, 24:32], SM[R16, 24:32]), [dSM], [dSM])
    op("dve", lambda e: e.tensor_tensor(h8(yT), h8(yT), b8(SM[R16, 24:32]), ALU.mult), [dYT, dSM], [dYT])
    op("dve", lambda e: e.tensor_tensor(yT, yT, LWv, ALU.mult), [dYT, dLD], [dYT])
    op("dve", lambda e: e.tensor_tensor(yT, yT, LBv, ALU.add), [dYT, dLD], [dYT])
    op("dve", lambda e: e.tensor_tensor(h8(tS), h8(vS), b8(SM[R16, 8:16]), ALU.mult), [dU, dSM, dTS], [dTS])
    op("dve", lambda e: e.tensor_tensor(yT, yT, tS, ALU.add), [dYT, dTS], [dYT])
    op("dve", lambda e: e.tensor_tensor(OS[R16, 0:512], yT, gS, ALU.mult), [dYT, dTS], [dOS])
    P.barrier()
    P.top = mS
    XA = P.alloc(1024); dLD2 = Dep(); dXA = Dep()
    TT_ = P.alloc(1024); dTT = Dep()
    B8 = P.alloc(32); dB8 = Dep()
    GNS = P.alloc(512)
    PK = P.alloc(8 * 328); PK3 = v3(PK, 8); dPK = Dep()
    PKB = P.alloc(328); dPKB = Dep()
    YM = P.alloc(64); dYM = Dep()
    YMT = P.alloc(512); dYMT = Dep()
    OSB = P.alloc(1024, BF16)
    mS2 = P.top
    CW = P.alloc(4096); CB = P.alloc(1024); SCV = P.alloc(3072)
    dma("sp", CW[R16, :], cw_r.partition_broadcast(16), writes=[dLD2])
    dma("sp", CB[R16, :], cb_r.partition_broadcast(16), writes=[dLD2])
    dma("sp", SCV[R16, :], st_conv, writes=[dLD2])
    dma("sp", ncv_s[:, 0:2048], SCV[R16, 1024:3072], reads=[dLD2])
    dma("sp", ncv_s[:, 2048:3072], PRS[R16, RPROJ + 512:RPROJ + 1536], reads=[dPRS])
    op("dve", lambda e: e.tensor_tensor(XA[R16, :], PRS[R16, RPROJ + 512:RPROJ + 1536], CW[R16, 3072:4096], ALU.mult), [dPRS, dLD2], [dXA])
    op("dve", lambda e: e.tensor_tensor(XA[R16, :], XA[R16, :], CB[R16, :], ALU.add), [dXA, dLD2], [dXA])
    for i in range(3):
        op("dve", lambda e: e.tensor_tensor(TT_[R16, :], SCV[R16, i * 1024:(i + 1) * 1024], CW[R16, i * 1024:(i + 1) * 1024], ALU.mult), [dLD2], [dTT])
        op("dve", lambda e: e.tensor_tensor(XA[R16, :], XA[R16, :], TT_[R16, :], ALU.add), [dXA, dTT], [dXA])
    op("act", lambda e: e.activation(XA[R16, :], XA[R16, :], AF.Silu), [dXA], [dXA])
    dma("sp", B8[R16, 0:8], dtb_r.partition_broadcast(16), writes=[dB8])
    dma("sp", B8[R16, 8:16], alog_r.partition_broadcast(16), writes=[dB8])
    dma("sp", B8[R16, 16:24], dsk_r.partition_broadcast(16), writes=[dB8])
    dma("sp", GNS[R16, :], gnw_r.partition_broadcast(16), writes=[dB8])
    op("act", lambda e: e.activation(B8[R16, 8:16], B8[R16, 8:16], AF.Exp), [dB8], [dB8])
    op("dve", lambda e: e.tensor_tensor(B8[R16, 24:32], PRS[R16, PROJ - 8:PROJ], B8[R16, 0:8], ALU.add), [dPRS, dB8], [dB8])
    op("act", lambda e: e.activation(B8[R16, 24:32], B8[R16, 24:32], AF.Exp), [dB8], [dB8])
    op("act", lambda e: e.activation(B8[R16, 24:32], B8[R16, 24:32], AF.Ln, bias=EPS[R16, 2:3], scale=1.0), [dB8, dFV], [dB8])
    op("dve", lambda e: e.memset(PK[R16, :], 0.0), [], [dPK])
    op("dve", lambda e: e.tensor_tensor(PK3[R16, :, 0:64], h8(XA[R16, 0:512]), b8(B8[R16, 24:32]), ALU.mult), [dXA, dB8], [dPK])
    for g in range(2):
        op("dve", lambda e: e.tensor_copy(PK3[R16, 4 * g:4 * g + 4, 64:192], XA[R16, 512 + g * 128:640 + g * 128].unsqueeze(1).to_broadcast([16, 4, 128])), [dXA], [dPK])
        op("dve", lambda e: e.tensor_copy(PK3[R16, 4 * g:4 * g + 4, 192:320], XA[R16, 768 + g * 128:896 + g * 128].unsqueeze(1).to_broadcast([16, 4, 128])), [dXA], [dPK])
    op("dve", lambda e: e.tensor_tensor(B8[R16, 8:16], B8[R16, 8:16], B8[R16, 24:32], ALU.mult), [dB8], [dB8])
    op("act", lambda e: e.activation(PK3[R16, :, 320], B8[R16, 8:16], AF.Exp, scale=-1.0), [dB8, dPK], [dPK])
    dSCR3 = Dep()
    dma("sp", scrC.rearrange("(b h) c -> b (h c)", h=8), PK[R16, :], reads=[dPK], writes=[dSCR3])
    dma("sp", PKB, scrC, reads=[dSCR3], writes=[dPKB])
    HS = P.alloc(4096); TM2 = P.alloc(4096); dHS = Dep(); dTM2 = Dep()
    H3 = v3(HS, 32); M3 = v3(TM2, 32)
    for half in range(2):
        dma("sp", HS, st_ssm[:, half * 4096:(half + 1) * 4096], writes=[dHS])
        op("dve", lambda e: e.tensor_scalar(HS, HS, PKB[:, 320:321], None, ALU.mult), [dHS, dPKB], [dHS])
        op("dve", lambda e: e.tensor_tensor(M3, PKB[:, half * 32:(half + 1) * 32].unsqueeze(2).to_broadcast([128, 32, 128]), PKB[:, 64:192].unsqueeze(1).to_broadcast([128, 32, 128]), ALU.mult), [dPKB], [dTM2])
        op("dve", lambda e: e.tensor_tensor(HS, HS, TM2, ALU.add), [dHS, dTM2], [dHS])
        dma("sp", nssm_s[:, half * 4096:(half + 1) * 4096], HS, reads=[dHS])
        op("dve", lambda e: e.tensor_tensor(M3, H3, PKB[:, 192:320].unsqueeze(1).to_broadcast([128, 32, 128]), ALU.mult), [dHS, dPKB], [dTM2])
        op("dve", lambda e: e.reduce_sum(YM[:, half * 32:(half + 1) * 32], M3, AX.X), [dTM2], [dYM])
    dSCR4 = Dep()
    dma("sp", scrD, YM, reads=[dYM], writes=[dSCR4])
    dma("sp", YMT[R16, :], scrD.rearrange("(b h) v -> b (h v)", h=8), reads=[dSCR4], writes=[dYMT])
    ym = YMT[R16, :]
    op("dve", lambda e: e.tensor_tensor(h8(TT_[R16, 0:512]), h8(XA[R16, 0:512]), b8(B8[R16, 16:24]), ALU.mult), [dXA, dB8], [dTT])
    op("dve", lambda e: e.tensor_tensor(ym, ym, TT_[R16, 0:512], ALU.add), [dYMT, dTT], [dYMT])
    op("act", lambda e: e.activation(TT_[R16, 512:1024], PRS[R16, RPROJ:RPROJ + 512], AF.Silu), [dPRS, dTT], [dTT])
    op("dve", lambda e: e.tensor_tensor(ym, ym, TT_[R16, 512:1024], ALU.mult), [dYMT, dTT], [dYMT])
    op("dve", lambda e: e.tensor_tensor(TT_[R16, 0:512], ym, ym, ALU.mult), [dYMT, dTT], [dTT])
    op("dve", lambda e: e.reduce_sum(SM[R16, 32:34], TT_[R16, 0:512].rearrange("p (g c) -> p g c", g=2), AX.X), [dTT], [dSM])
    op("dve", lambda e: e.tensor_scalar(SM[R16, 32:34], SM[R16, 32:34], 1.0 / 256.0, 1e-5, ALU.mult, ALU.add), [dSM], [dSM])
    op("act", lambda e: e.activation(SM[R16, 32:34], SM[R16, 32:34], AF.Sqrt), [dSM], [dSM])
    op("dve", lambda e: e.reciprocal(SM[R16, 32:34], SM[R16, 32:34]), [dSM], [dSM])
    op("dve", lambda e: e.tensor_tensor(ym.rearrange("p (g c) -> p g c", g=2), ym.rearrange("p (g c) -> p g c", g=2), SM[R16, 32:34].unsqueeze(2).to_broadcast([16, 2, 256]), ALU.mult), [dYMT, dSM], [dYMT])
    op("dve", lambda e: e.tensor_tensor(OS[R16, 512:1024], ym, GNS[R16, :], ALU.mult), [dYMT, dB8], [dOS])
    cp("act", OSB[R16, :], OS[R16, :], [dOS], [dOS])
    bk, dbk = P.bank()
    psb = bk[:].bitcast(BF16)
    for dc in range(8):
        op("pe", lambda e: e.transpose(psb[:, dc * 16:(dc + 1) * 16], OSB[R16, dc * 128:(dc + 1) * 128], identb[0:16, 0:16]), [dOS, dCSTB], [dbk])
    cp("dve", MIXT3[:, :, LP:NT], v3(psb[:, 0:128], 8), [dbk], [dMIX])
    P.barrier()
    P.top = mM

    stop('S')
    def ln_tile(V, rows, G, Bv, dV, dGB, par=0):
        rs_ = slice(0, rows)
        ST = STs[par]; dST = dSTs[par]; JUNK = JUNKs[par]; dJ = dJs[par]
        op("act", lambda e: e.activation(JUNK[rs_, :], V[rs_, :], AF.Copy, accum_out=ST[rs_, 0:1]), [dV, dST], [dJ, dST])
        op("act", lambda e: e.activation(JUNK[rs_, :], V[rs_, :], AF.Square, accum_out=ST[rs_, 1:2]), [dV, dST], [dJ, dST])
        op("dve", lambda e: e.tensor_scalar(ST[rs_, 2:3], ST[rs_, 0:1], 1.0 / DM, None, ALU.mult), [dST], [dST])
        op("dve", lambda e: e.tensor_tensor(ST[rs_, 3:4], ST[rs_, 2:3], ST[rs_, 2:3], ALU.mult), [dST], [dST])
        op("dve", lambda e: e.scalar_tensor_tensor(ST[rs_, 4:5], ST[rs_, 1:2], 1.0 / DM, ST[rs_, 3:4], ALU.mult, ALU.subtract), [dST], [dST])
        op("act", lambda e: e.activation(ST[rs_, 5:6], ST[rs_, 4:5], AF.Sqrt, bias=EPS[rs_, 1:2], scale=1.0), [dST, dFV], [dST])
        op("dve", lambda e: e.reciprocal(ST[rs_, 6:7], ST[rs_, 5:6]), [dST], [dST])
        op("dve", lambda e: e.scalar_tensor_tensor(ST[rs_, 7:8], ST[rs_, 2:3], -1.0, ST[rs_, 6:7], ALU.mult, ALU.mult), [dST], [dST])
        op("act", lambda e: e.activation(V[rs_, :], V[rs_, :], AF.Identity, bias=ST[rs_, 7:8], scale=ST[rs_, 6:7]), [dV, dST], [dV])
        op("dve", lambda e: e.tensor_tensor(V[rs_, :], V[rs_, :], G[rs_, :], ALU.mult), [dV, dGB], [dV])
        op("dve", lambda e: e.tensor_tensor(V[rs_, :], V[rs_, :], Bv[rs_, :], ALU.add), [dV, dGB], [dV])

    P.top = mAfterMIX
    W2 = P.alloc(32 * DM, BF16); W23 = v3(W2, 32); dW2 = Dep()
    mAfterW2 = P.top
    mO = P.top
    WO = P.alloc(8 * DM, BF16); WO3 = v3(WO, 8); dWO = Dep()
    dma("pool", WO3, w_out.rearrange("(a p) c -> p a c", p=128), writes=[dWO])
    for q in range(8):
        dma("pool", W23[:, q * 4:(q + 1) * 4, :], w_ff2[q * 512:(q + 1) * 512, :].rearrange("(a p) c -> p a c", p=128), writes=[dW2])
    LG = P.alloc(DM); LB_ = P.alloc(DM); dLG = Dep()
    dma("sp", LG, l1g_r.partition_broadcast(128), writes=[dLG])
    dma("sp", LB_, l1b_r.partition_broadcast(128), writes=[dLG])
    XIN = [P.alloc(DM) for _ in range(3)]; dXIN = [Dep() for _ in range(3)]
    V32 = [P.alloc(DM) for _ in range(3)]; dV32 = [Dep() for _ in range(3)]
    JUNKs = [P.alloc(DM, BF16) for _ in range(3)]; dJs = [Dep() for _ in range(3)]
    HBs = [P.alloc(DM, BF16) for _ in range(3)]; dHBs = [Dep() for _ in range(3)]
    HT3 = XT3
    dYD = [Dep() for _ in TILES]

    def ytile(j):
        t0, rows = TILES[j]
        return y_p[t0:t0 + rows, :] if j < 16 else y_s

    for j, (t0, rows) in enumerate(TILES):
        b = j % 3
        rs_ = slice(0, rows)
        dma("sp", XIN[b][rs_, :], x_p[t0:t0 + rows, :] if j < 16 else x_s, writes=[dXIN[b]])
        for half in range(2):
            bk, dbk = P.bank()
            for dc in range(8):
                mm(bk[rs_, :], MIXT3[:, dc, t0:t0 + rows], WO3[:, dc, half * 512:(half + 1) * 512], dc == 0, dc == 7, [dMIX, dWO], [dbk])
            op("dve", lambda e: e.scalar_tensor_tensor(V32[b][rs_, half * 512:(half + 1) * 512], XIN[b][rs_, half * 512:(half + 1) * 512], ALPHA, bk[rs_, :], ALU.mult, ALU.add), [dXIN[b], dbk], [dV32[b]])
        ln_tile(V32[b], rows, LG, LB_, dV32[b], dLG, b)
        HB = HBs[b]; dHB = dHBs[b]
        dma("sp", ytile(j), V32[b][rs_, :], reads=[dV32[b]], writes=[dYD[j]])
        cp("act", HB[rs_, :], V32[b][rs_, :], [dV32[b]], [dHB])
        bk, dbk = P.bank()
        psb = bk[:].bitcast(BF16)
        for dc in range(8):
            op("pe", lambda e: e.transpose(psb[:, dc * 128:dc * 128 + rows], HB[rs_, dc * 128:(dc + 1) * 128], identb[0:rows, 0:rows]), [dHB, dCSTB], [dbk])
        cp(ev_eng(), HT3[:, :, t0:t0 + rows], v3(psb, 8)[:, :, 0:rows], [dbk], [dXT[j]])
    P.barrier()

    stop('O')
    P.top = mAfterW2
    F1T = MIXT[:, 0:32 * 512]; F1T3 = v3(F1T, 32); dF1 = Dep()
    NW1 = 4
    W1 = [P.alloc(8 * 512, BF16) for _ in range(NW1)]; dW1 = [Dep() for _ in range(NW1)]
    RL = [P.alloc(512) for _ in range(2)]; dRL = [Dep(), Dep()]
    RLS = [P.alloc(16) for _ in range(2)]; dRLS = [Dep(), Dep()]
    F1S = P.alloc(32 * 16, BF16); F1S3 = v3(F1S, 32); dF1S = Dep()
    LG2 = P.alloc(DM); LB2 = P.alloc(DM); dLG2 = Dep()
    dma("sp", LG2, l2g_r.partition_broadcast(128), writes=[dLG2])
    dma("sp", LB2, l2b_r.partition_broadcast(128), writes=[dLG2])
    HIN = [P.alloc(DM) for _ in range(2)]; dHIN = [Dep(), Dep()]
    VV = [P.alloc(DM) for _ in range(2)]; dVV = [Dep(), Dep()]
    JUNKs = [P.alloc(DM, BF16), P.alloc(DM, BF16)]; dJs = [Dep(), Dep()]
    w1i = 0; rli = 0; tcount = 0
    for (s0, sn) in SLABS[:4]:
        last_blk = (s0 == 1536)
        for jg in range(8):
            wb = w1i % NW1; w1i += 1
            w13 = v3(W1[wb], 8)
            dma("pool", w13, w_ff1[:, jg * 512:(jg + 1) * 512].rearrange("(a p) c -> p a c", p=128), writes=[dW1[wb]])
            for jc in range(4):
                bk, dbk = P.bank()
                for dc in range(8):
                    mm(bk[:, 0:sn], w13[:, dc, jc * 128:(jc + 1) * 128], HT3[:, dc, s0:s0 + sn], dc == 0, dc == 7, [dW1[wb]] + xt_deps(s0, sn), [dbk])
                rb = rli % 2; rli += 1
                op("act", lambda e: e.activation(RL[rb][:, 0:sn], bk[:, 0:sn], AF.Relu), [dbk], [dRL[rb]])
                op("dve", lambda e: e.tensor_tensor(F1T3[:, jg * 4 + jc, 0:sn], RL[rb][:, 0:sn], RL[rb][:, 0:sn], ALU.mult), [dRL[rb]], [dF1])
                if last_blk:
                    bk, dbk = P.bank()
                    for dc in range(8):
                        mm(bk[:, 0:16], w13[:, dc, jc * 128:(jc + 1) * 128], HT3[:, dc, LP:NT], dc == 0, dc == 7, [dW1[wb], dXT[16]], [dbk])
                    op("act", lambda e: e.activation(RLS[rb][:, 0:16], bk[:, 0:16], AF.Relu), [dbk], [dRLS[rb]])
                    op("dve", lambda e: e.tensor_tensor(F1S3[:, jg * 4 + jc, :], RLS[rb][:, 0:16], RLS[rb][:, 0:16], ALU.mult), [dRLS[rb]], [dF1S])
        for j, (t0, rows) in enumerate(TILES):
            if not ((t0 >= s0 and t0 < s0 + sn) or (last_blk and j == 16)):
                continue
            b = tcount % 2; tcount += 1
            rs_ = slice(0, rows)
            lo = t0 - s0
            dma("sp", HIN[b][rs_, :], ytile(j), reads=[dYD[j]], writes=[dHIN[b]])
            for half in range(2):
                bk, dbk = P.bank()
                for jc in range(32):
                    if j == 16:
                        mm(bk[rs_, :], F1S3[:, jc, :], W23[:, jc, half * 512:(half + 1) * 512], jc == 0, jc == 31, [dF1S, dW2], [dbk])
                    else:
                        mm(bk[rs_, :], F1T3[:, jc, lo:lo + rows], W23[:, jc, half * 512:(half + 1) * 512], jc == 0, jc == 31, [dF1, dW2], [dbk])
                op("dve", lambda e: e.scalar_tensor_tensor(VV[b][rs_, half * 512:(half + 1) * 512], HIN[b][rs_, half * 512:(half + 1) * 512], ALPHA, bk[rs_, :], ALU.mult, ALU.add), [dHIN[b], dbk], [dVV[b]])
            ln_tile(VV[b], rows, LG2, LB2, dVV[b], dLG2, b)
            dma("sp", ytile(j), VV[b][rs_, :], reads=[dVV[b], dHIN[b]], writes=[dYD[j]])
    P.finish()
    return nc


def _consts():
    r = np.arange(128)
    c = np.arange(128)
    R, Cc = np.meshgrid(r, c, indexing="ij")
    ident = (R == Cc).astype(np.float32)
    sblk = (R // 64 == Cc // 64)
    mSU = (R < Cc).astype(np.float32)
    mSL = (R > Cc).astype(np.float32)
    mUI = (R <= Cc).astype(np.float32)
    bones = sblk.astype(np.float32)
    ones = np.ones((128, 128), np.float32)
    negm = np.where(R > Cc, -30000.0, 0.0).astype(np.float32)
    s_ = R % 64
    mask4 = np.where(Cc < 64, s_ < Cc, s_ <= (Cc - 64)).astype(np.float32)
    return np.concatenate([ident, mSU, mSL, mUI, bones, ones, negm, mask4], axis=1)


_NC_CACHE = {}


def kernel(x_prompt, x_sample, state_shift, state_wkv, state_conv, state_ssm, w_in, mu_shift,
           w0, w_up, a0, a_up, g_up, k_k, k_a, r_k, lnx_w, lnx_b, conv_w, conv_b, dt_bias,
           a_log, d_skip, gnorm_w, w_out, ln1_g, ln1_b, w_ff1, w_ff2, ln2_g, ln2_b):
    f = lambda a: np.ascontiguousarray(np.asarray(a, dtype=np.float32))
    x_prompt = f(x_prompt); x_sample = f(x_sample)
    fv = np.zeros((128, 96), np.float32)
    fv[:, 0:14] = f(mu_shift)[0].reshape(14, 128).T
    for i, v in enumerate((w0, a0, k_k, k_a, lnx_w, lnx_b)):
        fv[:, 14 + 4 * i:18 + 4 * i] = f(v)[0].reshape(4, 128).T
    cw = f(conv_w)[0]
    for fc in range(8):
        for i in range(4):
            fv[:, 38 + fc * 4 + i] = cw[i, fc * 128:(fc + 1) * 128]
    fv[:, 70:78] = f(conv_b)[0].reshape(8, 128).T
    rk = f(r_k)[0]
    rkb = np.zeros((128, 4, 128), np.float32)
    for hp in range(4):
        for hh in range(2):
            rkb[hh * 64:(hh + 1) * 64, hp, hh * 64:(hh + 1) * 64] = rk[2 * hp + hh][:, None]
    rkb = rkb.reshape(128, 512)
    cst = _consts()
    row = lambda a: f(a).reshape(1, -1)
    common = {
        "w_in": f(w_in)[0], "w_up": f(w_up)[0], "a_up": f(a_up)[0], "g_up": f(g_up)[0], "w_out": f(w_out)[0],
        "w_ff1": f(w_ff1)[0], "w_ff2": f(w_ff2)[0], "cst": cst, "fv": fv, "rkb": rkb,
        "mu_r": row(mu_shift), "w0_r": row(w0), "a0_r": row(a0), "kk_r": row(k_k), "ka_r": row(k_a),
        "lw_r": row(lnx_w), "lb_r": row(lnx_b), "rk_r": row(r_k), "cw_r": row(conv_w), "cb_r": row(conv_b),
        "dtb_r": row(dt_bias), "alog_r": row(a_log), "dsk_r": row(d_skip), "gnw_r": row(gnorm_w),
        "l1g_r": row(ln1_g), "l1b_r": row(ln1_b), "l2g_r": row(ln2_g), "l2b_r": row(ln2_b),
    }
    in_maps = []
    for c in range(NCORES):
        sl = slice(c * NSMP, (c + 1) * NSMP)
        m = dict(common)
        m["x_p"] = x_prompt[c]
        m["x_s"] = x_sample[sl, 0, :]
        m["st_shift"] = f(state_shift)[0, sl]
        m["st_wkv"] = f(state_wkv)[0, sl].reshape(128, 4096)
        m["st_conv"] = f(state_conv)[0, sl].reshape(NSMP, 3072)
        m["st_ssm"] = f(state_ssm)[0, sl].reshape(128, 8192)
        in_maps.append({k: np.ascontiguousarray(v) for k, v in m.items()})
    nc = build_program()
    res = run_bass_kernel_spmd(nc, in_maps, core_ids=list(range(NCORES)))
    R = res.results
    cat = lambda k: np.stack([np.asarray(r[k], dtype=np.float32) for r in R])
    y_p = cat("y_p")
    y_s = cat("y_s").reshape(128, 1, DM)
    nsh_p = cat("nsh_p").reshape(1, 8, RPROJ)
    nwkv_p = cat("nwkv_p").reshape(1, 8, 8, 64, 64)
    ncv_p = cat("ncv_p").reshape(1, 8, 3, 1024)
    nssm_p = cat("nssm_p").reshape(1, 8, 8, 64, 128)
    nsh_s = cat("nsh_s").reshape(1, 128, RPROJ)
    nwkv_s = cat("nwkv_s").reshape(1, 128, 8, 64, 64)
    ncv_s = cat("ncv_s").reshape(1, 128, 3, 1024)
    nssm_s = cat("nssm_s").reshape(1, 128, 8, 64, 128)
    return (y_p, y_s, nsh_p, nwkv_p, ncv_p, nssm_p, nsh_s, nwkv_s, ncv_s, nssm_s)
```

```python
import contextlib
import os
import math
import numpy as np
import concourse.bass as bass
import concourse.mybir as mybir
from concourse.bass_utils import run_bass_kernel_spmd

F32 = mybir.dt.float32
BF16 = mybir.dt.bfloat16
AF = mybir.ActivationFunctionType
ALU = mybir.AluOpType
AX = mybir.AxisListType

ENGS = ("pe", "act", "dve", "pool", "sp")
NCORES = 8
LP = 2048
NSMP = 16
NT = LP + NSMP
DM = 1024
RPROJ = 1792
PROJ = 3336
DFF = 4096
C0 = math.exp(-0.5)
ALPHA = 2.0 ** 0.25
SLABS = [(0, 512), (512, 512), (1024, 512), (1536, 512), (2048, 16)]
TILES = [(i * 128, 128) for i in range(16)] + [(2048, 16)]
ARENA_W = 53000


class Dep:
    __slots__ = ("wn", "rn", "name", "excl")

    def __init__(self, name="", excl=False):
        self.wn = None
        self.rn = []
        self.name = name
        self.excl = excl


class _Rec:
    def __getattr__(self, name):
        def f(*a, **k):
            self.last = (name, a, k)
            return self
        return f


class Node:
    __slots__ = ("eng", "kind", "payload", "preds", "cost", "seg", "order", "t0", "t1", "idx", "ev", "prio", "nsucc", "succs", "npend")

    def __init__(self, eng, kind, payload, cost, seg, order):
        self.eng = eng
        self.kind = kind
        self.payload = payload
        self.preds = {}
        self.cost = cost
        self.seg = seg
        self.order = order
        self.t0 = 0.0
        self.t1 = 0.0
        self.idx = 0
        self.ev = None
        self.prio = 0.0
        self.succs = []
        self.npend = 0


def _free_elems(ap):
    n = 1
    for d in ap.shape[1:]:
        n *= int(d)
    return n


def _op_cost(E, rec):
    name, a, k = rec
    try:
        out = a[0]
        n = _free_elems(out)
        if E == "pe":
            passes = 4 if (len(a) > 1 and a[1].dtype == F32) else 1
            return 0.035 + n * passes / 2400.0
        if E == "pool":
            return 0.4 + n / 240.0
        return 0.17 + n / 960.0
    except Exception:
        return 0.5


class Prog:
    def __init__(self, nc, n_dma_sems=48):
        self.nc = nc
        self.n_dma_sems = n_dma_sems
        self.stack = contextlib.ExitStack()
        self.sem = {}
        for e in ENGS:
            self.sem[e] = self.stack.enter_context(nc.semaphore("s_" + e))
        for i in range(n_dma_sems):
            self.sem[("dma", i)] = self.stack.enter_context(nc.semaphore("s_dma%d" % i))
        self.arena = self.stack.enter_context(nc.sbuf_tensor("arena", [128, ARENA_W], F32))
        self.top = 0
        self.banks = []
        for i in range(8):
            t = self.stack.enter_context(nc.psum_tensor("bank%d" % i, [128, 512], F32))
            self.banks.append((t, Dep("bank%d" % i, excl=True)))
        self.bank_rr = 0
        self.held = set()
        self.nodes = []
        self.seg = 0
        self.pools = {}

    def bank(self, hold=False, pool=None):
        if pool is not None:
            lst = pool
            k = self.pools.get(tuple(lst), 0)
            self.pools[tuple(lst)] = k + 1
            return self.banks[lst[k % len(lst)]]
        while self.bank_rr in self.held:
            self.bank_rr = (self.bank_rr + 1) % 8
        i = self.bank_rr
        self.bank_rr = (self.bank_rr + 1) % 8
        if hold:
            self.held.add(i)
        return self.banks[i]

    def release(self, bank):
        for i, b in enumerate(self.banks):
            if b[0] is bank[0]:
                self.held.discard(i)

    def alloc(self, cols, dt=F32):
        w = cols if dt == F32 else (cols + 1) // 2
        off = self.top
        self.top += w
        assert self.top <= ARENA_W, ("arena overflow", self.top)
        ap = self.arena[:, off:off + w]
        if dt != F32:
            ap = ap.bitcast(dt)[:, 0:cols]
        return ap

    def _link(self, node, reads, writes):
        for d in reads:
            if d.wn is not None:
                node.preds[d.wn] = True
            if d.excl:
                for r in d.rn:
                    if r.eng != node.eng:
                        node.preds[r] = True
        for d in writes:
            if d.wn is not None:
                node.preds[d.wn] = True
            for r in d.rn:
                if r is node:
                    continue
                hard = (r.eng != node.eng) or (r.kind == "dma") or (node.kind == "dma")
                if r not in node.preds or hard:
                    node.preds[r] = hard or node.preds.get(r, False)
        for d in reads:
            d.rn.append(node)
        for d in writes:
            d.wn = node
            d.rn = []
        node.preds.pop(node, None)

    def op(self, E, fn, reads=(), writes=()):
        rec = _Rec()
        fn(rec)
        node = Node(E, "op", rec.last, _op_cost(E, rec.last), self.seg, len(self.nodes))
        self._link(node, reads, writes)
        self.nodes.append(node)
        return node

    def dma(self, E, out, in_, reads=(), writes=()):
        try:
            nbytes = int(out.shape[0]) * _free_elems(out) * (4 if out.dtype == F32 else 2)
        except Exception:
            nbytes = 1 << 16
        cost = 2.0 + nbytes / 120e3
        node = Node(E, "dma", (out, in_), cost, self.seg, len(self.nodes))
        self._link(node, reads, writes)
        self.nodes.append(node)
        return node

    def barrier(self):
        self.seg += 1

    def _schedule_segment(self, nodes, t_base):
        import heapq
        inseg = set(nodes)
        for n in nodes:
            n.succs = []
        for n in nodes:
            n.npend = 0
            for p in n.preds:
                if p in inseg:
                    p.succs.append(n)
                    n.npend += 1
        for n in reversed(nodes):
            best = 0.0
            for s_ in n.succs:
                if s_.prio > best:
                    best = s_.prio
            n.prio = best + n.cost + 0.3
        efree = {e: t_base for e in ENGS}
        ready = [n for n in nodes if n.npend == 0]
        order = []
        LAT = 0.45
        while ready:
            bestn = None
            bestkey = None
            for n in ready:
                t = efree[n.eng]
                for p in n.preds:
                    if p in inseg:
                        tp = p.t1 + (LAT if (p.eng != n.eng or p.kind == "dma") else 0.05)
                        if p.eng == n.eng and not n.preds[p]:
                            tp = p.t0
                        if tp > t:
                            t = tp
                key = (t, -n.prio, n.order)
                if bestkey is None or key < bestkey:
                    bestkey = key
                    bestn = n
            n = bestn
            ready.remove(n)
            n.t0 = bestkey[0]
            if n.kind == "dma":
                n.t1 = n.t0 + n.cost
                efree[n.eng] = n.t0 + (1.0 if n.eng == "pool" else 0.1)
            else:
                n.t1 = n.t0 + n.cost
                efree[n.eng] = n.t1
            order.append(n)
            for s_ in n.succs:
                s_.npend -= 1
                if s_.npend == 0:
                    ready.append(s_)
        assert len(order) == len(nodes)
        t_end = max([n.t1 for n in nodes] + [t_base])
        return order, t_end

    def finish(self):
        streams = {e: [] for e in ENGS}
        count = {e: 0 for e in ENGS}
        known = {e: {} for e in ENGS}
        dma_val = [0] * self.n_dma_sems
        dma_rr = 0

        def wait(E, key, val):
            if val == 0:
                return
            if known[E].get(key, 0) >= val:
                return
            known[E][key] = val
            streams[E].append(("wait", key, val))

        def full_barrier():
            for E in ENGS:
                for X in ENGS:
                    if X != E and count[X]:
                        wait(E, X, count[X])
                for k in range(self.n_dma_sems):
                    if dma_val[k]:
                        wait(E, ("dma", k), dma_val[k])

        nseg = self.seg + 1
        segs = [[] for _ in range(nseg)]
        for n in self.nodes:
            segs[n.seg].append(n)
        t_base = 0.0
        for si, seg_nodes in enumerate(segs):
            if si > 0:
                full_barrier()
            if not seg_nodes:
                continue
            order, t_base = self._schedule_segment(seg_nodes, t_base)
            need = set()
            last = {}
            pos = {}
            for i_, n in enumerate(order):
                pos[n] = i_
            waitsets = {}
            for n in order:
                if n.kind == "op":
                    last[n.eng] = n
                best = {}
                for p, hard in n.preds.items():
                    if p.kind == "dma" or p not in pos:
                        continue
                    if p.eng == n.eng and n.kind != "dma" and (n.eng == "pe" or not hard):
                        continue
                    b_ = best.get(p.eng)
                    if b_ is None or pos[p] > pos[b_]:
                        best[p.eng] = p
                waitsets[n] = set(best.values())
                need.update(best.values())
            for n in last.values():
                need.add(n)
            for n in order:
                E = n.eng
                ws_ = waitsets[n]
                for p, hard in n.preds.items():
                    if p.ev is None:
                        continue
                    if p.kind != "dma" and p in pos and p not in ws_:
                        continue
                    key, val = p.ev
                    if p.kind != "dma" and p.eng == E:
                        if E == "pe" or not hard:
                            continue
                    wait(E, key, val)
                if n.kind == "op":
                    if n in need:
                        count[E] += 1
                        n.ev = (E, count[E])
                        streams[E].append(("op", n.payload, n.ev))
                    else:
                        n.ev = None
                        streams[E].append(("opq", n.payload))
                else:
                    k = dma_rr
                    dma_rr = (dma_rr + 1) % self.n_dma_sems
                    key = ("dma", k)
                    wait(E, key, dma_val[k])
                    dma_val[k] += 16
                    n.ev = (key, dma_val[k])
                    streams[E].append(("dma", n.payload[0], n.payload[1], n.ev))
        self.est_us = t_base
        for k in range(self.n_dma_sems):
            if dma_val[k]:
                wait("sp", ("dma", k), dma_val[k])
        for e in ENGS:
            if e != "sp" and count[e]:
                wait("sp", e, count[e])
        self.count = count
        self.streams = streams
        sem = self.sem

        def replay(eng, items):
            for it in items:
                if it[0] == "wait":
                    eng.wait_ge(sem[it[1]], it[2])
                elif it[0] == "op":
                    nm, a, k = it[1]
                    getattr(eng, nm)(*a, **k).then_inc(sem[it[2][0]], 1)
                elif it[0] == "opq":
                    nm, a, k = it[1]
                    getattr(eng, nm)(*a, **k)
                else:
                    eng.dma_start(out=it[1], in_=it[2]).then_inc(sem[it[3][0]], 16)

        with self.nc.Block() as block:
            @block.tensor
            def _(e):
                replay(e, streams["pe"])

            @block.scalar
            def _(e):
                replay(e, streams["act"])

            @block.vector
            def _(e):
                replay(e, streams["dve"])

            @block.gpsimd
            def _(e):
                replay(e, streams["pool"])

            @block.sync
            def _(e):
                replay(e, streams["sp"])
        self.stack.close()


def v3(ap, a):
    return ap.rearrange("p (a b) -> p a b", a=a)


def v4(ap, a, b):
    return ap.rearrange("p (a b c) -> p a b c", a=a, b=b)


class _Stop(Exception):
    pass


def build_program():
    try:
        return _build_program()
    except _Stop as st:
        return st.args[0]


def _build_program():
    nc = bass.Bass("TRN2", target_bir_lowering=False)

    def din(name, shape):
        return nc.dram_tensor(name, list(shape), F32, kind="ExternalInput").ap()

    def dout(name, shape):
        return nc.dram_tensor(name, list(shape), F32, kind="ExternalOutput").ap()

    x_p = din("x_p", [LP, DM]); x_s = din("x_s", [NSMP, DM])
    st_shift = din("st_shift", [NSMP, RPROJ]); st_wkv = din("st_wkv", [128, 4096])
    st_conv = din("st_conv", [NSMP, 3072]); st_ssm = din("st_ssm", [128, 8192])
    w_in = din("w_in", [DM, PROJ]); w_up = din("w_up", [64, 512]); a_up = din("a_up", [64, 512])
    g_up = din("g_up", [128, 512]); w_out = din("w_out", [DM, DM])
    w_ff1 = din("w_ff1", [DM, DFF]); w_ff2 = din("w_ff2", [DFF, DM])
    cst = din("cst", [128, 1024]); fv = din("fv", [128, 96]); rkb = din("rkb", [128, 512])
    mu_r = din("mu_r", [1, RPROJ]); w0_r = din("w0_r", [1, 512]); a0_r = din("a0_r", [1, 512])
    kk_r = din("kk_r", [1, 512]); ka_r = din("ka_r", [1, 512]); lw_r = din("lw_r", [1, 512])
    lb_r = din("lb_r", [1, 512]); rk_r = din("rk_r", [1, 512]); cw_r = din("cw_r", [1, 4096])
    cb_r = din("cb_r", [1, 1024]); dtb_r = din("dtb_r", [1, 8]); alog_r = din("alog_r", [1, 8])
    dsk_r = din("dsk_r", [1, 8]); gnw_r = din("gnw_r", [1, 512])
    l1g_r = din("l1g_r", [1, DM]); l1b_r = din("l1b_r", [1, DM]); l2g_r = din("l2g_r", [1, DM]); l2b_r = din("l2b_r", [1, DM])

    y_p = dout("y_p", [LP, DM]); y_s = dout("y_s", [NSMP, DM])
    nsh_p = dout("nsh_p", [1, RPROJ]); nwkv_p = dout("nwkv_p", [512, 64]); ncv_p = dout("ncv_p", [3, 1024])
    nssm_p = dout("nssm_p", [512, 128]); nsh_s = dout("nsh_s", [NSMP, RPROJ]); nwkv_s = dout("nwkv_s", [128, 4096])
    ncv_s = dout("ncv_s", [NSMP, 3072]); nssm_s = dout("nssm_s", [128, 8192])
    scrA = nc.dram_tensor("scrA", [128, 7 * 64], F32, kind="Internal").ap()
    scrB = nc.dram_tensor("scrB", [128, 64], F32, kind="Internal").ap()
    scrC = nc.dram_tensor("scrC", [128, 64 + 128 + 128 + 8], F32, kind="Internal").ap()
    scrD = nc.dram_tensor("scrD", [128, 64], F32, kind="Internal").ap()

    P = Prog(nc)

    def stop(tag):
        if os.environ.get('MK_STOP') == tag:
            P.finish()
            raise _Stop(nc)

    op = P.op
    dma = P.dma
    rr = {"i": 0}

    def ev_eng():
        rr["i"] += 1
        return "act" if rr["i"] % 2 else "dve"

    def cp(E, out, in_, reads, writes):
        if E == "act":
            return op("act", lambda e: e.copy(out, in_), reads, writes)
        return op(E, lambda e: e.tensor_copy(out, in_), reads, writes)

    def mm(out, lhsT, rhs, start, stop, reads, writes):
        return op("pe", lambda e: e.matmul(out, lhsT, rhs, start=start, stop=stop), reads, writes)

    CST = P.alloc(1024); dCST = Dep()
    dma("sp", CST, cst, writes=[dCST])
    identf = CST[:, 0:128]; mSU = CST[:, 128:256]; mSL = CST[:, 256:384]; mUI = CST[:, 384:512]
    bones = CST[:, 512:640]; ones = CST[:, 640:768]; negm = CST[:, 768:896]; mask4 = CST[:, 896:1024]
    CSTB = P.alloc(512, BF16); dCSTB = Dep()
    dma("pool", CSTB[:, 0:128], cst[:, 0:128], writes=[dCSTB])
    dma("pool", CSTB[:, 128:256], cst[:, 512:640], writes=[dCSTB])
    identb = CSTB[:, 0:128]; bonesb = CSTB[:, 128:256]
    NUI = P.alloc(128)
    op("dve", lambda e: e.tensor_scalar(NUI, mUI, -1.0, None, ALU.mult), [dCST], [dCST])
    FV = P.alloc(96); dFV = Dep()
    dma("sp", FV, fv, writes=[dFV])
    OMM = P.alloc(14)
    op("dve", lambda e: e.tensor_scalar(OMM, FV[:, 0:14], -1.0, 1.0, ALU.mult, ALU.add), [dFV], [dFV])
    OMK = P.alloc(4)
    op("dve", lambda e: e.tensor_scalar(OMK, FV[:, 26:30], -1.0, 1.0, ALU.mult, ALU.add), [dFV], [dFV])
    EPS = P.alloc(4)
    op("dve", lambda e: e.memset(EPS[:, 0:1], 64e-5), [], [dFV])
    op("dve", lambda e: e.memset(EPS[:, 1:2], 1e-5), [], [dFV])
    op("dve", lambda e: e.memset(EPS[:, 2:3], 1.0), [], [dFV])
    op("dve", lambda e: e.memset(EPS[:, 3:4], 0.5), [], [dFV])
    HB0 = P.alloc(8)
    op("dve", lambda e: e.tensor_scalar(HB0, FV[:, 14:22], -1.0, None, ALU.mult), [dFV], [dFV])
    RKB = P.alloc(512, BF16)
    dma("pool", RKB, rkb, writes=[dFV])
    WUA = P.alloc(512, BF16); GUP = P.alloc(512, BF16); dLW = Dep()
    dma("pool", WUA[0:64, :], w_up, writes=[dLW])
    dma("pool", WUA[64:128, :], a_up, writes=[dLW])
    dma("pool", GUP, g_up, writes=[dLW])
    STs = [P.alloc(8) for _ in range(4)]; dSTs = [Dep() for _ in range(4)]
    SM = P.alloc(64); dSM = Dep()
    XT = P.alloc(8 * NT, BF16); XT3 = v3(XT, 8)
    dXT = [Dep() for _ in TILES]
    MIXT = P.alloc(8 * NT, BF16); MIXT3 = v3(MIXT, 8)
    dMIX = Dep()
    mAfterMIX = P.top
    PRS = P.alloc(PROJ); dPRS = Dep()
    LUW = P.alloc(NT, BF16); LUG = P.alloc(NT, BF16); dLUW = [Dep() for _ in range(5)]; dLUG = [Dep() for _ in range(5)]
    base_top = P.top

    def xt_deps(t0, n):
        return [dXT[j] for j, (a, r) in enumerate(TILES) if a < t0 + n and a + r > t0]

    XB = [P.alloc(DM, BF16) for _ in range(2)]; dXB = [Dep(), Dep()]
    for j, (t0, rows) in enumerate(TILES):
        b = j % 2
        src = x_p[t0:t0 + rows, :] if j < 16 else x_s
        dma("pool", XB[b][0:rows, :], src, writes=[dXB[b]])
        bk, dbk = P.bank()
        psb = bk[:].bitcast(BF16)
        for dc in range(8):
            op("pe", lambda e, b=b, dc=dc, rows=rows, psb=psb: e.transpose(psb[:, dc * 128:dc * 128 + rows], XB[b][0:rows, dc * 128:(dc + 1) * 128], identb[0:rows, 0:rows]),
               [dXB[b], dCSTB], [dbk])
        cp(ev_eng(), XT3[:, :, t0:t0 + rows], v3(psb, 8)[:, :, 0:rows], [dbk], [dXT[j]])

    stop('A')
    def load_w(buf, dbuf, col0, ncols):
        w3 = v3(buf, 8)[:, :, 0:ncols]
        dma("pool", w3, w_in[:, col0:col0 + ncols].rearrange("(a p) c -> p a c", p=128), writes=[dbuf])
        return w3

    def proj_slab(w3, dw, t0, n, dst, ddst, eng="act"):
        bk, dbk = P.bank()
        for dc in range(8):
            mm(bk[:, 0:n], w3[:, dc, :], XT3[:, dc, t0:t0 + n], dc == 0, dc == 7, [dw] + xt_deps(t0, n), [dbk])
        cp(eng, dst, bk[:, 0:n], [dbk], [ddst])

    def proj_samples(w3, dw, col0, ncols):
        bk, dbk = P.bank()
        for dc in range(8):
            mm(bk[0:16, 0:ncols], XT3[:, dc, LP:NT], w3[:, dc, :], dc == 0, dc == 7, [dw, dXT[16]], [dbk])
        cp("dve", PRS[0:16, col0:col0 + ncols], bk[0:16, 0:ncols], [dbk], [dPRS])

    def col_to_dram(src_col, dram_row, reads):
        bk, dbk = P.bank()
        mm(bk[0:1, 0:128], src_col, identf, True, True, reads + [dCST], [dbk])
        cp("dve", ROWT[0:1, :], bk[0:1, 0:128], [dbk], [dROWT])
        dma("sp", dram_row, ROWT[0:1, :], reads=[dROWT])

    ROWT = P.alloc(128); dROWT = Dep()

    def bc8(ap):
        return ap.unsqueeze(2).to_broadcast([128, 8, 64])

    mM = P.top
    BC8 = P.alloc(24); dBC8 = Dep()
    dma("sp", BC8[:, 0:8], dtb_r.partition_broadcast(128), writes=[dBC8])
    dma("sp", BC8[:, 8:16], alog_r.partition_broadcast(128), writes=[dBC8])
    dma("sp", BC8[:, 16:24], dsk_r.partition_broadcast(128), writes=[dBC8])
    op("act", lambda e: e.activation(BC8[:, 8:16], BC8[:, 8:16], AF.Exp), [dBC8], [dBC8])
    op("dve", lambda e: e.tensor_scalar(BC8[:, 8:16], BC8[:, 8:16], -1.0, None, ALU.mult), [dBC8], [dBC8])
    GNW = P.alloc(512); dGNW = Dep()
    dma("sp", GNW, gnw_r.partition_broadcast(128), writes=[dGNW])
    HT32 = P.alloc(512); HTB = P.alloc(512, BF16); dHT = Dep()
    op("dve", lambda e: e.memset(HT32, 0.0), [], [dHT])
    op("dve", lambda e: e.memset(HTB, 0.0), [], [dHT])
    DT = P.alloc(128); DT3 = v3(DT, 16); ADT = P.alloc(128); ADT3 = v3(ADT, 16); dDT = Dep()
    WZ = P.alloc(8 * 512, BF16); dWZ = Dep()
    WZ3 = load_w(WZ, dWZ, RPROJ, 512)
    WDT = P.alloc(8 * 8, BF16); dWDT = Dep()
    WDT3 = load_w(WDT, dWDT, PROJ - 8, 8)
    WX = [P.alloc(8 * 128, BF16) for _ in range(3)]; dWX = [Dep(), Dep(), Dep()]
    bk, dbk = P.bank()
    for j, (t0, rows) in enumerate(TILES):
        for dc in range(8):
            mm(bk[0:rows, j * 8:(j + 1) * 8], XT3[:, dc, t0:t0 + rows], WDT3[:, dc, :], dc == 0, dc == 7, [dWDT, dXT[j]], [dbk])
    cp("dve", PRS[0:16, PROJ - 8:PROJ], bk[0:16, 128:136], [dbk], [dPRS])
    op("dve", lambda e: e.tensor_tensor(DT3, v3(bk[:, 0:128], 16), BC8[:, 0:8].unsqueeze(1).to_broadcast([128, 16, 8]), ALU.add), [dbk, dBC8], [dDT])
    op("act", lambda e: e.activation(DT, DT, AF.Exp), [dDT], [dDT])
    op("act", lambda e: e.activation(DT, DT, AF.Ln, bias=EPS[:, 2:3], scale=1.0), [dDT, dFV], [dDT])
    op("dve", lambda e: e.tensor_tensor(ADT3, DT3, BC8[:, 8:16].unsqueeze(1).to_broadcast([128, 16, 8]), ALU.mult), [dDT, dBC8], [dDT])
    stop('M1')
    proj_samples(WZ3, dWZ, RPROJ, 512)
    stop('M1b')

    XBC = P.alloc(8 * 515); XBC3 = v3(XBC, 8); dXBCs = [Dep() for _ in range(8)]
    op("dve", lambda e: e.memset(XBC, 0.0), [], dXBCs)
    ACCs = [P.alloc(512) for _ in range(3)]; dACCs = [Dep() for _ in range(3)]
    MSETS = []
    for _sb in range(2):
        MSETS.append((v3(P.alloc(8 * 512, BF16), 8), Dep(), v3(P.alloc(4 * 512, BF16), 4), Dep(), v3(P.alloc(4 * 256, BF16), 4), Dep(), v3(P.alloc(4 * 512, BF16), 4), Dep()))
    NCV = P.alloc(1024); dNCV = Dep()
    CS = P.alloc(64); dCS = Dep()
    ADTB = P.alloc(8 * 128); ADTB3 = v3(ADTB, 8); dADTB = Dep()
    XBF = P.alloc(512, BF16); XHAT = P.alloc(512, BF16); dXB2 = Dep()
    CBM = P.alloc(256); CBM3 = v3(CBM, 2); dCBM = Dep()
    LSB = P.alloc(1024); LSB3 = v3(LSB, 8); dLSB = Dep()
    GB = P.alloc(1024, BF16); GB3 = v3(GB, 8); dGB = Dep()
    T1 = P.alloc(512); T2 = P.alloc(512); dT1 = Dep(); dT2 = Dep()
    SS = P.alloc(8); dSS = Dep()
    OMB = P.alloc(512, BF16); dOMB = Dep()

    for sl_i, (s0, sn) in enumerate(SLABS[:4]):
        XSA3, dXSA, XSTOK3, dXST, BTOK3, dBT, ZS3, dZS = MSETS[sl_i % 2]
        for jj in range(4):
            j = sl_i * 4 + jj
            t0 = j * 128
            bk, dbk = P.bank()
            for dc in range(8):
                mm(bk[:, :], XT3[:, dc, t0:t0 + 128], WZ3[:, dc, :], dc == 0, dc == 7, [dWZ, dXT[j]], [dbk])
            op("act", lambda e: e.activation(ZS3[:, jj, :], bk[:, :], AF.Silu), [dbk], [dZS])
        stop('M2')
        for fc in range(8):
            wi = (sl_i * 8 + fc) % 3
            dXBC = dXBCs[fc]; ACC = ACCs[wi]; dACC = dACCs[wi]
            w3 = load_w(WX[wi], dWX[wi], RPROJ + 512 + fc * 128, 128)
            if sl_i == 0:
                proj_samples(w3, dWX[wi], RPROJ + 512 + fc * 128, 128)
            else:
                op("dve", lambda e: e.tensor_copy(XBC3[:, fc, 0:3], XBC3[:, fc, 512:515]), [dXBC], [dXBC])
            cw = lambda i: FV[:, 38 + fc * 4 + i:39 + fc * 4 + i]
            bk, dbk = P.bank()
            for dc in range(8):
                mm(bk[:, 0:512], w3[:, dc, :], XT3[:, dc, s0:s0 + 512], dc == 0, dc == 7, [dWX[wi]] + xt_deps(s0, 512), [dbk])
            op("act", lambda e: e.activation(ACC, bk[:, 0:512], AF.Identity, bias=FV[:, 70 + fc:71 + fc], scale=cw(3)), [dbk, dFV], [dACC])
            cp("act", XBC3[:, fc, 3:515], bk[:, 0:512], [dbk], [dXBC])
            for i in (0, 1, 2):
                op("dve", lambda e: e.scalar_tensor_tensor(ACC, XBC3[:, fc, i:i + 512], cw(i), ACC, ALU.mult, ALU.add), [dXBC, dFV, dACC], [dACC])
            op("act", lambda e: e.activation(XSA3[:, fc, :], ACC, AF.Silu), [dACC], [dXSA])
            if sl_i == 3:
                bk, dbk = P.bank()
                mm(bk[0:3, 0:128], XBC3[:, fc, 512:515], identf, True, True, [dXBC, dCST], [dbk])
                cp("dve", NCV[0:3, fc * 128:(fc + 1) * 128], bk[0:3, 0:128], [dbk], [dNCV])
        stop('M3')
        for jj in range(4):
            bk, dbk = P.bank()
            psb = bk[:].bitcast(BF16)
            for fc in range(6):
                op("pe", lambda e: e.transpose(psb[:, fc * 128:(fc + 1) * 128], XSA3[:, fc, jj * 128:(jj + 1) * 128], identb), [dXSA, dCSTB], [dbk])
                stop('T%d' % fc)
            cp("act", XSTOK3[:, jj, :], psb[:, 0:512], [dbk], [dXST])
            stop('T6')
            cp("act", BTOK3[:, jj, :], psb[:, 512:768], [dbk], [dBT])
            stop('T7')
        stop('M4')
        for jj in range(4):
            j = sl_i * 4 + jj
            tsl = slice(jj * 128, (jj + 1) * 128)
            gsl = slice(j * 128, (j + 1) * 128)
            bk, dbk = P.bank()
            mm(bk[:, 0:8], mUI, ADT3[:, j, :], True, True, [dCST, dDT], [dbk])
            mm(bk[:, 8:16], ones, ADT3[:, j, :], True, True, [dCST, dDT], [dbk])
            op("act", lambda e: e.activation(CS[:, 0:16], bk[:, 0:16], AF.Exp), [dbk], [dCS])
            op("dve", lambda e: e.tensor_copy(CS[:, 32:48], bk[:, 0:16]), [dbk], [dCS])
            op("dve", lambda e: e.tensor_tensor(CS[:, 24:32], CS[:, 40:48], CS[:, 32:40], ALU.subtract), [dCS], [dCS])
            op("act", lambda e: e.activation(CS[:, 16:24], CS[:, 24:32], AF.Exp), [dCS], [dCS])
            op("act", lambda e: e.copy(ADTB3, ADT3[:, j, :].unsqueeze(2).to_broadcast([128, 8, 128])), [dDT], [dADTB])
            op("dve", lambda e: e.tensor_tensor(v3(XBF, 8), v3(XSTOK3[:, jj, :], 8), bc8(DT3[:, j, :]), ALU.mult), [dXST, dDT], [dXB2])
            op("dve", lambda e: e.tensor_tensor(v3(XHAT, 8), v3(XBF, 8), bc8(CS[:, 16:24]), ALU.mult), [dXB2, dCS], [dXB2])
            stop('M5')
            bkc, dbkc = P.bank()
            for g in range(2):
                mm(bkc[:, g * 128:(g + 1) * 128], XSA3[:, 4 + g, tsl], XSA3[:, 6 + g, tsl], True, True, [dXSA], [dbkc])
            op("dve", lambda e: e.tensor_tensor(CBM3, v3(bkc[:, 0:256], 2), mUI.unsqueeze(1).to_broadcast([128, 2, 128]), ALU.mult), [dbkc, dCST], [dCBM])
            for g in range(2):
                bkd, dbkd = P.bank()
                for hh in range(4):
                    h = g * 4 + hh
                    o = bkd[:, hh * 128:(hh + 1) * 128]
                    mm(o, ADTB3[:, h, :], mUI, True, False, [dADTB, dCST], [dbkd])
                    mm(o, NUI, ADTB3[:, h, :], False, False, [dADTB, dCST], [dbkd])
                    mm(o, identf, negm, False, True, [dCST], [dbkd])
                op("act", lambda e: e.activation(LSB[:, g * 512:(g + 1) * 512], bkd[:, :], AF.Exp), [dbkd], [dLSB])
                op("dve", lambda e: e.tensor_tensor(GB3[:, g * 4:(g + 1) * 4, :], LSB3[:, g * 4:(g + 1) * 4, :], CBM3[:, g, :].unsqueeze(1).to_broadcast([128, 4, 128]), ALU.mult), [dLSB, dCBM], [dGB])
            stop('M6')
            bky, dbky = P.bank()
            for h in range(8):
                mm(bky[:, h * 64:(h + 1) * 64], GB3[:, h, :], XBF[:, h * 64:(h + 1) * 64], True, True, [dGB, dXB2], [dbky])
            bko, dbko = P.bank()
            for g in range(2):
                mm(bko[:, g * 256:(g + 1) * 256], XSA3[:, 6 + g, tsl], HTB[:, g * 256:(g + 1) * 256], True, True, [dXSA, dHT], [dbko])
            op("dve", lambda e: e.tensor_tensor(v3(T1, 8), v3(bko[:, :], 8), bc8(CS[:, 0:8]), ALU.mult), [dbko, dCS], [dT1])
            op("dve", lambda e: e.tensor_tensor(T1, T1, bky[:, :], ALU.add), [dbky, dT1], [dT1])
            op("pool", lambda e: e.tensor_tensor(v3(T2, 8), v3(XSTOK3[:, jj, :], 8), bc8(BC8[:, 16:24]), ALU.mult), [dXST, dBC8], [dT2])
            op("dve", lambda e: e.tensor_tensor(T1, T1, T2, ALU.add), [dT1, dT2], [dT1])
            op("dve", lambda e: e.tensor_tensor(T1, T1, ZS3[:, jj, :], ALU.mult), [dT1, dZS], [dT1])
            for g in range(2):
                op("act", lambda e: e.activation(T2[:, g * 256:(g + 1) * 256], T1[:, g * 256:(g + 1) * 256], AF.Square, accum_out=SS[:, g:g + 1]), [dT1, dT2], [dT2, dSS])
            op("act", lambda e: e.activation(SS[:, 2:4], SS[:, 0:2], AF.Ln, bias=EPS[:, 1:2], scale=1.0 / 256.0), [dSS, dFV], [dSS])
            op("act", lambda e: e.activation(SS[:, 4:6], SS[:, 2:4], AF.Exp, scale=-0.5), [dSS], [dSS])
            op("dve", lambda e: e.tensor_tensor(v3(T1, 2), v3(T1, 2), SS[:, 4:6].unsqueeze(2).to_broadcast([128, 2, 256]), ALU.mult), [dT1, dSS], [dT1])
            op("dve", lambda e: e.tensor_tensor(OMB, T1, GNW, ALU.mult), [dT1, dGNW], [dOMB])
            stop('M7')
            bkt, dbkt = P.bank()
            psb = bkt[:].bitcast(BF16)
            for fc in range(4):
                op("pe", lambda e: e.transpose(psb[:, fc * 128:(fc + 1) * 128], OMB[:, fc * 128:(fc + 1) * 128], identb), [dOMB, dCSTB], [dbkt])
            cp("act", MIXT3[:, 4:8, gsl], v3(psb[:, 0:512], 4), [dbkt], [dMIX])
            bkh, dbkh = P.bank()
            for g in range(2):
                mm(bkh[:, g * 256:(g + 1) * 256], BTOK3[:, jj, g * 128:(g + 1) * 128], XHAT[:, g * 256:(g + 1) * 256], True, True, [dBT, dXB2], [dbkh])
            op("dve", lambda e: e.tensor_tensor(v3(HT32, 8), v3(HT32, 8), bc8(CS[:, 8:16]), ALU.mult), [dHT, dCS], [dHT])
            op("dve", lambda e: e.tensor_tensor(HT32, HT32, bkh[:, :], ALU.add), [dHT, dbkh], [dHT])
            cp("act", HTB, HT32, [dHT], [dHT])
    P.held = set()
    dma("sp", ncv_p, NCV[0:3, :], reads=[dNCV])
    for blk in range(4):
        bk, dbk = P.bank()
        mm(bk[:, 0:128], HT32[:, blk * 128:(blk + 1) * 128], identf, True, True, [dHT, dCST], [dbk])
        cp("act", T1[:, (blk % 4) * 128:(blk % 4 + 1) * 128], bk[:, 0:128], [dbk], [dT1])
    dma("sp", nssm_p.rearrange("(b p) n -> p b n", p=128), v3(T1, 4), reads=[dT1])
    P.barrier()
    P.top = mM
    stop('M')
    SHT = P.alloc(32); dSHT = Dep()
    STSH = P.alloc(256); dSTSH = Dep()
    dma("sp", STSH[0:16, :], st_shift[:, 1536:1792], writes=[dSTSH])
    for i in range(2):
        bk, dbk = P.bank()
        mm(bk[:, 0:16], STSH[0:16, i * 128:(i + 1) * 128], identf[0:16, 0:16], True, True, [dSTSH, dCST], [dbk])
        cp("dve", SHT[:, i * 16:(i + 1) * 16], bk[:, 0:16], [dbk], [dSHT])
    WR = [P.alloc(8 * 128, BF16) for _ in range(3)]; dWR = [Dep(), Dep(), Dep()]
    PT = P.alloc(3 * 513); PT3 = v3(PT, 3); dPT = Dep()
    UR = P.alloc(512); UK = P.alloc(512); SW = P.alloc(512); AA = P.alloc(512); CL = P.alloc(512); EE = P.alloc(512); KKN = P.alloc(512)
    EX1 = XB[0].bitcast(F32); EX2 = XB[1].bitcast(F32); dEX1 = dXB[0]; dEX2 = dXB[1]
    dUR = Dep(); dUK = Dep(); dSW = Dep(); dAA = Dep(); dCL = Dep(); dEE = Dep(); dKKN = Dep()
    for fc in (12, 13):
        w3 = load_w(WR[0], dWR[0], fc * 128, 128)
        proj_samples(w3, dWR[0], fc * 128, 128)
        op("dve", lambda e: e.memset(PT3[:, 0, 0:1], 0.0), [], [dPT])
        dst = LUW if fc == 12 else LUG
        for si, (s0, sn) in enumerate(SLABS):
            if si > 0 and si < 4:
                op("dve", lambda e: e.tensor_copy(PT3[:, 0, 0:1], PT3[:, 0, 512:513]), [dPT], [dPT])
            proj_slab(w3, dWR[0], s0, sn, PT3[:, 0, 1:1 + sn], dPT)
            if si == 3:
                col_to_dram(PT3[:, 0, 512:513], nsh_p[0:1, fc * 128:(fc + 1) * 128], [dPT])
            prev = PT3[:, 0, 0:sn] if si < 4 else SHT[:, (fc - 12) * 16:(fc - 11) * 16]
            op("dve", lambda e: e.tensor_tensor(CL[:, 0:sn], prev, PT3[:, 0, 1:1 + sn], ALU.subtract), [dPT, dSHT], [dCL])
            op("dve", lambda e: e.scalar_tensor_tensor(UR[:, 0:sn], CL[:, 0:sn], FV[:, fc:fc + 1], PT3[:, 0, 1:1 + sn], ALU.mult, ALU.add), [dCL, dPT, dFV], [dUR])
            if fc == 12:
                op("act", lambda e: e.activation(LUW[0:64, s0:s0 + sn], UR[0:64, 0:sn], AF.Tanh), [dUR], [dLUW[si]])
                cp("act", LUW[64:128, s0:s0 + sn], UR[64:128, 0:sn], [dUR], [dLUW[si]])
            else:
                op("act", lambda e: e.activation(UR[:, 0:sn], UR[:, 0:sn], AF.Exp, scale=-1.0), [dUR], [dUR])
                op("dve", lambda e: e.tensor_scalar(UR[:, 0:sn], UR[:, 0:sn], 1.0, None, ALU.add), [dUR], [dUR])
                op("dve", lambda e: e.reciprocal(UR[:, 0:sn], UR[:, 0:sn]), [dUR], [dUR])
                cp("act", LUG[:, s0:s0 + sn], UR[:, 0:sn], [dUR], [dLUG[si]])

    stop('R1')
    P.held = {0, 1, 2, 5, 6, 7}
    POOL_Y = [0]
    POOL_B = [1, 2]
    POOL_I = [5, 6, 7]
    SETS = []
    for _sb in range(2):
        st_ = {}
        st_["AZ"] = P.alloc(8 * 128, BF16); st_["AR"] = P.alloc(8 * 128, BF16); st_["TTA"] = P.alloc(8 * 128, BF16)
        st_["AMG"] = P.alloc(8 * 256, BF16); st_["AKZ"] = P.alloc(8 * 256, BF16); st_["ZTG"] = P.alloc(8 * 256, BF16); st_["UVZ"] = P.alloc(8 * 256, BF16)
        st_["VBF"] = P.alloc(512, BF16); st_["RKP"] = P.alloc(512, BF16); st_["PC"] = P.alloc(8)
        for nm in ("dAZ", "dAR", "dTTA", "dAMG", "dAKZ", "dZTG", "dUVZ", "dVB", "dRKP", "dPC"):
            st_[nm] = Dep(nm)
        for nm, dn in (("AZ", "dAZ"), ("AKZ", "dAKZ"), ("ZTG", "dZTG"), ("UVZ", "dUVZ")):
            op("pool", lambda e: e.memset(st_[nm], 0.0), [], [st_[dn]])
        SETS.append(st_)
    BZ = P.alloc(8 * 128, BF16); BZ3 = v3(BZ, 8)
    BK = P.alloc(8 * 128, BF16); BK3 = v3(BK, 8); BK4 = v4(BK, 8, 2)
    BKH = P.alloc(8 * 128, BF16); BKH3 = v3(BKH, 8)
    dBK = Dep()
    op("pool", lambda e: e.memset(BZ, 0.0), [], [dBK])
    SQB = P.alloc(512, BF16); dSQB = Dep()
    Y32 = P.alloc(512); YSQ = P.alloc(512); YC = P.alloc(512); M2 = P.alloc(512)
    dY32 = Dep(); dYSQ = Dep(); dYC = Dep(); dM2 = Dep()
    RST = P.alloc(512); dRST = Dep()
    op("pool", lambda e: e.memset(RST, 1.0), [], [dRST])
    op("pool", lambda e: e.memset(v3(RST, 8)[:, :, 0:1], 0.0), [], [dRST])
    S32 = P.alloc(64); SBF = P.alloc(64, BF16); SZ = [P.alloc(128, BF16) for _ in range(2)]
    dS = Dep(); dSB = Dep(); dSZ = [Dep(), Dep()]
    WSB = P.alloc(64, BF16); dWSB = Dep()
    MBs = [[P.alloc(512, BF16) for _ in range(2)] for _ in range(2)]; NBs = [[P.alloc(512, BF16) for _ in range(2)] for _ in range(2)]; PBs = [[P.alloc(512, BF16) for _ in range(2)] for _ in range(2)]
    dMBs = [[Dep(), Dep()] for _ in range(2)]; dNBs = [[Dep(), Dep()] for _ in range(2)]; dPBs = [[Dep(), Dep()] for _ in range(2)]
    SOUT = P.alloc(128); dSO = Dep()

    for hp in range(4):
        w3s = []
        for kind in range(3):
            fc = kind * 4 + hp
            w3 = load_w(WR[kind], dWR[kind], fc * 128, 128)
            proj_samples(w3, dWR[kind], fc * 128, 128)
            w3s.append(w3)
        op("dve", lambda e: e.memset(PT3[:, :, 0:1], 0.0), [], [dPT])
        op("dve", lambda e: e.memset(S32, 0.0), [], [dS])
        op("dve", lambda e: e.memset(SBF, 0.0), [], [dSB])
        op("dve", lambda e: e.memset(SZ[0], 0.0), [], [dSZ[0]])
        op("dve", lambda e: e.memset(SZ[1], 0.0), [], [dSZ[1]])
        szi = 0
        for g8, (s0, sn) in enumerate(SLABS[:4]):
            tsl = slice(s0, s0 + 512)
            st_ = SETS[(hp * 4 + g8) % 2]
            AZ = st_["AZ"]; AZ3 = v3(AZ, 8); AR = st_["AR"]; AR3 = v3(AR, 8); AR4 = v4(AR, 8, 2); TTA = st_["TTA"]; TTA3 = v3(TTA, 8)
            AMG = st_["AMG"]; AMG4 = v4(AMG, 8, 2); AKZ4 = v4(st_["AKZ"], 8, 2); ZTG4 = v4(st_["ZTG"], 8, 2); UVZ4 = v4(st_["UVZ"], 8, 2)
            VBF = st_["VBF"]; RKP = st_["RKP"]; PC = st_["PC"]
            dAZ = st_["dAZ"]; dAR = st_["dAR"]; dTTA = st_["dTTA"]; dAMG = st_["dAMG"]; dAKZ = st_["dAKZ"]; dZTG = st_["dZTG"]; dUVZ = st_["dUVZ"]
            dVB = st_["dVB"]; dRKP = st_["dRKP"]; dPC = st_["dPC"]
            if g8 > 0:
                op("dve", lambda e: e.tensor_copy(PT3[:, :, 0:1], PT3[:, :, 512:513]), [dPT], [dPT])
            for kind, dst, dd, xs_, dxs in ((0, UR, dUR, UR, dUR), (1, UK, dUK, UK, dUK), (2, VBF, dVB, CL, dCL)):
                fc = kind * 4 + hp
                bk, dbk = P.bank()
                for dc in range(8):
                    mm(bk[:, 0:512], w3s[kind][:, dc, :], XT3[:, dc, s0:s0 + 512], dc == 0, dc == 7, [dWR[kind]] + xt_deps(s0, 512), [dbk])
                cp("act", PT3[:, kind, 1:513], bk[:, 0:512], [dbk], [dPT])
                op("act", lambda e: e.activation(xs_, bk[:, 0:512], AF.Identity, scale=OMM[:, fc:fc + 1]), [dbk, dFV], [dxs])
                if g8 == 3:
                    col_to_dram(PT3[:, kind, 512:513], nsh_p[0:1, fc * 128:(fc + 1) * 128], [dPT])
                op("dve", lambda e: e.scalar_tensor_tensor(dst, PT3[:, kind, 0:512], FV[:, fc:fc + 1], xs_, ALU.mult, ALU.add), [dPT, dxs, dFV], [dd])
            bk, dbk = P.bank()
            mm(bk[:, :], WUA[0:64, hp * 128:(hp + 1) * 128], LUW[0:64, tsl], True, True, [dLW, dLUW[g8]], [dbk])
            op("act", lambda e: e.activation(SW, bk[:, :], AF.Exp, bias=HB0[:, hp:hp + 1], scale=-1.0), [dbk, dFV], [dSW])
            op("act", lambda e: e.activation(SW, SW, AF.Identity, bias=EPS[:, 2:3], scale=1.0), [dSW, dFV], [dSW])
            op("dve", lambda e: e.reciprocal(SW, SW), [dSW], [dSW])
            bk2, dbk2 = P.bank()
            mm(bk2[:, :], WUA[64:128, hp * 128:(hp + 1) * 128], LUW[64:128, tsl], True, True, [dLW, dLUW[g8]], [dbk2])
            op("act", lambda e: e.activation(AA, bk2[:, :], AF.Exp, bias=HB0[:, 4 + hp:5 + hp], scale=-1.0), [dbk2, dFV], [dAA])
            op("act", lambda e: e.activation(AA, AA, AF.Identity, bias=EPS[:, 2:3], scale=1.0), [dAA, dFV], [dAA])
            op("dve", lambda e: e.reciprocal(AA, AA), [dAA], [dAA])
            op("act", lambda e: e.activation(KKN, UK, AF.Identity, scale=FV[:, 22 + hp:23 + hp]), [dUK, dFV], [dKKN])
            op("act", lambda e: e.activation(SQB, KKN, AF.Square), [dKKN], [dSQB])
            bk, dbk = P.bank()
            mm(bk[:, :], bonesb, SQB, True, True, [dCSTB, dSQB], [dbk])
            op("dve", lambda e: e.tensor_scalar(EE, bk[:, :], 1e-24, None, ALU.max), [dbk], [dEE])
            op("act", lambda e: e.activation(EE, EE, AF.Ln), [dEE], [dEE])
            op("act", lambda e: e.activation(EE, EE, AF.Exp, scale=-0.5), [dEE], [dEE])
            op("dve", lambda e: e.tensor_tensor(KKN, KKN, EE, ALU.mult), [dKKN, dEE], [dKKN])
            op("act", lambda e: e.activation(EE, AA, AF.Identity, bias=OMK[:, hp:hp + 1], scale=FV[:, 26 + hp:27 + hp]), [dAA, dFV], [dEE])
            op("dve", lambda e: e.tensor_tensor(UK, UK, EE, ALU.mult), [dUK, dEE], [dUK])
            op("dve", lambda e: e.tensor_tensor(RKP, UR, UK, ALU.mult), [dUR, dUK], [dRKP])
            op("dve", lambda e: e.tensor_tensor_scan(CL, RST, SW, 0.0, ALU.mult, ALU.add), [dRST, dSW], [dCL])
            op("dve", lambda e: e.tensor_tensor(SW, CL, SW, ALU.subtract), [dCL, dSW], [dSW])
            op("act", lambda e: e.activation(EE, SW, AF.Exp, scale=-C0), [dSW], [dEE])
            op("dve", lambda e: e.scalar_tensor_tensor(AR4[:, :, 0, :], v3(KKN, 8), -1.0, v3(EE, 8), ALU.mult, ALU.mult), [dKKN, dEE], [dAR])
            op("act", lambda e: e.activation(EX1, CL, AF.Exp, scale=-C0), [dCL], [dEX1])
            op("dve", lambda e: e.tensor_tensor(AR4[:, :, 1, :], v3(UR, 8), v3(EX1, 8), ALU.mult), [dUR, dEX1], [dAR])
            op("dve", lambda e: e.tensor_copy(PC, v3(EX1, 8)[:, :, 63]), [dEX1], [dPC])
            op("dve", lambda e: e.tensor_tensor(KKN, KKN, AA, ALU.mult), [dKKN, dAA], [dKKN])
            op("act", lambda e: e.activation(EX2, CL, AF.Exp, scale=C0), [dCL], [dEX2])
            op("dve", lambda e: e.tensor_tensor(BK4[:, :, 0, :], v3(KKN, 8), v3(EX2, 8), ALU.mult), [dKKN, dEX2], [dBK])
            op("dve", lambda e: e.tensor_tensor(BK4[:, :, 1, :], v3(UK, 8), v3(EX2, 8), ALU.mult), [dUK, dEX2], [dBK])
            op("dve", lambda e: e.tensor_tensor(BKH3, BK3, PC.unsqueeze(2).to_broadcast([128, 8, 128]), ALU.mult), [dBK, dPC], [dBK])
            for hh in range(2):
                ps_ = slice(hh * 64, (hh + 1) * 64)
                cp("act", AZ3[ps_, :, hh * 64:(hh + 1) * 64], AR4[ps_, :, 0, :], [dAR], [dAZ])
                cp("act", BZ3[ps_, :, hh * 64:(hh + 1) * 64], BK4[ps_, :, 0, :], [dBK], [dBK])
            stop('R2')
            for gq in range(2):
                MB = MBs[gq]; NB = NBs[gq]; PB = PBs[gq]; dMB = dMBs[gq]; dNB = dNBs[gq]; dPB = dPBs[gq]
                bm, dbm = P.bank(pool=POOL_I); bn, dbn = P.bank(pool=POOL_I)
                for i in range(4):
                    c = gq * 4 + i
                    mm(bm[:, i * 128:(i + 1) * 128], BZ3[:, c, :], AZ3[:, c, :], True, True, [dBK, dAZ], [dbm])
                    mm(bn[:, i * 128:(i + 1) * 128], AZ3[:, c, :], BZ3[:, c, :], True, True, [dBK, dAZ], [dbn])
                cur = 0
                op("dve", lambda e: e.tensor_tensor(v3(MB[0], 4), v3(bm[:, :], 4), mSU.unsqueeze(1).to_broadcast([128, 4, 128]), ALU.mult), [dbm, dCST], [dMB[0]])
                op("dve", lambda e: e.tensor_tensor(v3(NB[0], 4), v3(bn[:, :], 4), mSL.unsqueeze(1).to_broadcast([128, 4, 128]), ALU.mult), [dbn, dCST], [dNB[0]])
                op("dve", lambda e: e.tensor_tensor(v3(PB[0], 4), v3(MB[0], 4), identb.unsqueeze(1).to_broadcast([128, 4, 128]), ALU.add), [dMB[0], dCSTB], [dPB[0]])
                for lvl in range(1, 6):
                    nx = 1 - cur
                    if lvl <= 4:
                        bm, dbm = P.bank(pool=POOL_I)
                        for i in range(4):
                            sl = slice(i * 128, (i + 1) * 128)
                            mm(bm[:, sl], NB[cur][:, sl], MB[cur][:, sl], True, True, [dNB[cur], dMB[cur]], [dbm])
                    bn, dbn = P.bank(pool=POOL_I)
                    for i in range(4):
                        sl = slice(i * 128, (i + 1) * 128)
                        mm(bn[:, sl], MB[cur][:, sl], NB[cur][:, sl], True, True, [dNB[cur], dMB[cur]], [dbn])
                    if lvl <= 4:
                        cp("act", MB[nx], bm[:, :], [dbm], [dMB[nx]])
                    cp("dve", NB[nx], bn[:, :], [dbn], [dNB[nx]])
                    bp, dbp = P.bank(pool=POOL_I)
                    for i in range(4):
                        sl = slice(i * 128, (i + 1) * 128)
                        mm(bp[:, sl], NB[nx][:, sl], PB[cur][:, sl], True, False, [dNB[nx], dPB[cur]], [dbp])
                        mm(bp[:, sl], identb, PB[cur][:, sl], False, True, [dCSTB, dPB[cur]], [dbp])
                    if lvl < 5:
                        cp("act", PB[nx], bp[:, :], [dbp], [dPB[nx]])
                    else:
                        cp("act", TTA[:, gq * 512:(gq + 1) * 512], bp[:, :], [dbp], [dTTA])
                    cur = nx
            stop('R3')
            for half in range(2):
                bks = [P.bank(pool=POOL_I), P.bank(pool=POOL_I)]
                for ci4 in range(4):
                    ci = half * 4 + ci4
                    for hh in range(2):
                        ps_ = slice(hh * 64, (hh + 1) * 64)
                        bk, dbk = bks[hh]
                        mm(bk[:, ci4 * 128:(ci4 + 1) * 128], BK3[ps_, ci, :], AR3[ps_, ci, :], True, True, [dBK, dAR], [dbk])
                for hh in range(2):
                    bk, dbk = bks[hh]
                    op("dve", lambda e: e.tensor_tensor(AMG4[:, half * 4:half * 4 + 4, hh, :], v3(bk[:, :], 4), mask4.unsqueeze(1).to_broadcast([128, 4, 128]), ALU.mult), [dbk, dCST], [dAMG])
                    op("dve", lambda e: e.tensor_tensor(AKZ4[64:128, half * 4:half * 4 + 4, hh, hh * 64:(hh + 1) * 64], v3(bk[64:128, :], 4)[:, :, 0:64], mask4[64:128, 0:64].unsqueeze(1).to_broadcast([64, 4, 64]), ALU.mult), [dbk, dCST], [dAKZ])
            stop('R3b')
            bk, dbk = P.bank(pool=POOL_I)
            psb = bk[:].bitcast(BF16)
            for ci in range(8):
                op("pe", lambda e: e.transpose(psb[:, ci * 128:(ci + 1) * 128], BKH3[:, ci, :], identb), [dBK, dCSTB], [dbk])
            for hh in range(2):
                cp("act" if hh else "dve", ZTG4[:, :, hh, hh * 64:(hh + 1) * 64], v3(psb, 8)[:, :, hh * 64:(hh + 1) * 64], [dbk], [dZTG])
            stop('R3c')
            bk, dbk = P.bank(pool=POOL_I)
            psb = bk[:].bitcast(BF16)
            for ti in range(4):
                op("pe", lambda e: e.transpose(psb[:, ti * 128:(ti + 1) * 128], VBF[:, ti * 128:(ti + 1) * 128], identb), [dVB, dCSTB], [dbk])
            UVZ5 = st_["UVZ"].rearrange("p (t q h c) -> p t q h c", t=4, q=2, h=2)
            for par in range(2):
                for hh in range(2):
                    cp("act" if hh else "dve", UVZ5[64:128, :, par, hh, hh * 64:(hh + 1) * 64], v3(psb[par * 64:(par + 1) * 64, 0:512], 4)[:, :, hh * 64:(hh + 1) * 64], [dbk], [dUVZ])
            stop('R4')
            bky, dbky = P.bank(pool=POOL_Y)
            for ci in range(8):
                bw, dbw = P.bank(pool=POOL_B)
                mm(bw[:, 0:64], AZ3[:, ci, :], SBF, True, False, [dAZ, dSB], [dbw])
                for hh in range(2):
                    mm(bw[:, 0:64], AKZ4[64:128, ci, hh, :], UVZ4[64:128, ci, hh, hh * 64:(hh + 1) * 64], False, hh == 1, [dAKZ, dUVZ], [dbw])
                cp("act", WSB, bw[:, 0:64], [dbw], [dWSB])
                bu, dbu = P.bank(pool=POOL_B)
                mm(bu[:, 0:64], TTA3[:, ci, :], WSB, True, True, [dTTA, dWSB], [dbu])
                cp("dve", UVZ4[0:64, ci, 0, 0:64], bu[0:64, 0:64], [dbu], [dUVZ])
                cp("act", UVZ4[0:64, ci, 1, 64:128], bu[64:128, 0:64], [dbu], [dUVZ])
                yo = bky[:, ci * 64:(ci + 1) * 64]
                mm(yo, SZ[szi], AR4[:, ci, 1, :], True, False, [dSZ[szi], dAR], [dbky])
                for hh in range(2):
                    mm(yo, UVZ4[:, ci, hh, :], AMG4[:, ci, hh, 64:128], False, hh == 1, [dUVZ, dAMG], [dbky])
                bs, dbs = P.bank(pool=POOL_B)
                for hh in range(2):
                    mm(bs[:, 0:64], ZTG4[:, ci, hh, :], UVZ4[:, ci, hh, hh * 64:(hh + 1) * 64], hh == 0, hh == 1, [dZTG, dUVZ], [dbs])
                op("dve", lambda e: e.scalar_tensor_tensor(SBF, S32, PC[:, ci:ci + 1], bs[:, 0:64], ALU.mult, ALU.add), [dS, dPC, dbs], [dSB])
                op("dve", lambda e: e.scalar_tensor_tensor(S32, S32, PC[:, ci:ci + 1], bs[:, 0:64], ALU.mult, ALU.add), [dS, dPC, dbs], [dS])
                szi = 1 - szi
                for hh in range(2):
                    ps_ = slice(hh * 64, (hh + 1) * 64)
                    cp("act", SZ[szi][ps_, hh * 64:(hh + 1) * 64], S32[ps_, :], [dS], [dSZ[szi]])
            stop('R5')
            cp("act", Y32, bky[:, :], [dbky], [dY32])
            op("dve", lambda e: e.tensor_tensor(YSQ, Y32, Y32, ALU.mult), [dY32], [dYSQ])
            stop('Ea')
            b1, db1 = P.bank(pool=POOL_B); b2, db2 = P.bank(pool=POOL_B)
            mm(b1[:, :], bones, Y32, True, True, [dCST, dY32], [db1])
            mm(b2[:, :], bones, YSQ, True, True, [dCST, dYSQ], [db2])
            stop('Ea1')
            op("dve", lambda e: e.scalar_tensor_tensor(YC, b1[:, :], -1.0 / 64.0, Y32, ALU.mult, ALU.add), [db1, dY32], [dYC])
            stop('Ea2')
            op("act", lambda e: e.activation(M2, b1[:, :], AF.Square, scale=1.0 / 64.0), [db1], [dM2])
            stop('Ea3')
            op("dve", lambda e: e.scalar_tensor_tensor(M2, b2[:, :], 1.0 / 64.0, M2, ALU.mult, ALU.subtract), [db2, dM2], [dM2])
            stop('Ea4')
            op("act", lambda e: e.activation(M2, M2, AF.Ln, bias=EPS[:, 0:1], scale=1.0), [dM2, dFV], [dM2])
            stop('Ea5')
            op("act", lambda e: e.activation(M2, M2, AF.Exp, scale=-0.5), [dM2], [dM2])
            stop('Ea6')
            op("dve", lambda e: e.tensor_tensor(YC, YC, M2, ALU.mult), [dYC, dM2], [dYC])
            stop('Ea7')
            op("act", lambda e: e.activation(YC, YC, AF.Identity, bias=FV[:, 34 + hp:35 + hp], scale=FV[:, 30 + hp:31 + hp]), [dYC, dFV], [dYC])
            stop('Ea8')
            stop('Eb')
            b3, db3 = P.bank(pool=POOL_B)
            mm(b3[:, :], RKB[:, hp * 128:(hp + 1) * 128], RKP, True, True, [dFV, dRKP], [db3])
            op("dve", lambda e: e.tensor_tensor(M2, b3[:, :], VBF, ALU.mult), [db3, dVB, dM2], [dM2])
            op("dve", lambda e: e.tensor_tensor(YC, YC, M2, ALU.add), [dYC, dM2], [dYC])
            b4, db4 = P.bank(pool=POOL_B)
            mm(b4[:, :], GUP[:, hp * 128:(hp + 1) * 128], LUG[:, tsl], True, True, [dLW, dLUG[g8]], [db4])
            op("dve", lambda e: e.tensor_tensor(MIXT3[:, hp, tsl], YC, b4[:, :], ALU.mult), [db4, dYC], [dMIX])
            stop('E%d' % g8)
        stop('R6')
        bk, dbk = P.bank(pool=POOL_B)
        mm(bk[0:64, 0:128], S32, identf, True, True, [dS, dCST], [dbk])
        cp("dve", SOUT[0:64, :], bk[0:64, 0:128], [dbk], [dSO])
        dma("sp", nwkv_p.rearrange("(h v) k -> v h k", v=64)[:, 2 * hp:2 * hp + 2, :], v3(SOUT[0:64, :], 2), reads=[dSO])
    dma("sp", nsh_s, PRS[0:16, 0:RPROJ], reads=[dPRS])
    P.barrier()
    P.held = set()
    P.top = mM
    stop('R')
    R16 = slice(0, 16)
    OS = P.alloc(1024); dOS = Dep()
    mS = P.top
    U = P.alloc(RPROJ); dU = Dep(); dLD = Dep()
    VEC = P.alloc(7 * 512); VEC3 = v3(VEC, 7)
    Q = P.alloc(6 * 512); Q3 = v3(Q, 6); dQ = Dep()
    TS = P.alloc(4 * 512); TS3 = v3(TS, 4); dTS = Dep()
    QB = P.alloc(6 * 64); QB3 = v3(QB, 6); dQB = Dep()
    YSB = P.alloc(128); dYSB = Dep()
    YT = P.alloc(512); dYT = Dep()
    mS1 = P.top
    MU = P.alloc(RPROJ); SHP = P.alloc(RPROJ)
    dma("sp", MU[R16, :], mu_r.partition_broadcast(16), writes=[dLD])
    dma("sp", SHP[R16, :], st_shift, writes=[dLD])
    for i, src in enumerate((w0_r, a0_r, kk_r, ka_r, lw_r, lb_r, rk_r)):
        dma("sp", VEC3[R16, i, :], src.partition_broadcast(16), writes=[dLD])
    W0v, A0v, KKv, KAv, LWv, LBv, RKv = [VEC3[R16, i, :] for i in range(7)]
    op("dve", lambda e: e.tensor_tensor(U[R16, :], SHP[R16, :], PRS[R16, 0:RPROJ], ALU.subtract), [dLD, dPRS], [dU])
    op("dve", lambda e: e.tensor_tensor(U[R16, :], U[R16, :], MU[R16, :], ALU.mult), [dU, dLD], [dU])
    op("dve", lambda e: e.tensor_tensor(U[R16, :], U[R16, :], PRS[R16, 0:RPROJ], ALU.add), [dU, dPRS], [dU])
    rS, kS, vS = U[R16, 0:512], U[R16, 512:1024], U[R16, 1024:1536]
    h8 = lambda ap: ap.rearrange("p (h k) -> p h k", h=8)
    b8 = lambda ap: ap.unsqueeze(2).to_broadcast([16, 8, 64])
    bk, dbk = P.bank()
    mm(bk[R16, :], LUW[0:64, LP:NT], WUA[0:64, :], True, True, [dLUW[4], dLW], [dbk])
    op("dve", lambda e: e.tensor_tensor(Q3[R16, 1, :], bk[R16, :], W0v, ALU.add), [dbk, dLD], [dQ])
    op("act", lambda e: e.activation(Q3[R16, 1, :], Q3[R16, 1, :], AF.Sigmoid), [dQ], [dQ])
    op("act", lambda e: e.activation(Q3[R16, 1, :], Q3[R16, 1, :], AF.Exp, scale=-C0), [dQ], [dQ])
    bk, dbk = P.bank()
    mm(bk[R16, :], LUW[64:128, LP:NT], WUA[64:128, :], True, True, [dLUW[4], dLW], [dbk])
    op("dve", lambda e: e.tensor_tensor(TS3[R16, 0, :], bk[R16, :], A0v, ALU.add), [dbk, dLD], [dTS])
    op("act", lambda e: e.activation(TS3[R16, 0, :], TS3[R16, 0, :], AF.Sigmoid), [dTS], [dTS])
    bk, dbk = P.bank()
    mm(bk[R16, :], LUG[:, LP:NT], GUP, True, True, [dLUG[4], dLW], [dbk])
    cp("act", TS3[R16, 3, :], bk[R16, :], [dbk], [dTS])
    aS = TS3[R16, 0, :]; kkS = TS3[R16, 1, :]; tS = TS3[R16, 2, :]; gS = TS3[R16, 3, :]
    op("dve", lambda e: e.tensor_copy(Q3[R16, 0, :], rS), [dU], [dQ])
    op("dve", lambda e: e.tensor_copy(Q3[R16, 3, :], vS), [dU], [dQ])
    op("dve", lambda e: e.tensor_tensor(kkS, kS, KKv, ALU.mult), [dU, dLD], [dTS])
    op("dve", lambda e: e.tensor_tensor(tS, kkS, kkS, ALU.mult), [dTS], [dTS])
    op("dve", lambda e: e.reduce_sum(SM[R16, 0:8], h8(tS), AX.X), [dTS], [dSM])
    op("dve", lambda e: e.tensor_scalar(SM[R16, 0:8], SM[R16, 0:8], 1e-24, None, ALU.max), [dSM], [dSM])
    op("act", lambda e: e.activation(SM[R16, 0:8], SM[R16, 0:8], AF.Sqrt), [dSM], [dSM])
    op("dve", lambda e: e.reciprocal(SM[R16, 0:8], SM[R16, 0:8]), [dSM], [dSM])
    op("dve", lambda e: e.tensor_tensor(h8(kkS), h8(kkS), b8(SM[R16, 0:8]), ALU.mult), [dTS, dSM], [dTS])
    op("dve", lambda e: e.tensor_scalar(Q3[R16, 4, :], kkS, -1.0, None, ALU.mult), [dTS], [dQ])
    op("dve", lambda e: e.tensor_tensor(Q3[R16, 5, :], kkS, aS, ALU.mult), [dTS], [dQ])
    op("dve", lambda e: e.tensor_tensor(tS, aS, KAv, ALU.mult), [dTS, dLD], [dTS])
    op("dve", lambda e: e.tensor_tensor(tS, tS, KAv, ALU.subtract), [dTS, dLD], [dTS])
    op("dve", lambda e: e.tensor_scalar(tS, tS, 1.0, None, ALU.add), [dTS], [dTS])
    op("dve", lambda e: e.tensor_tensor(Q3[R16, 2, :], kS, tS, ALU.mult), [dU, dTS], [dQ])
    op("dve", lambda e: e.tensor_tensor(tS, rS, Q3[R16, 2, :], ALU.mult), [dU, dQ], [dTS])
    op("dve", lambda e: e.tensor_tensor(tS, tS, RKv, ALU.mult), [dTS, dLD], [dTS])
    op("dve", lambda e: e.reduce_sum(SM[R16, 8:16], h8(tS), AX.X), [dTS], [dSM])
    dSCR = Dep()
    for q_ in range(6):
        dma("sp", scrA.rearrange("(b h) (q k) -> b q h k", h=8, k=64)[:, q_], Q3[R16, q_, :].rearrange("p (h k) -> p h k", h=8), reads=[dQ], writes=[dSCR])
    dma("sp", QB, scrA[:, 0:384], reads=[dSCR], writes=[dQB])
    SW_ = P.alloc(4096); TMP = P.alloc(4096); dSt = Dep(); dTmp = Dep()
    dma("sp", SW_, st_wkv, writes=[dSt])
    S3_ = v3(SW_, 64); T3_ = v3(TMP, 64)
    bv = lambda q: QB3[:, q, :].unsqueeze(1).to_broadcast([128, 64, 64])
    bkk = lambda ap: ap.unsqueeze(2).to_broadcast([128, 64, 64])
    op("dve", lambda e: e.tensor_tensor(T3_, S3_, bv(4), ALU.mult), [dSt, dQB], [dTmp])
    op("dve", lambda e: e.reduce_sum(YSB[:, 64:128], T3_, AX.X), [dTmp], [dYSB])
    op("dve", lambda e: e.tensor_tensor(S3_, S3_, bv(1), ALU.mult), [dSt, dQB, dTmp], [dSt])
    op("dve", lambda e: e.tensor_tensor(T3_, bkk(YSB[:, 64:128]), bv(5), ALU.mult), [dYSB, dQB], [dTmp])
    op("dve", lambda e: e.tensor_tensor(S3_, S3_, T3_, ALU.add), [dSt, dTmp], [dSt])
    op("dve", lambda e: e.tensor_tensor(T3_, bkk(QB3[:, 3, :]), bv(2), ALU.mult), [dQB, dSt], [dTmp])
    op("dve", lambda e: e.tensor_tensor(S3_, S3_, T3_, ALU.add), [dSt, dTmp], [dSt])
    dma("sp", nwkv_s, SW_, reads=[dSt])
    op("dve", lambda e: e.tensor_tensor(T3_, S3_, bv(0), ALU.mult), [dSt, dQB], [dTmp])
    op("dve", lambda e: e.reduce_sum(YSB[:, 0:64], T3_, AX.X), [dTmp], [dYSB])
    dSCR2 = Dep()
    dma("sp", scrB, YSB[:, 0:64], reads=[dYSB], writes=[dSCR2])
    dma("sp", YT[R16, :], scrB.rearrange("(b h) v -> b (h v)", h=8), reads=[dSCR2], writes=[dYT])
    yT = YT[R16, :]
    op("dve", lambda e: e.reduce_sum(SM[R16, 16:24], h8(yT), AX.X), [dYT], [dSM])
    op("dve", lambda e: e.tensor_scalar(SM[R16, 16:24], SM[R16, 16:24], 1.0 / 64.0, None, ALU.mult), [dSM], [dSM])
    op("dve", lambda e: e.tensor_tensor(h8(yT), h8(yT), b8(SM[R16, 16:24]), ALU.subtract), [dYT, dSM], [dYT])
    op("dve", lambda e: e.tensor_tensor(tS, yT, yT, ALU.mult), [dYT, dTS], [dTS])
    op("dve", lambda e: e.reduce_sum(SM[R16, 24:32], h8(tS), AX.X), [dTS], [dSM])
    op("dve", lambda e: e.tensor_scalar(SM[R16, 24:32], SM[R16, 24:32], 1.0 / 64.0, 64e-5, ALU.mult, ALU.add), [dSM], [dSM])
    op("act", lambda e: e.activation(SM[R16, 24:32], SM[R16, 24:32], AF.Sqrt), [dSM], [dSM])
    op("dve", lambda e: e.reciprocal(SM[R16, 24:32], SM[R16, 24:32]), [dSM], [dSM])
    op("dve", lambda e: e.tensor_tensor(h8(yT), h8(yT), b8(SM[R16, 24:32]), ALU.mult), [dYT, dSM], [dYT])
    op("dve", lambda e: e.tensor_tensor(yT, yT, LWv, ALU.mult), [dYT, dLD], [dYT])
    op("dve", lambda e: e.tensor_tensor(yT, yT, LBv, ALU.add), [dYT, dLD], [dYT])
    op("dve", lambda e: e.tensor_tensor(h8(tS), h8(vS), b8(SM[R16, 8:16]), ALU.mult), [dU, dSM, dTS], [dTS])
    op("dve", lambda e: e.tensor_tensor(yT, yT, tS, ALU.add), [dYT, dTS], [dYT])
    op("dve", lambda e: e.tensor_tensor(OS[R16, 0:512], yT, gS, ALU.mult), [dYT, dTS], [dOS])
    P.barrier()
    P.top = mS
    XA = P.alloc(1024); dLD2 = Dep(); dXA = Dep()
    TT_ = P.alloc(1024); dTT = Dep()
    B8 = P.alloc(32); dB8 = Dep()
    GNS = P.alloc(512)
    PK = P.alloc(8 * 328); PK3 = v3(PK, 8); dPK = Dep()
    PKB = P.alloc(328); dPKB = Dep()
    YM = P.alloc(64); dYM = Dep()
    YMT = P.alloc(512); dYMT = Dep()
    OSB = P.alloc(1024, BF16)
    mS2 = P.top
    CW = P.alloc(4096); CB = P.alloc(1024); SCV = P.alloc(3072)
    dma("sp", CW[R16, :], cw_r.partition_broadcast(16), writes=[dLD2])
    dma("sp", CB[R16, :], cb_r.partition_broadcast(16), writes=[dLD2])
    dma("sp", SCV[R16, :], st_conv, writes=[dLD2])
    dma("sp", ncv_s[:, 0:2048], SCV[R16, 1024:3072], reads=[dLD2])
    dma("sp", ncv_s[:, 2048:3072], PRS[R16, RPROJ + 512:RPROJ + 1536], reads=[dPRS])
    op("dve", lambda e: e.tensor_tensor(XA[R16, :], PRS[R16, RPROJ + 512:RPROJ + 1536], CW[R16, 3072:4096], ALU.mult), [dPRS, dLD2], [dXA])
    op("dve", lambda e: e.tensor_tensor(XA[R16, :], XA[R16, :], CB[R16, :], ALU.add), [dXA, dLD2], [dXA])
    for i in range(3):
        op("dve", lambda e: e.tensor_tensor(TT_[R16, :], SCV[R16, i * 1024:(i + 1) * 1024], CW[R16, i * 1024:(i + 1) * 1024], ALU.mult), [dLD2], [dTT])
        op("dve", lambda e: e.tensor_tensor(XA[R16, :], XA[R16, :], TT_[R16, :], ALU.add), [dXA, dTT], [dXA])
    op("act", lambda e: e.activation(XA[R16, :], XA[R16, :], AF.Silu), [dXA], [dXA])
    dma("sp", B8[R16, 0:8], dtb_r.partition_broadcast(16), writes=[dB8])
    dma("sp", B8[R16, 8:16], alog_r.partition_broadcast(16), writes=[dB8])
    dma("sp", B8[R16, 16:24], dsk_r.partition_broadcast(16), writes=[dB8])
    dma("sp", GNS[R16, :], gnw_r.partition_broadcast(16), writes=[dB8])
    op("act", lambda e: e.activation(B8[R16, 8:16], B8[R16, 8:16], AF.Exp), [dB8], [dB8])
    op("dve", lambda e: e.tensor_tensor(B8[R16, 24:32], PRS[R16, PROJ - 8:PROJ], B8[R16, 0:8], ALU.add), [dPRS, dB8], [dB8])
    op("act", lambda e: e.activation(B8[R16, 24:32], B8[R16, 24:32], AF.Exp), [dB8], [dB8])
    op("act", lambda e: e.activation(B8[R16, 24:32], B8[R16, 24:32], AF.Ln, bias=EPS[R16, 2:3], scale=1.0), [dB8, dFV], [dB8])
    op("dve", lambda e: e.memset(PK[R16, :], 0.0), [], [dPK])
    op("dve", lambda e: e.tensor_tensor(PK3[R16, :, 0:64], h8(XA[R16, 0:512]), b8(B8[R16, 24:32]), ALU.mult), [dXA, dB8], [dPK])
    for g in range(2):
        op("dve", lambda e: e.tensor_copy(PK3[R16, 4 * g:4 * g + 4, 64:192], XA[R16, 512 + g * 128:640 + g * 128].unsqueeze(1).to_broadcast([16, 4, 128])), [dXA], [dPK])
        op("dve", lambda e: e.tensor_copy(PK3[R16, 4 * g:4 * g + 4, 192:320], XA[R16, 768 + g * 128:896 + g * 128].unsqueeze(1).to_broadcast([16, 4, 128])), [dXA], [dPK])
    op("dve", lambda e: e.tensor_tensor(B8[R16, 8:16], B8[R16, 8:16], B8[R16, 24:32], ALU.mult), [dB8], [dB8])
    op("act", lambda e: e.activation(PK3[R16, :, 320], B8[R16, 8:16], AF.Exp, scale=-1.0), [dB8, dPK], [dPK])
    dSCR3 = Dep()
    dma("sp", scrC.rearrange("(b h) c -> b (h c)", h=8), PK[R16, :], reads=[dPK], writes=[dSCR3])
    dma("sp", PKB, scrC, reads=[dSCR3], writes=[dPKB])
    HS = P.alloc(4096); TM2 = P.alloc(4096); dHS = Dep(); dTM2 = Dep()
    H3 = v3(HS, 32); M3 = v3(TM2, 32)
    for half in range(2):
        dma("sp", HS, st_ssm[:, half * 4096:(half + 1) * 4096], writes=[dHS])
        op("act", lambda e: e.activation(HS, HS, AF.Identity, scale=PKB[:, 320:321]), [dHS, dPKB], [dHS])
        op("dve", lambda e: e.tensor_tensor(M3, PKB[:, half * 32:(half + 1) * 32].unsqueeze(2).to_broadcast([128, 32, 128]), PKB[:, 64:192].unsqueeze(1).to_broadcast([128, 32, 128]), ALU.mult), [dPKB], [dTM2])
        op("dve", lambda e: e.tensor_tensor(HS, HS, TM2, ALU.add), [dHS, dTM2], [dHS])
        dma("sp", nssm_s[:, half * 4096:(half + 1) * 4096], HS, reads=[dHS])
        op("dve", lambda e: e.tensor_tensor(M3, H3, PKB[:, 192:320].unsqueeze(1).to_broadcast([128, 32, 128]), ALU.mult), [dHS, dPKB], [dTM2])
        op("dve", lambda e: e.reduce_sum(YM[:, half * 32:(half + 1) * 32], M3, AX.X), [dTM2], [dYM])
    dSCR4 = Dep()
    dma("sp", scrD, YM, reads=[dYM], writes=[dSCR4])
    dma("sp", YMT[R16, :], scrD.rearrange("(b h) v -> b (h v)", h=8), reads=[dSCR4], writes=[dYMT])
    ym = YMT[R16, :]
    op("dve", lambda e: e.tensor_tensor(h8(TT_[R16, 0:512]), h8(XA[R16, 0:512]), b8(B8[R16, 16:24]), ALU.mult), [dXA, dB8], [dTT])
    op("dve", lambda e: e.tensor_tensor(ym, ym, TT_[R16, 0:512], ALU.add), [dYMT, dTT], [dYMT])
    op("act", lambda e: e.activation(TT_[R16, 512:1024], PRS[R16, RPROJ:RPROJ + 512], AF.Silu), [dPRS, dTT], [dTT])
    op("dve", lambda e: e.tensor_tensor(ym, ym, TT_[R16, 512:1024], ALU.mult), [dYMT, dTT], [dYMT])
    op("dve", lambda e: e.tensor_tensor(TT_[R16, 0:512], ym, ym, ALU.mult), [dYMT, dTT], [dTT])
    op("dve", lambda e: e.reduce_sum(SM[R16, 32:34], TT_[R16, 0:512].rearrange("p (g c) -> p g c", g=2), AX.X), [dTT], [dSM])
    op("dve", lambda e: e.tensor_scalar(SM[R16, 32:34], SM[R16, 32:34], 1.0 / 256.0, 1e-5, ALU.mult, ALU.add), [dSM], [dSM])
    op("act", lambda e: e.activation(SM[R16, 32:34], SM[R16, 32:34], AF.Sqrt), [dSM], [dSM])
    op("dve", lambda e: e.reciprocal(SM[R16, 32:34], SM[R16, 32:34]), [dSM], [dSM])
    op("dve", lambda e: e.tensor_tensor(ym.rearrange("p (g c) -> p g c", g=2), ym.rearrange("p (g c) -> p g c", g=2), SM[R16, 32:34].unsqueeze(2).to_broadcast([16, 2, 256]), ALU.mult), [dYMT, dSM], [dYMT])
    op("dve", lambda e: e.tensor_tensor(OS[R16, 512:1024], ym, GNS[R16, :], ALU.mult), [dYMT, dB8], [dOS])
    cp("act", OSB[R16, :], OS[R16, :], [dOS], [dOS])
    bk, dbk = P.bank()
    psb = bk[:].bitcast(BF16)
    for dc in range(8):
        op("pe", lambda e: e.transpose(psb[:, dc * 16:(dc + 1) * 16], OSB[R16, dc * 128:(dc + 1) * 128], identb[0:16, 0:16]), [dOS, dCSTB], [dbk])
    cp("dve", MIXT3[:, :, LP:NT], v3(psb[:, 0:128], 8), [dbk], [dMIX])
    P.barrier()
    P.top = mM

    stop('S')
    def ln_tile(V, rows, G, Bv, dV, dGB, par=0):
        rs_ = slice(0, rows)
        ST = STs[par]; dST = dSTs[par]; JUNK = JUNKs[par]; dJ = dJs[par]
        op("act", lambda e: e.activation(JUNK[rs_, :], V[rs_, :], AF.Copy, accum_out=ST[rs_, 0:1]), [dV, dST], [dJ, dST])
        op("act", lambda e: e.activation(JUNK[rs_, :], V[rs_, :], AF.Square, accum_out=ST[rs_, 1:2]), [dV, dST], [dJ, dST])
        op("dve", lambda e: e.tensor_scalar(ST[rs_, 2:3], ST[rs_, 0:1], 1.0 / DM, None, ALU.mult), [dST], [dST])
        op("dve", lambda e: e.tensor_tensor(ST[rs_, 3:4], ST[rs_, 2:3], ST[rs_, 2:3], ALU.mult), [dST], [dST])
        op("dve", lambda e: e.scalar_tensor_tensor(ST[rs_, 4:5], ST[rs_, 1:2], 1.0 / DM, ST[rs_, 3:4], ALU.mult, ALU.subtract), [dST], [dST])
        op("act", lambda e: e.activation(ST[rs_, 5:6], ST[rs_, 4:5], AF.Sqrt, bias=EPS[rs_, 1:2], scale=1.0), [dST, dFV], [dST])
        op("dve", lambda e: e.reciprocal(ST[rs_, 6:7], ST[rs_, 5:6]), [dST], [dST])
        op("dve", lambda e: e.scalar_tensor_tensor(ST[rs_, 7:8], ST[rs_, 2:3], -1.0, ST[rs_, 6:7], ALU.mult, ALU.mult), [dST], [dST])
        op("act", lambda e: e.activation(V[rs_, :], V[rs_, :], AF.Identity, bias=ST[rs_, 7:8], scale=ST[rs_, 6:7]), [dV, dST], [dV])
        op("dve", lambda e: e.tensor_tensor(V[rs_, :], V[rs_, :], G[rs_, :], ALU.mult), [dV, dGB], [dV])
        op("dve", lambda e: e.tensor_tensor(V[rs_, :], V[rs_, :], Bv[rs_, :], ALU.add), [dV, dGB], [dV])

    P.top = mAfterMIX
    W2 = P.alloc(32 * DM, BF16); W23 = v3(W2, 32); dW2 = Dep()
    mAfterW2 = P.top
    mO = P.top
    WO = P.alloc(8 * DM, BF16); WO3 = v3(WO, 8); dWO = Dep()
    dma("pool", WO3, w_out.rearrange("(a p) c -> p a c", p=128), writes=[dWO])
    for q in range(8):
        dma("pool", W23[:, q * 4:(q + 1) * 4, :], w_ff2[q * 512:(q + 1) * 512, :].rearrange("(a p) c -> p a c", p=128), writes=[dW2])
    LG = P.alloc(DM); LB_ = P.alloc(DM); dLG = Dep()
    dma("sp", LG, l1g_r.partition_broadcast(128), writes=[dLG])
    dma("sp", LB_, l1b_r.partition_broadcast(128), writes=[dLG])
    XIN = [P.alloc(DM) for _ in range(3)]; dXIN = [Dep() for _ in range(3)]
    V32 = [P.alloc(DM) for _ in range(3)]; dV32 = [Dep() for _ in range(3)]
    JUNKs = [P.alloc(DM, BF16) for _ in range(3)]; dJs = [Dep() for _ in range(3)]
    HBs = [P.alloc(DM, BF16) for _ in range(3)]; dHBs = [Dep() for _ in range(3)]
    HT3 = XT3
    dYD = [Dep() for _ in TILES]

    def ytile(j):
        t0, rows = TILES[j]
        return y_p[t0:t0 + rows, :] if j < 16 else y_s

    for j, (t0, rows) in enumerate(TILES):
        b = j % 3
        rs_ = slice(0, rows)
        dma("sp", XIN[b][rs_, :], x_p[t0:t0 + rows, :] if j < 16 else x_s, writes=[dXIN[b]])
        for half in range(2):
            bk, dbk = P.bank()
            for dc in range(8):
                mm(bk[rs_, :], MIXT3[:, dc, t0:t0 + rows], WO3[:, dc, half * 512:(half + 1) * 512], dc == 0, dc == 7, [dMIX, dWO], [dbk])
            op("dve", lambda e: e.scalar_tensor_tensor(V32[b][rs_, half * 512:(half + 1) * 512], XIN[b][rs_, half * 512:(half + 1) * 512], ALPHA, bk[rs_, :], ALU.mult, ALU.add), [dXIN[b], dbk], [dV32[b]])
        ln_tile(V32[b], rows, LG, LB_, dV32[b], dLG, b)
        HB = HBs[b]; dHB = dHBs[b]
        dma("sp", ytile(j), V32[b][rs_, :], reads=[dV32[b]], writes=[dYD[j]])
        cp("act", HB[rs_, :], V32[b][rs_, :], [dV32[b]], [dHB])
        bk, dbk = P.bank()
        psb = bk[:].bitcast(BF16)
        for dc in range(8):
            op("pe", lambda e: e.transpose(psb[:, dc * 128:dc * 128 + rows], HB[rs_, dc * 128:(dc + 1) * 128], identb[0:rows, 0:rows]), [dHB, dCSTB], [dbk])
        cp(ev_eng(), HT3[:, :, t0:t0 + rows], v3(psb, 8)[:, :, 0:rows], [dbk], [dXT[j]])
    P.barrier()

    stop('O')
    P.top = mAfterW2
    F1T = MIXT[:, 0:32 * 512]; F1T3 = v3(F1T, 32); dF1 = Dep()
    NW1 = 4
    W1 = [P.alloc(8 * 512, BF16) for _ in range(NW1)]; dW1 = [Dep() for _ in range(NW1)]
    RL = [P.alloc(512) for _ in range(2)]; dRL = [Dep(), Dep()]
    RLS = [P.alloc(16) for _ in range(2)]; dRLS = [Dep(), Dep()]
    F1S = P.alloc(32 * 16, BF16); F1S3 = v3(F1S, 32); dF1S = Dep()
    LG2 = P.alloc(DM); LB2 = P.alloc(DM); dLG2 = Dep()
    dma("sp", LG2, l2g_r.partition_broadcast(128), writes=[dLG2])
    dma("sp", LB2, l2b_r.partition_broadcast(128), writes=[dLG2])
    HIN = [P.alloc(DM) for _ in range(2)]; dHIN = [Dep(), Dep()]
    VV = [P.alloc(DM) for _ in range(2)]; dVV = [Dep(), Dep()]
    JUNKs = [P.alloc(DM, BF16), P.alloc(DM, BF16)]; dJs = [Dep(), Dep()]
    w1i = 0; rli = 0; tcount = 0
    for (s0, sn) in SLABS[:4]:
        last_blk = (s0 == 1536)
        for jg in range(8):
            wb = w1i % NW1; w1i += 1
            w13 = v3(W1[wb], 8)
            dma("pool", w13, w_ff1[:, jg * 512:(jg + 1) * 512].rearrange("(a p) c -> p a c", p=128), writes=[dW1[wb]])
            for jc in range(4):
                bk, dbk = P.bank()
                for dc in range(8):
                    mm(bk[:, 0:sn], w13[:, dc, jc * 128:(jc + 1) * 128], HT3[:, dc, s0:s0 + sn], dc == 0, dc == 7, [dW1[wb]] + xt_deps(s0, sn), [dbk])
                rb = rli % 2; rli += 1
                op("act", lambda e: e.activation(RL[rb][:, 0:sn], bk[:, 0:sn], AF.Relu), [dbk], [dRL[rb]])
                op("dve", lambda e: e.tensor_tensor(F1T3[:, jg * 4 + jc, 0:sn], RL[rb][:, 0:sn], RL[rb][:, 0:sn], ALU.mult), [dRL[rb]], [dF1])
                if last_blk:
                    bk, dbk = P.bank()
                    for dc in range(8):
                        mm(bk[:, 0:16], w13[:, dc, jc * 128:(jc + 1) * 128], HT3[:, dc, LP:NT], dc == 0, dc == 7, [dW1[wb], dXT[16]], [dbk])
                    op("act", lambda e: e.activation(RLS[rb][:, 0:16], bk[:, 0:16], AF.Relu), [dbk], [dRLS[rb]])
                    op("dve", lambda e: e.tensor_tensor(F1S3[:, jg * 4 + jc, :], RLS[rb][:, 0:16], RLS[rb][:, 0:16], ALU.mult), [dRLS[rb]], [dF1S])
        for j, (t0, rows) in enumerate(TILES):
            if not ((t0 >= s0 and t0 < s0 + sn) or (last_blk and j == 16)):
                continue
            b = tcount % 2; tcount += 1
            rs_ = slice(0, rows)
            lo = t0 - s0
            dma("sp", HIN[b][rs_, :], ytile(j), reads=[dYD[j]], writes=[dHIN[b]])
            for half in range(2):
                bk, dbk = P.bank()
                for jc in range(32):
                    if j == 16:
                        mm(bk[rs_, :], F1S3[:, jc, :], W23[:, jc, half * 512:(half + 1) * 512], jc == 0, jc == 31, [dF1S, dW2], [dbk])
                    else:
                        mm(bk[rs_, :], F1T3[:, jc, lo:lo + rows], W23[:, jc, half * 512:(half + 1) * 512], jc == 0, jc == 31, [dF1, dW2], [dbk])
                op("dve", lambda e: e.scalar_tensor_tensor(VV[b][rs_, half * 512:(half + 1) * 512], HIN[b][rs_, half * 512:(half + 1) * 512], ALPHA, bk[rs_, :], ALU.mult, ALU.add), [dHIN[b], dbk], [dVV[b]])
            ln_tile(VV[b], rows, LG2, LB2, dVV[b], dLG2, b)
            dma("sp", ytile(j), VV[b][rs_, :], reads=[dVV[b], dHIN[b]], writes=[dYD[j]])
    P.finish()
    return nc


def _consts():
    r = np.arange(128)
    c = np.arange(128)
    R, Cc = np.meshgrid(r, c, indexing="ij")
    ident = (R == Cc).astype(np.float32)
    sblk = (R // 64 == Cc // 64)
    mSU = (R < Cc).astype(np.float32)
    mSL = (R > Cc).astype(np.float32)
    mUI = (R <= Cc).astype(np.float32)
    bones = sblk.astype(np.float32)
    ones = np.ones((128, 128), np.float32)
    negm = np.where(R > Cc, -30000.0, 0.0).astype(np.float32)
    s_ = R % 64
    mask4 = np.where(Cc < 64, s_ < Cc, s_ <= (Cc - 64)).astype(np.float32)
    return np.concatenate([ident, mSU, mSL, mUI, bones, ones, negm, mask4], axis=1)


_NC_CACHE = {}


def kernel(x_prompt, x_sample, state_shift, state_wkv, state_conv, state_ssm, w_in, mu_shift,
           w0, w_up, a0, a_up, g_up, k_k, k_a, r_k, lnx_w, lnx_b, conv_w, conv_b, dt_bias,
           a_log, d_skip, gnorm_w, w_out, ln1_g, ln1_b, w_ff1, w_ff2, ln2_g, ln2_b):
    f = lambda a: np.ascontiguousarray(np.asarray(a, dtype=np.float32))
    x_prompt = f(x_prompt); x_sample = f(x_sample)
    fv = np.zeros((128, 96), np.float32)
    fv[:, 0:14] = f(mu_shift)[0].reshape(14, 128).T
    for i, v in enumerate((w0, a0, k_k, k_a, lnx_w, lnx_b)):
        fv[:, 14 + 4 * i:18 + 4 * i] = f(v)[0].reshape(4, 128).T
    cw = f(conv_w)[0]
    for fc in range(8):
        for i in range(4):
            fv[:, 38 + fc * 4 + i] = cw[i, fc * 128:(fc + 1) * 128]
    fv[:, 70:78] = f(conv_b)[0].reshape(8, 128).T
    rk = f(r_k)[0]
    rkb = np.zeros((128, 4, 128), np.float32)
    for hp in range(4):
        for hh in range(2):
            rkb[hh * 64:(hh + 1) * 64, hp, hh * 64:(hh + 1) * 64] = rk[2 * hp + hh][:, None]
    rkb = rkb.reshape(128, 512)
    cst = _consts()
    row = lambda a: f(a).reshape(1, -1)
    common = {
        "w_in": f(w_in)[0], "w_up": f(w_up)[0], "a_up": f(a_up)[0], "g_up": f(g_up)[0], "w_out": f(w_out)[0],
        "w_ff1": f(w_ff1)[0], "w_ff2": f(w_ff2)[0], "cst": cst, "fv": fv, "rkb": rkb,
        "mu_r": row(mu_shift), "w0_r": row(w0), "a0_r": row(a0), "kk_r": row(k_k), "ka_r": row(k_a),
        "lw_r": row(lnx_w), "lb_r": row(lnx_b), "rk_r": row(r_k), "cw_r": row(conv_w), "cb_r": row(conv_b),
        "dtb_r": row(dt_bias), "alog_r": row(a_log), "dsk_r": row(d_skip), "gnw_r": row(gnorm_w),
        "l1g_r": row(ln1_g), "l1b_r": row(ln1_b), "l2g_r": row(ln2_g), "l2b_r": row(ln2_b),
    }
    in_maps = []
    for c in range(NCORES):
        sl = slice(c * NSMP, (c + 1) * NSMP)
        m = dict(common)
        m["x_p"] = x_prompt[c]
        m["x_s"] = x_sample[sl, 0, :]
        m["st_shift"] = f(state_shift)[0, sl]
        m["st_wkv"] = f(state_wkv)[0, sl].reshape(128, 4096)
        m["st_conv"] = f(state_conv)[0, sl].reshape(NSMP, 3072)
        m["st_ssm"] = f(state_ssm)[0, sl].reshape(128, 8192)
        in_maps.append({k: np.ascontiguousarray(v) for k, v in m.items()})
    nc = build_program()
    res = run_bass_kernel_spmd(nc, in_maps, core_ids=list(range(NCORES)))
    R = res.results
    cat = lambda k: np.stack([np.asarray(r[k], dtype=np.float32) for r in R])
    y_p = cat("y_p")
    y_s = cat("y_s").reshape(128, 1, DM)
    nsh_p = cat("nsh_p").reshape(1, 8, RPROJ)
    nwkv_p = cat("nwkv_p").reshape(1, 8, 8, 64, 64)
    ncv_p = cat("ncv_p").reshape(1, 8, 3, 1024)
    nssm_p = cat("nssm_p").reshape(1, 8, 8, 64, 128)
    nsh_s = cat("nsh_s").reshape(1, 128, RPROJ)
    nwkv_s = cat("nwkv_s").reshape(1, 128, 8, 64, 64)
    ncv_s = cat("ncv_s").reshape(1, 128, 3, 1024)
    nssm_s = cat("nssm_s").reshape(1, 128, 8, 64, 128)
    return (y_p, y_s, nsh_p, nwkv_p, ncv_p, nssm_p, nsh_s, nwkv_s, ncv_s, nssm_s)
```
